# Optimizing a Trainium2 kernel written in Bass

```python
import math
import jax, jax.numpy as jnp
from jax import lax
import numpy as np

D_MODEL = 1024
BATCH = 8
SEQ = 4096
DEPTH = 2

CTX_LEN = 256
GRID_W = 64
N_EVEN = (DEPTH + 1) // 2
N_ODD = DEPTH // 2

DEEPNORM_ALPHA = (2.0 * DEPTH) ** 0.25
DEEPNORM_BETA = (8.0 * DEPTH) ** -0.25
LN_EPS = 1e-5
RMS_EPS = 1e-6
N_MOD = 9

D_FF = 2816
MACARON_WEIGHT = 0.5

GDN_HEADS = 4
GDN_HEAD_DIM = 128
GDN_WIDTH = GDN_HEADS * GDN_HEAD_DIM
GDN_CONV = 5
GDN_CHUNK = 64
CONV_CH = D_MODEL // 2
CONV_WIDTH = 31
EVEN_SPLITS = (3 * GDN_WIDTH, 4 * GDN_WIDTH, 4 * GDN_WIDTH + 4 * GDN_HEADS)
EVEN_IN = 4 * GDN_WIDTH + 4 * GDN_HEADS + 2 * CONV_CH
EVEN_OUT = GDN_WIDTH + CONV_CH

ATT_HEADS = 8
ATT_KV_HEADS = 2
ATT_GROUP = ATT_HEADS // ATT_KV_HEADS
ATT_HEAD_DIM = 128
ATT_BLOCK = 128
ROPE_THETA = 10000.0
ODD_IN = (ATT_HEADS + 2 * ATT_KV_HEADS) * ATT_HEAD_DIM
ODD_OUT = ATT_HEADS * ATT_HEAD_DIM

kernel_name = "hybrid_gdn_conformer_gqa_dit_block"


def layer_norm(x, g, b):
    xf = x.astype(jnp.float32)
    mu = xf.mean(-1, keepdims=True)
    var = jnp.square(xf - mu).mean(-1, keepdims=True)
    return ((xf - mu) * lax.rsqrt(var + LN_EPS) * g + b).astype(x.dtype)


def rms_norm(x, g):
    xf = x.astype(jnp.float32)
    return (xf * lax.rsqrt(jnp.square(xf).mean(-1, keepdims=True) + RMS_EPS) * g).astype(x.dtype)


def l2_normalize(x):
    xf = x.astype(jnp.float32)
    return xf * lax.rsqrt(jnp.square(xf).sum(-1, keepdims=True) + RMS_EPS)


def modulate(x, shift, scale):
    return x * (1 + scale) + shift


def deepnorm_residual(x, y, g, b):
    return layer_norm(DEEPNORM_ALPHA * x + y, g, b)


def adaln_modulation(cond, w_ada, b_ada):
    return jnp.split(jax.nn.silu(cond) @ w_ada + b_ada, N_MOD, axis=-1)


def swiglu(x, w_gu, w_down):
    gate, up = jnp.split(x @ w_gu, 2, axis=-1)
    return (jax.nn.silu(gate) * up) @ w_down


def ffn_sublayer(x, shift, scale, gate, w_gu, w_down, g, b):
    y = swiglu(modulate(x, shift, scale), w_gu, w_down)
    return deepnorm_residual(x, MACARON_WEIGHT * gate * y, g, b)


def depthwise_conv(x, w):
    k = w.shape[0]
    return lax.conv_general_dilated(
        x, w[:, None, :], window_strides=(1,), padding=[(k // 2, k // 2)],
        dimension_numbers=('NWC', 'WIO', 'NWC'), feature_group_count=x.shape[-1])


def axial_rope_tables(rows, head_dim, dtype):
    row = jnp.repeat(jnp.arange(rows, dtype=jnp.float32), GRID_W)
    col = jnp.tile(jnp.arange(GRID_W, dtype=jnp.float32), rows)
    n_freq = head_dim // 4
    inv_freq = jnp.float32(ROPE_THETA) ** (-jnp.arange(n_freq, dtype=jnp.float32) / n_freq)
    ang = jnp.concatenate([row[:, None] * inv_freq, col[:, None] * inv_freq], axis=-1)
    return jnp.cos(ang).astype(dtype), jnp.sin(ang).astype(dtype)


def apply_rope(x, cos, sin):
    x1, x2 = x[..., 0::2], x[..., 1::2]
    c, s = cos[:, None, :], sin[:, None, :]
    return jnp.stack([x1 * c - x2 * s, x1 * s + x2 * c], axis=-1).reshape(x.shape)


def gated_delta_chunked(q, k, v, g, beta, state0):
    B, L, H, dk = q.shape
    dv = v.shape[-1]
    C = GDN_CHUNK
    n = L // C
    f32 = jnp.float32

    def to_chunks(t):
        t = t.astype(f32).reshape((B, n, C, H) + t.shape[3:])
        return jnp.moveaxis(t, (1, 3), (0, 2))

    qc = to_chunks(q) * (dk ** -0.5)
    kc, vc = to_chunks(k), to_chunks(v)
    gcum = jnp.cumsum(to_chunks(g), axis=-1)
    bc = to_chunks(beta)
    tril = jnp.tril(jnp.ones((C, C), dtype=bool))
    strict = jnp.tril(jnp.ones((C, C), dtype=bool), -1)
    diff = gcum[..., :, None] - gcum[..., None, :]
    decay = jnp.where(tril, jnp.exp(jnp.where(tril, diff, 0.0)), 0.0)
    k_beta = kc * bc[..., None]
    v_beta = vc * bc[..., None]
    m = jnp.where(strict, jnp.einsum('nbhik,nbhjk->nbhij', k_beta, kc) * decay, 0.0)
    a = jnp.eye(C, dtype=f32) + m
    u = lax.linalg.triangular_solve(a, v_beta, left_side=True, lower=True, unit_diagonal=True)
    w = lax.linalg.triangular_solve(a, k_beta * jnp.exp(gcum)[..., None],
                                    left_side=True, lower=True, unit_diagonal=True)
    attn_intra = jnp.where(tril, jnp.einsum('nbhik,nbhjk->nbhij', qc, kc) * decay, 0.0)
    g_last = gcum[..., -1]
    k_tail = kc * jnp.exp(g_last[..., None] - gcum)[..., None]

    def step(S, inp):
        q_i, u_i, w_i, at_i, gc_i, gl_i, kt_i = inp
        v_new = u_i - w_i @ S
        o = (q_i * jnp.exp(gc_i)[..., None]) @ S + at_i @ v_new
        S = S * jnp.exp(gl_i)[..., None, None] + jnp.swapaxes(kt_i, -1, -2) @ v_new
        return S, o

    S, o = lax.scan(step, state0.astype(f32), (qc, u, w, attn_intra, gcum, g_last, k_tail))
    o = jnp.moveaxis(o, (0, 2), (1, 3)).reshape(B, L, H, dv)
    return o, S


def gdn_bidirectional(q, k, v, g_f, g_b, b_f, b_b, s_f0, s_b0):
    flip = lambda t: jnp.flip(t, axis=1)
    o_f, s_f = gated_delta_chunked(q, k, v, g_f, b_f, s_f0)
    o_b, s_b = gated_delta_chunked(flip(q), flip(k), flip(v), flip(g_b), flip(b_b), s_b0)
    return o_f + flip(o_b), s_f, s_b


def gdn_gates(ab, a_log, dt_bias):
    B, L, _ = ab.shape
    af = ab.astype(jnp.float32)
    a = af[..., :2 * GDN_HEADS].reshape(B, L, 2, GDN_HEADS)
    b = af[..., 2 * GDN_HEADS:].reshape(B, L, 2, GDN_HEADS)
    g = -jnp.exp(a_log.astype(jnp.float32)) * jax.nn.softplus(a + dt_bias.astype(jnp.float32))
    beta = jax.nn.sigmoid(b)
    return g[:, :, 0], g[:, :, 1], beta[:, :, 0], beta[:, :, 1]


def even_mixer(h_lat, h_ctx, w_in, qkv_conv, a_log, dt_bias, out_norm,
               dw_conv, dw_bias, cln_g, cln_b, w_out, need_ctx):
    def project(h):
        B, L, _ = h.shape
        qkv, z, ab, glu = jnp.split(h @ w_in, EVEN_SPLITS, axis=-1)
        qkv = jax.nn.silu(depthwise_conv(qkv, qkv_conv))
        q, k, v = [t.reshape(B, L, GDN_HEADS, GDN_HEAD_DIM) for t in jnp.split(qkv, 3, axis=-1)]
        scan_in = (l2_normalize(q), l2_normalize(k), v) + gdn_gates(ab, a_log, dt_bias)
        return scan_in, z, glu

    def conformer(glu):
        val, gate = jnp.split(glu, 2, axis=-1)
        y = depthwise_conv(val * jax.nn.sigmoid(gate), dw_conv) + dw_bias
        return jax.nn.silu(layer_norm(y, cln_g, cln_b))

    def merge(o, z, glu):
        B, L, _ = z.shape
        o = rms_norm(o, out_norm).astype(z.dtype) * jax.nn.silu(z.reshape(B, L, GDN_HEADS, GDN_HEAD_DIM))
        return jnp.concatenate([o.reshape(B, L, GDN_WIDTH), conformer(glu)], axis=-1) @ w_out

    ctx_in, z_c, glu_c = project(h_ctx)
    lat_in, z_l, glu_l = project(h_lat)
    zeros = jnp.zeros((h_ctx.shape[0], GDN_HEADS, GDN_HEAD_DIM, GDN_HEAD_DIM), jnp.float32)
    o_c, s_f, s_b = gdn_bidirectional(*ctx_in, zeros, zeros)
    o_l, _, _ = gdn_bidirectional(*lat_in, s_f, s_b)
    lat_out = merge(o_l, z_l, glu_l)
    ctx_out = merge(o_c, z_c, glu_c) if need_ctx else None
    return lat_out, ctx_out


def gqa_attend(q, k, v):
    s = jnp.einsum('bqhgd,blhd->bhgql', q, k).astype(jnp.float32) * (ATT_HEAD_DIM ** -0.5)
    p = jax.nn.softmax(s, axis=-1).astype(v.dtype)
    return jnp.einsum('bhgql,blhd->bqhgd', p, v)


def odd_mixer(h_lat, h_ctx, w_in, q_norm, k_norm, w_out, rope_cos, rope_sin, need_ctx):
    hd = ATT_HEAD_DIM

    def project(h):
        B, L, _ = h.shape
        q, k, v = jnp.split(h @ w_in, [ATT_HEADS * hd, (ATT_HEADS + ATT_KV_HEADS) * hd], axis=-1)
        q = rms_norm(q.reshape(B, L, ATT_HEADS, hd), q_norm)
        k = rms_norm(k.reshape(B, L, ATT_KV_HEADS, hd), k_norm)
        return q, k, v.reshape(B, L, ATT_KV_HEADS, hd)

    q_c, k_c, v_c = project(h_ctx)
    q_l, k_l, v_l = project(h_lat)
    q_l = apply_rope(q_l, rope_cos, rope_sin)
    k_l = apply_rope(k_l, rope_cos, rope_sin)
    B, L = h_lat.shape[:2]
    k_all = jnp.concatenate([k_l, k_c], axis=1)
    v_all = jnp.concatenate([v_l, v_c], axis=1)
    n_blk = L // ATT_BLOCK
    q_blocks = q_l.reshape(B, n_blk, ATT_BLOCK, ATT_KV_HEADS, ATT_GROUP, hd).swapaxes(0, 1)
    o = lax.map(lambda qb: gqa_attend(qb, k_all, v_all), q_blocks)
    lat_out = o.swapaxes(0, 1).reshape(B, L, ODD_OUT) @ w_out
    ctx_out = None
    if need_ctx:
        Bc, Lc = h_ctx.shape[:2]
        o_c = gqa_attend(q_c.reshape(Bc, Lc, ATT_KV_HEADS, ATT_GROUP, hd), k_c, v_c)
        ctx_out = o_c.reshape(Bc, Lc, ODD_OUT) @ w_out
    return lat_out, ctx_out


def setup_inputs(seed: int = 0) -> dict:
    key = jax.random.key(seed)
    ks = jax.random.split(key, 32)
    f32 = jnp.float32
    nrm = lambda k, shape, s: jax.random.normal(k, shape, f32) * s
    dt = jnp.exp(jax.random.uniform(ks[13], (N_EVEN, 2, GDN_HEADS), f32, math.log(1e-3), math.log(1e-1)))
    return {
        "x": nrm(ks[0], (BATCH, SEQ, D_MODEL), 1.0),
        "c": nrm(ks[1], (BATCH, D_MODEL), 1.0),
        "ctx": nrm(ks[2], (BATCH, CTX_LEN, D_MODEL), 1.0),
        "c_ctx": nrm(ks[3], (D_MODEL,), 1.0),
        "ada_w": nrm(ks[4], (DEPTH, D_MODEL, N_MOD * D_MODEL), 0.5 * D_MODEL ** -0.5),
        "ada_b": nrm(ks[5], (DEPTH, N_MOD * D_MODEL), 0.02),
        "ln_g": 1.0 + nrm(ks[6], (DEPTH, 3, D_MODEL), 0.02),
        "ln_b": nrm(ks[7], (DEPTH, 3, D_MODEL), 0.02),
        "ffn_w_gu": nrm(ks[8], (DEPTH, 2, D_MODEL, 2 * D_FF), D_MODEL ** -0.5),
        "ffn_w_down": nrm(ks[9], (DEPTH, 2, D_FF, D_MODEL), DEEPNORM_BETA * D_FF ** -0.5),
        "even_w_in": nrm(ks[10], (N_EVEN, D_MODEL, EVEN_IN), D_MODEL ** -0.5),
        "even_qkv_conv": nrm(ks[11], (N_EVEN, GDN_CONV, 3 * GDN_WIDTH), GDN_CONV ** -0.5),
        "gdn_a_log": jnp.log(jax.random.uniform(ks[12], (N_EVEN, 2, GDN_HEADS), f32, 1.0, 16.0)),
        "gdn_dt_bias": dt + jnp.log(-jnp.expm1(-dt)),
        "gdn_out_norm": 1.0 + nrm(ks[14], (N_EVEN, GDN_HEAD_DIM), 0.02),
        "cf_dw_conv": nrm(ks[15], (N_EVEN, CONV_WIDTH, CONV_CH), CONV_WIDTH ** -0.5),
        "cf_dw_bias": nrm(ks[16], (N_EVEN, CONV_CH), 0.02),
        "cf_ln_g": 1.0 + nrm(ks[17], (N_EVEN, CONV_CH), 0.02),
        "cf_ln_b": nrm(ks[18], (N_EVEN, CONV_CH), 0.02),
        "even_w_out": nrm(ks[19], (N_EVEN, EVEN_OUT, D_MODEL), DEEPNORM_BETA * EVEN_OUT ** -0.5),
        "attn_w_in": nrm(ks[20], (N_ODD, D_MODEL, ODD_IN), D_MODEL ** -0.5),
        "attn_q_norm": 1.0 + nrm(ks[21], (N_ODD, ATT_HEAD_DIM), 0.02),
        "attn_k_norm": 1.0 + nrm(ks[22], (N_ODD, ATT_HEAD_DIM), 0.02),
        "attn_w_out": nrm(ks[23], (N_ODD, ODD_OUT, D_MODEL), DEEPNORM_BETA * ODD_OUT ** -0.5),
    }


def reference(x, c, ctx, c_ctx, ada_w, ada_b, ln_g, ln_b, ffn_w_gu, ffn_w_down,
              even_w_in, even_qkv_conv, gdn_a_log, gdn_dt_bias, gdn_out_norm,
              cf_dw_conv, cf_dw_bias, cf_ln_g, cf_ln_b, even_w_out,
              attn_w_in, attn_q_norm, attn_k_norm, attn_w_out):
    rows = x.shape[1] // GRID_W
    rope_cos, rope_sin = axial_rope_tables(rows, ATT_HEAD_DIM, x.dtype)
    for layer in range(DEPTH):
        last = layer == DEPTH - 1
        i = layer // 2
        mods_lat = [m[:, None, :] for m in adaln_modulation(c, ada_w[layer], ada_b[layer])]
        mods_ctx = adaln_modulation(c_ctx, ada_w[layer], ada_b[layer])
        x = ffn_sublayer(x, *mods_lat[0:3], ffn_w_gu[layer, 0], ffn_w_down[layer, 0], ln_g[layer, 0], ln_b[layer, 0])
        ctx = ffn_sublayer(ctx, *mods_ctx[0:3], ffn_w_gu[layer, 0], ffn_w_down[layer, 0], ln_g[layer, 0], ln_b[layer, 0])
        h_lat = modulate(x, *mods_lat[3:5])
        h_ctx = modulate(ctx, *mods_ctx[3:5])
        if layer % 2 == 0:
            o_lat, o_ctx = even_mixer(h_lat, h_ctx, even_w_in[i], even_qkv_conv[i], gdn_a_log[i], gdn_dt_bias[i],
                                      gdn_out_norm[i], cf_dw_conv[i], cf_dw_bias[i], cf_ln_g[i], cf_ln_b[i],
                                      even_w_out[i], not last)
        else:
            o_lat, o_ctx = odd_mixer(h_lat, h_ctx, attn_w_in[i], attn_q_norm[i], attn_k_norm[i], attn_w_out[i],
                                     rope_cos, rope_sin, not last)
        x = deepnorm_residual(x, mods_lat[5] * o_lat, ln_g[layer, 1], ln_b[layer, 1])
        x = ffn_sublayer(x, *mods_lat[6:9], ffn_w_gu[layer, 1], ffn_w_down[layer, 1], ln_g[layer, 2], ln_b[layer, 2])
        if not last:
            ctx = deepnorm_residual(ctx, mods_ctx[5] * o_ctx, ln_g[layer, 1], ln_b[layer, 1])
            ctx = ffn_sublayer(ctx, *mods_ctx[6:9], ffn_w_gu[layer, 1], ffn_w_down[layer, 1], ln_g[layer, 2], ln_b[layer, 2])
    return x
```

```python
import contextlib
import os
import numpy as np
import concourse.bass as bass
import concourse.mybir as mybir
from concourse.bass_utils import run_bass_kernel_spmd

F32 = mybir.dt.float32
BF16 = mybir.dt.bfloat16
AF = mybir.ActivationFunctionType
ALU = mybir.AluOpType
AX = mybir.AxisListType

ENGS = ("pe", "act", "dve", "pool", "sp")
SAME_ENGINE_SYNC = True

D = 1024
KT = 8
DFF = 2816
FT = 22
LCTX = 256
LLAT = 4096
T = LCTX + LLAT
TN = 256
NTILE = T // TN
DEPTH = 2
ALPHA = (2.0 * DEPTH) ** 0.25
LN_EPS = 1e-5
RMS_EPS = 1e-6


class Buf:
    __slots__ = ("name", "lw", "rd")

    def __init__(self, name=""):
        self.name = name
        self.lw = None
        self.rd = {}


class Op:
    __slots__ = ("eng", "fn", "cw", "dw", "is_dma", "need_inc", "ticket", "semkey", "semval")

    def __init__(self, eng, fn, is_dma, semkey):
        self.eng = eng
        self.fn = fn
        self.cw = {}
        self.dw = {}
        self.is_dma = is_dma
        self.need_inc = False
        self.ticket = 0
        self.semkey = semkey
        self.semval = 0


class Prog:
    def __init__(self):
        self.ops = []
        self.dcnt = {}
        self.last = {}
        self.barrier_idx = None

    def _dep(self, op, d):
        dop = self.ops[d]
        if dop.is_dma:
            k = dop.semkey
            v = self.dcnt[k]
            if v > op.dw.get(k, 0):
                op.dw[k] = v
        else:
            if dop.eng == op.eng and not op.is_dma:
                if dop.eng == "pe" or not SAME_ENGINE_SYNC:
                    return
            if d > op.cw.get(dop.eng, -1):
                op.cw[dop.eng] = d

    def add(self, eng, fn, R=(), W=(), dma=False, semkey=None):
        i = len(self.ops)
        op = Op(eng, fn, dma, semkey)
        self.ops.append(op)
        for b in R:
            if b.lw is not None:
                self._dep(op, b.lw)
        for b in W:
            if b.lw is not None:
                self._dep(op, b.lw)
            for r in b.rd.values():
                self._dep(op, r)
        if self.barrier_idx is not None:
            self._dep(op, self.barrier_idx)
        rk = ("d", semkey) if dma else eng
        for b in R:
            b.rd[rk] = i
        for b in W:
            b.lw = i
            b.rd = {}
        if dma:
            self.dcnt[semkey] = self.dcnt.get(semkey, 0) + 16
            op.semval = self.dcnt[semkey]
            self.last[("d", semkey)] = i
        else:
            self.last[eng] = i
        return i

    def pe(self, fn, R=(), W=()):
        return self.add("pe", fn, R, W)

    def act(self, fn, R=(), W=()):
        return self.add("act", fn, R, W)

    def dve(self, fn, R=(), W=()):
        return self.add("dve", fn, R, W)

    def pool(self, fn, R=(), W=()):
        return self.add("pool", fn, R, W)

    def dma(self, q, fn, R=(), W=(), semkey=None):
        return self.add(q, fn, R, W, dma=True, semkey=semkey)

    def barrier(self, fn):
        i = len(self.ops)
        op = Op("dve", fn, False, None)
        self.ops.append(op)
        for k, d in self.last.items():
            dop = self.ops[d]
            if dop.is_dma:
                op.dw[dop.semkey] = self.dcnt[dop.semkey]
            elif d > op.cw.get(dop.eng, -1):
                op.cw[dop.eng] = d
        self.last["dve"] = i
        self.barrier_idx = i
        return i

    def emit(self, nc, final_wait_eng="sp"):
        ops = self.ops
        for op in ops:
            for e, d in op.cw.items():
                ops[d].need_inc = True
        cnt = {e: 0 for e in ENGS}
        for op in ops:
            if not op.is_dma and op.need_inc:
                cnt[op.eng] += 1
                op.ticket = cnt[op.eng]
        dcnt = self.dcnt
        self.stats = dict(cnt=dict(cnt), n_ops=len(ops), n_dma_sems=len(dcnt))
        per_eng = {e: [] for e in ENGS}
        for op in ops:
            per_eng[op.eng].append(op)
        with contextlib.ExitStack() as es:
            esem = {e: es.enter_context(nc.semaphore("s_" + e)) for e in ENGS}
            dsem = {k: es.enter_context(nc.semaphore("d_%s" % (k,))) for k in dcnt}
            block = es.enter_context(nc.Block())

            def make(e):
                def body(eng):
                    known = {f: 0 for f in ENGS}
                    kd = {}
                    for op in per_eng[e]:
                        for f, d in op.cw.items():
                            t = ops[d].ticket
                            if t > known[f]:
                                eng.wait_ge(esem[f], t)
                                known[f] = t
                        for k, v in op.dw.items():
                            if v > kd.get(k, 0):
                                eng.wait_ge(dsem[k], v)
                                kd[k] = v
                        ins = op.fn(eng)
                        if op.is_dma:
                            ins.then_inc(dsem[op.semkey], 16)
                        elif op.need_inc:
                            ins.then_inc(esem[e], 1)
                    if e == final_wait_eng:
                        for f in ENGS:
                            if f != e and cnt[f] > 0:
                                eng.wait_ge(esem[f], cnt[f])
                        for k, v in dcnt.items():
                            eng.wait_ge(dsem[k], v)
                return body

            block.tensor(make("pe"))
            block.scalar(make("act"))
            block.vector(make("dve"))
            block.gpsimd(make("pool"))
            block.sync(make("sp"))


class Arena:
    def __init__(self, t, nwords):
        self.t = t
        self.n = nwords
        self.off = 0
        self.mark = 0

    def alloc(self, shape, dtype):
        n = 1
        for s in shape:
            n *= s
        if dtype == BF16:
            w = (n + 1) // 2
        else:
            w = n
        assert self.off + w <= self.n, ("SBUF arena overflow", self.off, w, self.n)
        v = self.t[:, self.off:self.off + w]
        self.off += w
        if dtype == BF16:
            v = v.bitcast(BF16)[:, :n]
        if len(shape) == 1:
            return v
        if len(shape) == 2:
            return v.rearrange("p (a b) -> p a b", a=shape[0])
        if len(shape) == 3:
            return v.rearrange("p (a b c) -> p a b c", a=shape[0], b=shape[1])
        raise ValueError(shape)

    def set_mark(self):
        self.mark = self.off

    def reset(self):
        self.off = self.mark


class K:
    pass


def build_program(stages=("mods", "l0f1", "l0mix", "l0f2", "l1f1", "l1mix", "l1f2"), dbg=None):
    nc = bass.Bass("TRN2", target_bir_lowering=False)
    k = K()
    k.nc = nc
    k.P = Prog()
    P = k.P

    def din(name, shape):
        return nc.dram_tensor(name, list(shape), F32, kind="ExternalInput").ap()

    k.xT = din("xT", [D, LLAT])
    k.ctxT = din("ctxT", [D, LCTX])
    k.cond = din("cond", [128, KT, 2])
    k.ada_w = [din("ada_w%d" % l, [D, 9 * D]) if "mods" in stages else None for l in range(DEPTH)]
    k.ada_b = din("ada_b", [128, DEPTH, 72])
    k.ln_g = din("ln_g", [128, DEPTH * 3, KT])
    k.ln_b = din("ln_b", [128, DEPTH * 3, KT])
    k.w_gu = [[din("w_gu%d%d" % (l, s), [D, 2 * DFF]) if ("l%df%d" % (l, s + 1)) in stages else None
               for s in range(2)] for l in range(DEPTH)]
    k.w_dn = [[din("w_dn%d%d" % (l, s), [DFF, D]) if ("l%df%d" % (l, s + 1)) in stages else None
               for s in range(2)] for l in range(DEPTH)]
    if "l1mix" in stages:
        k.attn_w_in = din("attn_w_in", [D, 1536])
        k.attn_w_out = din("attn_w_out", [D, D])
        k.rotm = din("rotm", [128, 128])
        k.qk_gain = din("qk_gain", [128, 2])
        k.rope = din("rope", [128, 2, LLAT])
    if "l0mix" in stages:
        k.even_w_in = din("even_w_in", [D, 3088])
        k.even_w_out = din("even_w_out", [D, D])
        k.gdnc = din("gdnc", [128, 8, 128])
        k.cw5 = din("cw5", [128, 12, 5])
        k.cw31 = din("cw31", [128, 4, 31])
        k.cvec = din("cvec", [128, 4, 3])
        k.rowc = din("rowc", [128, 2, 8])
        k.gnorm = din("gnorm", [128, 512])
        k.CFd = nc.dram_tensor("CFd", [512, T], BF16, kind="Internal").ap()
        k.ZSd = nc.dram_tensor("ZSd", [T, 512], BF16, kind="Internal").ap()
        k.Od = [nc.dram_tensor("Od%d" % i, [T, 512], F32, kind="Internal").ap() for i in range(2)]
    k.out = nc.dram_tensor("yT", [D, LLAT], F32, kind="ExternalOutput").ap()
    k.S = [nc.dram_tensor("stream%d" % i, [D, T], F32, kind=("ExternalOutput" if dbg else "Internal")).ap() for i in range(2)]

    with contextlib.ExitStack() as es:
        arena_t = es.enter_context(nc.sbuf_tensor("arena", [128, 53200], F32))
        k.A = Arena(arena_t, 53200)
        k.ps = [es.enter_context(nc.psum_tensor("ps%d" % i, [128, 512], F32)) for i in range(8)]
        k.psb = [Buf("ps%d" % i) for i in range(8)]
        setup_consts(k)
        if "mods" in stages:
            emit_mods(k)
        if dbg == "mods":
            dm = nc.dram_tensor("dbg_mods", [128, DEPTH * 72 * 2], F32, kind="ExternalOutput").ap()
            P.dma("sp", lambda e: e.dma_start(out=dm, in_=k.mods.rearrange("p l n c -> p (l n c)")), R=[k.b_mods], semkey="dbg")
        def in_tile(i):
            if i == 0:
                return k.ctxT.rearrange("(kt p) t -> p kt t", p=128)
            return k.xT[:, (i - 1) * TN:i * TN].rearrange("(kt p) t -> p kt t", p=128)

        def s_tile(s):
            return lambda i: k.S[s][:, i * TN:(i + 1) * TN].rearrange("(kt p) t -> p kt t", p=128)

        def out_tile(i):
            return k.out[:, (i - 1) * TN:i * TN].rearrange("(kt p) t -> p kt t", p=128)

        k.sbuf_S = [[Buf("S%d_%d" % (s_, i)) for i in range(NTILE)] for s_ in range(2)]
        k.b_in = [Buf("in%d" % i) for i in range(NTILE)]
        k.b_out = [Buf("out%d" % i) for i in range(NTILE)]
        def in_cols(g0, g1):
            if g1 <= LCTX:
                return k.ctxT[:, g0:g1].rearrange("(kt p) t -> p kt t", p=128)
            assert g0 >= LCTX
            return k.xT[:, g0 - LCTX:g1 - LCTX].rearrange("(kt p) t -> p kt t", p=128)
        cur = (in_tile, k.b_in, in_cols)
        nxt = [0]

        def scratch():
            j = nxt[0]
            nxt[0] = 1 - j
            return (s_tile(j), k.sbuf_S[j],
                    lambda g0, g1, j=j: k.S[j][:, g0:g1].rearrange("(kt p) t -> p kt t", p=128))
        for st in ("l0f1", "l0mix", "l0f2", "l1f1", "l1mix", "l1f2"):
            if st not in stages:
                continue
            l = int(st[1])
            if st == "l1f2":
                dstp = (out_tile, k.b_out, None)
                emit_ffn(k, 1, 1, cur[0], cur[1], dstp[0], dstp[1], range(1, NTILE))
            elif st.endswith("f1") or st.endswith("f2"):
                dstp = scratch()
                tl = range(NTILE) if st != "l1f2" else range(1, NTILE)
                emit_ffn(k, l, 0 if st.endswith("f1") else 1, cur[0], cur[1], dstp[0], dstp[1], tl)
            elif st == "l1mix":
                dstp = scratch()
                emit_attn(k, 1, cur[0], cur[1], dstp[0], dstp[1])
            elif st == "l0mix":
                dstp = scratch()
                emit_even(k, 0, cur[2], cur[1], cur[0], dstp[0], dstp[1])
            cur = dstp
        P.emit(nc)
    k.stats = P.stats
    return nc, k


def setup_consts(k):
    nc, P, A = k.nc, k.P, k.A
    k.ones_bf = A.alloc([128], BF16)
    k.b_ones = Buf("ones")
    P.dve(lambda e: e.memset(k.ones_bf, 1.0), W=[k.b_ones])
    k.scr = A.alloc([8], F32)
    k.mods = A.alloc([DEPTH, 72, 2], F32)
    k.b_mods = Buf("mods")
    k.mod1p = A.alloc([DEPTH, 72, 2], F32)
    k.lng = A.alloc([DEPTH * 3, KT], F32)
    k.lnb = A.alloc([DEPTH * 3, KT], F32)
    k.b_ln = Buf("ln")
    P.dma("sp", lambda e: e.dma_start(out=k.lng, in_=k.ln_g),
          W=[k.b_ln], semkey="c0")
    P.dma("sp", lambda e: e.dma_start(out=k.lnb, in_=k.ln_b),
          W=[k.b_ln], semkey="c0")
    A.set_mark()


def emit_mods(k):
    nc, P, A = k.nc, k.P, k.A
    A.reset()
    cs = A.alloc([KT, 2], F32)
    sc = A.alloc([KT, 2], F32)
    bsb = A.alloc([DEPTH, 72], F32)
    NCH = 1024
    wbuf = [A.alloc([KT, NCH], F32) for _ in range(2)]
    b_c, b_sc, b_b = Buf(), Buf(), Buf()
    b_w = [Buf(), Buf()]
    P.dma("sp", lambda e: e.dma_start(out=cs, in_=k.cond), W=[b_c], semkey="c1")
    P.dma("sp", lambda e: e.dma_start(out=bsb, in_=k.ada_b), W=[b_b], semkey="c2")
    P.act(lambda e: e.activation(out=sc, in_=cs, func=AF.Exp, scale=-1.0), R=[b_c], W=[b_sc])
    P.act(lambda e: e.activation(out=sc, in_=sc, func=AF.Ln, bias=1.0), R=[b_sc], W=[b_sc])
    P.act(lambda e: e.activation(out=sc, in_=sc, func=AF.Exp, scale=-1.0), R=[b_sc], W=[b_sc])
    P.dve(lambda e: e.tensor_tensor(out=sc, in0=sc, in1=cs, op=ALU.mult), R=[b_sc, b_c], W=[b_sc])
    pst = k.ps[0]
    for l in range(DEPTH):
        for ch in range(9):
            wb = wbuf[(l * 9 + ch) % 2]
            bw = b_w[(l * 9 + ch) % 2]
            for kt in range(KT):
                P.dma("sp" if (kt % 2 == 0 or os.environ.get("NOACTQ")) else "act",
                      lambda e, l=l, ch=ch, kt=kt, wb=wb: e.dma_start(
                          out=wb[:, kt, :], in_=k.ada_w[l][kt * 128:(kt + 1) * 128, ch * NCH:(ch + 1) * NCH]),
                      W=[bw], semkey="mw%d" % ((l * 9 + ch) % 2))
            for nt in range(8):
                n = ch * 8 + nt
                for kt in range(KT):
                    P.pe(lambda e, wb=wb, kt=kt, nt=nt, n=n: e.matmul(
                        pst[:, 2 * n:2 * n + 2], lhsT=wb[:, kt, nt * 128:(nt + 1) * 128], rhs=sc[:, kt, :],
                        start=(kt == 0), stop=(kt == KT - 1)), R=[bw, b_sc], W=[k.psb[0]])
        P.dve(lambda e, l=l: e.tensor_tensor(
            out=k.mods[:, l, :, :], in0=pst[:, 0:144].rearrange("p (n c) -> p n c", c=2),
            in1=bsb[:, l, :].unsqueeze(2).to_broadcast([128, 72, 2]), op=ALU.add),
            R=[k.psb[0], b_b], W=[k.b_mods])
    P.dve(lambda e: e.tensor_scalar_add(out=k.mod1p, in0=k.mods, scalar1=1.0), R=[k.b_mods], W=[k.b_mods])
    for j in (2, 8):
        P.dve(lambda e, j=j: e.tensor_scalar_mul(out=k.mod1p[:, :, j * 8:(j + 1) * 8, :],
                                                 in0=k.mods[:, :, j * 8:(j + 1) * 8, :], scalar1=0.5),
              R=[k.b_mods], W=[k.b_mods])


def emit_ffn(k, l, s, src, src_bufs, dst, dst_bufs, tiles, final_out=None):
    nc, P, A = k.nc, k.P, k.A
    A.reset()
    P.barrier(lambda e: e.memset(k.scr, 0.0))
    j0 = 0 if s == 0 else 6
    lnrow = l * 3 + (0 if s == 0 else 2)
    NG = 2
    wgu = A.alloc([KT, 2 * DFF], BF16)
    wdn = A.alloc([FT, D], BF16)
    b_wgu = [[[Buf() for _ in range(KT)] for _ in range(NG)] for _ in range(2)]
    b_wdn = [Buf() for _ in range(FT)]
    for g in range(NG):
        for gu in range(2):
            for kt in range(KT):
                c0 = gu * DFF + g * 11 * 128
                P.dma("pool", lambda e, kt=kt, c0=c0: e.dma_start(
                    out=wgu[:, kt, c0:c0 + 11 * 128], in_=k.w_gu[l][s][kt * 128:(kt + 1) * 128, c0:c0 + 11 * 128]),
                    W=[b_wgu[gu][g][kt]], semkey="wgu%d" % (gu * NG + g))
    for f in range(FT):
        P.dma("pool", lambda e, f=f: e.dma_start(out=wdn[:, f, :], in_=k.w_dn[l][s][f * 128:(f + 1) * 128, :]),
              W=[b_wdn[f]], semkey="wdn%d" % (f // 11))
    xs = [A.alloc([KT, TN], F32) for _ in range(2)]
    xm = [A.alloc([KT, TN], BF16) for _ in range(2)]
    h = A.alloc([FT, TN], BF16)
    rr = [A.alloc([KT, TN], F32) for _ in range(2)]
    tmp = [A.alloc([TN], F32) for _ in range(2)]
    sg = [A.alloc([TN], F32) for _ in range(2)]
    rb = [A.alloc([TN], BF16) for _ in range(2)]
    rq = [A.alloc([TN], BF16) for _ in range(2)]
    st_m = A.alloc([TN], F32)
    st_r = A.alloc([TN], F32)
    st_v = A.alloc([TN], F32)
    b_xs = [Buf(), Buf()]
    b_xm = [Buf(), Buf()]
    b_h = [Buf() for _ in range(FT)]
    b_rr = [[Buf() for _ in range(KT)] for _ in range(2)]
    b_rl = [Buf(), Buf()]
    b_tmp = [Buf(), Buf()]
    b_sg = [Buf(), Buf()]
    b_rb = [Buf(), Buf()]
    b_rq = [Buf(), Buf()]
    b_st = Buf()
    pg, pu, pd, pS1, pS2 = k.ps[0:2], k.ps[2:4], k.ps[4:6], k.ps[6], k.ps[7]
    bpg, bpu, bpd, bS1, bS2 = k.psb[0:2], k.psb[2:4], k.psb[4:6], k.psb[6], k.psb[7]
    tiles = list(tiles)
    cnt_f = [0]
    cnt_d = [0]

    def load_x(idx):
        i = tiles[idx]
        bi = idx % 2
        P.dma("sp", lambda e: e.dma_start(out=xs[bi], in_=src(i)), R=[src_bufs[i]], W=[b_xs[bi]], semkey="xs%d" % bi)

    def load_r(idx):
        i = tiles[idx]
        bi = idx % 2
        P.dma("sp", lambda e: e.dma_start(out=rr[bi], in_=src(i)), R=[src_bufs[i]], W=b_rr[bi] + [b_rl[bi]],
              semkey="rr%d" % bi)

    def modulate(idx):
        i = tiles[idx]
        bi = idx % 2
        c = 1 if i == 0 else 0
        for kt in range(KT):
            P.act(lambda e, kt=kt: e.activation(
                out=xm[bi][:, kt, :], in_=xs[bi][:, kt, :], func=AF.Identity,
                scale=k.mod1p[:, l, (j0 + 1) * 8 + kt, c:c + 1], bias=k.mods[:, l, j0 * 8 + kt, c:c + 1]),
                R=[b_xs[bi], k.b_mods], W=[b_xm[bi]])

    def tail_steps(idx):
        i = tiles[idx]
        bi = idx % 2
        steps = []

        def stat_math():
            P.dve(lambda e: e.tensor_scalar_mul(out=st_m, in0=pS1[:, :TN], scalar1=1.0 / D), R=[bS1], W=[b_st])
            P.dve(lambda e: e.tensor_tensor(out=st_v, in0=st_m, in1=st_m, op=ALU.mult), R=[b_st], W=[b_st])
            P.dve(lambda e: e.scalar_tensor_tensor(out=st_v, in0=pS2[:, :TN], scalar=1.0 / D, in1=st_v,
                                                   op0=ALU.mult, op1=ALU.subtract), R=[bS2, b_st], W=[b_st])
            P.act(lambda e: e.activation(out=st_r, in_=st_v, func=AF.Ln, bias=LN_EPS), R=[b_st], W=[b_st])
            P.act(lambda e: e.activation(out=st_r, in_=st_r, func=AF.Exp, scale=-0.5), R=[b_st], W=[b_st])
        steps.append(stat_math)
        for d in range(KT):
            def norm(d=d):
                P.dve(lambda e: e.tensor_tensor(out=rr[bi][:, d, :], in0=rr[bi][:, d, :], in1=st_m, op=ALU.subtract),
                      R=[b_st, b_rr[bi][d]], W=[b_rr[bi][d]])
                P.dve(lambda e: e.tensor_tensor(out=rr[bi][:, d, :], in0=rr[bi][:, d, :], in1=st_r, op=ALU.mult),
                      R=[b_st, b_rr[bi][d]], W=[b_rr[bi][d]])
            steps.append(norm)

            def affine(d=d):
                P.act(lambda e: e.activation(out=rr[bi][:, d, :], in_=rr[bi][:, d, :], func=AF.Identity,
                                             scale=k.lng[:, lnrow, d:d + 1], bias=k.lnb[:, lnrow, d:d + 1]),
                      R=[b_rr[bi][d], k.b_ln], W=[b_rr[bi][d]])
            steps.append(affine)

        def store():
            if final_out is not None and i >= 1:
                P.dma("sp", lambda e: e.dma_start(out=final_out[0](i), in_=rr[bi]), R=b_rr[bi], W=[final_out[1][i]],
                      semkey="st%d" % bi)
            else:
                P.dma("sp", lambda e: e.dma_start(out=dst(i), in_=rr[bi]), R=b_rr[bi], W=[dst_bufs[i]],
                      semkey="st%d" % bi)
        steps.append(store)
        return steps

    def tile_body(idx, deferred):
        i = tiles[idx]
        bi = idx % 2
        c = 1 if i == 0 else 0
        if idx + 1 < len(tiles):
            load_x(idx + 1)
        load_r(idx)
        if idx == 0:
            modulate(0)
        for f in range(FT):
            fb = cnt_f[0] % 2
            cnt_f[0] += 1
            g = f // 11
            for kt in range(KT):
                P.pe(lambda e, kt=kt, f=f, fb=fb: e.matmul(
                    pg[fb][:, :TN], lhsT=wgu[:, kt, f * 128:(f + 1) * 128], rhs=xm[bi][:, kt, :],
                    start=(kt == 0), stop=(kt == KT - 1)), R=[b_wgu[0][g][kt], b_xm[bi]], W=[bpg[fb]])
            for kt in range(KT):
                P.pe(lambda e, kt=kt, f=f, fb=fb: e.matmul(
                    pu[fb][:, :TN], lhsT=wgu[:, kt, DFF + f * 128:DFF + (f + 1) * 128], rhs=xm[bi][:, kt, :],
                    start=(kt == 0), stop=(kt == KT - 1)), R=[b_wgu[1][g][kt], b_xm[bi]], W=[bpu[fb]])
            if deferred and f >= 1:
                deferred.pop(0)()
            P.act(lambda e, fb=fb: e.activation(out=sg[fb], in_=pg[fb][:, :TN], func=AF.Exp, scale=-1.0),
                  R=[bpg[fb]], W=[b_sg[fb]])
            P.act(lambda e, fb=fb: e.activation(out=sg[fb], in_=sg[fb], func=AF.Ln, bias=1.0),
                  R=[b_sg[fb]], W=[b_sg[fb]])
            P.act(lambda e, fb=fb: e.activation(out=sg[fb], in_=sg[fb], func=AF.Exp, scale=-1.0),
                  R=[b_sg[fb]], W=[b_sg[fb]])
            P.dve(lambda e, fb=fb: e.tensor_tensor(out=sg[fb], in0=pg[fb][:, :TN], in1=sg[fb], op=ALU.mult),
                  R=[bpg[fb], b_sg[fb]], W=[b_sg[fb]])
            P.dve(lambda e, fb=fb, f=f: e.tensor_tensor(out=h[:, f, :], in0=pu[fb][:, :TN], in1=sg[fb], op=ALU.mult),
                  R=[bpu[fb], b_sg[fb]], W=[b_h[f]])
        while deferred:
            deferred.pop(0)()
        if idx + 1 < len(tiles):
            modulate(idx + 1)
        pend = None
        for d in range(KT):
            db = cnt_d[0] % 2
            cnt_d[0] += 1
            for f in range(FT):
                P.pe(lambda e, f=f, d=d, db=db: e.matmul(
                    pd[db][:, :TN], lhsT=wdn[:, f, d * 128:(d + 1) * 128], rhs=h[:, f, :],
                    start=(f == 0), stop=(f == FT - 1)), R=[b_wdn[f], b_h[f]], W=[bpd[db]])
            if pend is not None:
                pend()

            def epi(d=d, db=db):
                tb = d % 2
                P.act(lambda e: e.activation(out=tmp[tb], in_=pd[db][:, :TN], func=AF.Copy,
                                             scale=k.mod1p[:, l, (j0 + 2) * 8 + d, c:c + 1]),
                      R=[bpd[db], k.b_mods], W=[b_tmp[tb]])
                P.dve(lambda e: e.scalar_tensor_tensor(out=rr[bi][:, d, :], in0=rr[bi][:, d, :], scalar=ALPHA,
                                                       in1=tmp[tb], op0=ALU.mult, op1=ALU.add),
                      R=[b_tmp[tb], b_rr[bi][d]], W=[b_rr[bi][d]])
                P.act(lambda e: e.activation(out=rq[tb], in_=rr[bi][:, d, :], func=AF.Square),
                      R=[b_rr[bi][d]], W=[b_rq[tb]])
                P.dve(lambda e: e.tensor_copy(out=rb[tb], in_=rr[bi][:, d, :]), R=[b_rr[bi][d]], W=[b_rb[tb]])
            epi()

            def stats(d=d):
                tb = d % 2
                P.pe(lambda e: e.matmul(pS1[:, :TN], lhsT=k.ones_bf, rhs=rb[tb],
                                        start=(d == 0), stop=(d == KT - 1)), R=[b_rb[tb], k.b_ones], W=[bS1])
                P.pe(lambda e: e.matmul(pS2[:, :TN], lhsT=k.ones_bf, rhs=rq[tb],
                                        start=(d == 0), stop=(d == KT - 1)), R=[b_rq[tb], k.b_ones], W=[bS2])
            pend = stats
        return [pend] + tail_steps(idx)

    load_x(0)
    deferred = []
    for idx in range(len(tiles)):
        deferred = tile_body(idx, deferred)
    while deferred:
        deferred.pop(0)()


def emit_proj_ln(k, l, lnrow, gate_j, NK, wout, b_wout, rhs_fn, rhs_bufs_fn, src, src_bufs, dst, dst_bufs, tiles,
                 pre_tile=None):
    nc, P, A = k.nc, k.P, k.A
    rr = [A.alloc([KT, TN], F32) for _ in range(2)]
    tmp = [A.alloc([TN], F32) for _ in range(2)]
    rb = [A.alloc([TN], BF16) for _ in range(2)]
    rq = [A.alloc([TN], BF16) for _ in range(2)]
    st_m, st_r, st_v = A.alloc([TN], F32), A.alloc([TN], F32), A.alloc([TN], F32)
    b_rr = [[Buf() for _ in range(KT)] for _ in range(2)]
    b_tmp, b_rb, b_rq = [Buf(), Buf()], [Buf(), Buf()], [Buf(), Buf()]
    b_st = Buf()
    pd, pS1, pS2 = k.ps[4:6], k.ps[6], k.ps[7]
    bpd, bS1, bS2 = k.psb[4:6], k.psb[6], k.psb[7]
    cnt_d = [0]

    def tile(idx):
        i = tiles[idx]
        bi = idx % 2
        c = 1 if i == 0 else 0
        if pre_tile is not None:
            pre_tile(i, idx)
        P.dma("sp", lambda e: e.dma_start(out=rr[bi], in_=src(i)), R=[src_bufs[i]], W=b_rr[bi], semkey="prr%d" % bi)
        for d in range(KT):
            db = cnt_d[0] % 2
            cnt_d[0] += 1
            tb = d % 2
            for f in range(NK):
                P.pe(lambda e, f=f, d=d, db=db: e.matmul(
                    pd[db][:, :TN], lhsT=wout[:, f, d * 128:(d + 1) * 128], rhs=rhs_fn(i, f, idx),
                    start=(f == 0), stop=(f == NK - 1)), R=[b_wout] + rhs_bufs_fn(i, f, idx), W=[bpd[db]])
            P.act(lambda e, d=d, db=db, tb=tb: e.activation(out=tmp[tb], in_=pd[db][:, :TN], func=AF.Copy,
                                                          scale=k.mods[:, l, gate_j * 8 + d, c:c + 1]),
                  R=[bpd[db], k.b_mods], W=[b_tmp[tb]])
            P.dve(lambda e, d=d, tb=tb: e.scalar_tensor_tensor(out=rr[bi][:, d, :], in0=rr[bi][:, d, :], scalar=ALPHA,
                                                             in1=tmp[tb], op0=ALU.mult, op1=ALU.add),
                  R=[b_tmp[tb], b_rr[bi][d]], W=[b_rr[bi][d]])
            P.act(lambda e, d=d, tb=tb: e.activation(out=rq[tb], in_=rr[bi][:, d, :], func=AF.Square),
                  R=[b_rr[bi][d]], W=[b_rq[tb]])
            P.dve(lambda e, d=d, tb=tb: e.tensor_copy(out=rb[tb], in_=rr[bi][:, d, :]), R=[b_rr[bi][d]], W=[b_rb[tb]])
            P.pe(lambda e, d=d, tb=tb: e.matmul(pS1[:, :TN], lhsT=k.ones_bf, rhs=rb[tb],
                                              start=(d == 0), stop=(d == KT - 1)), R=[b_rb[tb], k.b_ones], W=[bS1])
            P.pe(lambda e, d=d, tb=tb: e.matmul(pS2[:, :TN], lhsT=k.ones_bf, rhs=rq[tb],
                                              start=(d == 0), stop=(d == KT - 1)), R=[b_rq[tb], k.b_ones], W=[bS2])
        P.dve(lambda e: e.tensor_scalar_mul(out=st_m, in0=pS1[:, :TN], scalar1=1.0 / D), R=[bS1], W=[b_st])
        P.dve(lambda e: e.tensor_tensor(out=st_v, in0=st_m, in1=st_m, op=ALU.mult), R=[b_st], W=[b_st])
        P.dve(lambda e: e.scalar_tensor_tensor(out=st_v, in0=pS2[:, :TN], scalar=1.0 / D, in1=st_v,
                                               op0=ALU.mult, op1=ALU.subtract), R=[bS2, b_st], W=[b_st])
        P.act(lambda e: e.activation(out=st_r, in_=st_v, func=AF.Ln, bias=LN_EPS), R=[b_st], W=[b_st])
        P.act(lambda e: e.activation(out=st_r, in_=st_r, func=AF.Exp, scale=-0.5), R=[b_st], W=[b_st])
        for d in range(KT):
            P.dve(lambda e, d=d: e.tensor_tensor(out=rr[bi][:, d, :], in0=rr[bi][:, d, :], in1=st_m, op=ALU.subtract),
                  R=[b_st, b_rr[bi][d]], W=[b_rr[bi][d]])
            P.dve(lambda e, d=d: e.tensor_tensor(out=rr[bi][:, d, :], in0=rr[bi][:, d, :], in1=st_r, op=ALU.mult),
                  R=[b_st, b_rr[bi][d]], W=[b_rr[bi][d]])
            P.act(lambda e, d=d: e.activation(out=rr[bi][:, d, :], in_=rr[bi][:, d, :], func=AF.Identity,
                                             scale=k.lng[:, lnrow, d:d + 1], bias=k.lnb[:, lnrow, d:d + 1]),
                  R=[b_rr[bi][d], k.b_ln], W=[b_rr[bi][d]])
        P.dma("sp", lambda e: e.dma_start(out=dst(i), in_=rr[bi]), R=b_rr[bi], W=[dst_bufs[i]], semkey="pst%d" % bi)

    for idx in range(len(tiles)):
        tile(idx)


def emit_attn(k, l, src, src_bufs, dst, dst_bufs):
    nc, P, A = k.nc, k.P, k.A
    A.reset()
    P.barrier(lambda e: e.memset(k.scr, 0.0))
    NQ = LLAT // 512
    win = A.alloc([KT, 1536], BF16)
    wout = A.alloc([KT, D], BF16)
    qT = A.alloc([8, LLAT], BF16)
    kT = A.alloc([2, T], BF16)
    V = A.alloc([T // 128, 256], BF16)
    Rm = A.alloc([128], BF16)
    gq = A.alloc([2], F32)
    b_win, b_wout, b_c = Buf(), Buf(), Buf()
    for kt in range(KT):
        P.dma("pool", lambda e, kt=kt: e.dma_start(out=win[:, kt, :], in_=k.attn_w_in[kt * 128:(kt + 1) * 128, :]),
              W=[b_win], semkey="awin") if kt == 0 else P.dma(
            "pool", lambda e, kt=kt: e.dma_start(out=win[:, kt, :], in_=k.attn_w_in[kt * 128:(kt + 1) * 128, :]),
            R=[], W=[Buf()], semkey="awin")
    for kt in range(KT):
        P.dma("pool", lambda e, kt=kt: e.dma_start(out=wout[:, kt, :], in_=k.attn_w_out[kt * 128:(kt + 1) * 128, :]),
              W=[b_wout] if kt == 0 else [Buf()], semkey="awout")
    P.dma("pool", lambda e: e.dma_start(out=Rm, in_=k.rotm), W=[b_c], semkey="ac")
    P.dma("sp", lambda e: e.dma_start(out=gq, in_=k.qk_gain), W=[b_c], semkey="ac")
    b_wl = Buf()
    P.act(lambda e: e.activation(out=k.scr[:, 0:1], in_=k.scr[:, 1:2], func=AF.Copy), R=[b_win, b_wout, b_c], W=[b_wl])
    mark_a = A.off
    xs = [A.alloc([KT, TN], F32) for _ in range(2)]
    xm = [A.alloc([KT, TN], BF16) for _ in range(2)]
    cs = [A.alloc([2, TN], F32) for _ in range(2)]
    qf = [A.alloc([TN], F32) for _ in range(2)]
    sq = [A.alloc([TN], BF16) for _ in range(2)]
    rs = [A.alloc([TN], F32) for _ in range(2)]
    qn = [A.alloc([TN], F32) for _ in range(2)]
    qnb = [A.alloc([TN], BF16) for _ in range(2)]
    t1 = [A.alloc([TN], F32) for _ in range(2)]
    b_xs, b_xm, b_cs = [Buf(), Buf()], [Buf(), Buf()], [Buf(), Buf()]
    b_qf, b_sq, b_rs, b_qn, b_qnb, b_t1 = ([Buf(), Buf()] for _ in range(6))
    b_qT = [[Buf() for _ in range(NTILE)] for _ in range(8)]
    b_kT = [[Buf() for _ in range(NTILE)] for _ in range(2)]
    b_V = [Buf() for _ in range(NTILE)]
    pp, pm, pr, pv = k.ps[0:2], k.ps[2:4], k.ps[4:6], k.ps[6:8]
    bpp, bpm, bpr, bpv = k.psb[0:2], k.psb[2:4], k.psb[4:6], k.psb[6:8]
    cn = [0]
    for i in range(NTILE):
        bi = i % 2
        c = 1 if i == 0 else 0
        P.dma("sp", lambda e, i=i, bi=bi: e.dma_start(out=xs[bi], in_=src(i)), R=[src_bufs[i]], W=[b_xs[bi]],
              semkey="axs%d" % bi)
        if i >= 1:
            P.dma("sp", lambda e, i=i, bi=bi: e.dma_start(out=cs[bi], in_=k.rope[:, :, (i - 1) * TN:i * TN]),
                  W=[b_cs[bi]], semkey="acs%d" % bi)
        for kt in range(KT):
            P.act(lambda e, kt=kt, bi=bi, c=c: e.activation(
                out=xm[bi][:, kt, :], in_=xs[bi][:, kt, :], func=AF.Identity,
                scale=k.mod1p[:, l, 4 * 8 + kt, c:c + 1], bias=k.mods[:, l, 3 * 8 + kt, c:c + 1]),
                R=[b_xs[bi], k.b_mods], W=[b_xm[bi]])
        fts = ([] if i == 0 else list(range(8))) + [8, 9]
        for ft in fts:
            j = cn[0] % 2
            cn[0] += 1
            isq = ft < 8
            for kt in range(KT):
                P.pe(lambda e, kt=kt, ft=ft, j=j, bi=bi: e.matmul(
                    pp[j][:, :TN], lhsT=win[:, kt, ft * 128:(ft + 1) * 128], rhs=xm[bi][:, kt, :],
                    start=(kt == 0), stop=(kt == KT - 1)), R=[b_wl, b_xm[bi]], W=[bpp[j]])
            P.act(lambda e, j=j: e.activation(out=qf[j], in_=pp[j][:, :TN], func=AF.Copy), R=[bpp[j]], W=[b_qf[j]])
            P.act(lambda e, j=j: e.activation(out=sq[j], in_=pp[j][:, :TN], func=AF.Square), R=[bpp[j]], W=[b_sq[j]])
            P.pe(lambda e, j=j: e.matmul(pm[j][:, :TN], lhsT=k.ones_bf, rhs=sq[j], start=True, stop=True),
                 R=[b_sq[j], k.b_ones], W=[bpm[j]])
            P.act(lambda e, j=j: e.activation(out=rs[j], in_=pm[j][:, :TN], func=AF.Ln, scale=1.0 / 128, bias=RMS_EPS),
                  R=[bpm[j]], W=[b_rs[j]])
            P.act(lambda e, j=j: e.activation(out=rs[j], in_=rs[j], func=AF.Exp, scale=-0.5), R=[b_rs[j]], W=[b_rs[j]])
            gcol = gq[:, 0:1] if isq else gq[:, 1:2]
            if i == 0:
                dstv = kT[:, ft - 8, 0:TN]
                P.dve(lambda e, j=j, gcol=gcol, dstv=dstv: e.scalar_tensor_tensor(
                    out=dstv, in0=qf[j], scalar=gcol, in1=rs[j], op0=ALU.mult, op1=ALU.mult),
                    R=[b_qf[j], b_rs[j], b_wl], W=[b_kT[ft - 8][i]])
                continue
            P.dve(lambda e, j=j, gcol=gcol: e.scalar_tensor_tensor(
                out=qn[j], in0=qf[j], scalar=gcol, in1=rs[j], op0=ALU.mult, op1=ALU.mult),
                R=[b_qf[j], b_rs[j], b_wl], W=[b_qn[j]])
            P.act(lambda e, j=j: e.activation(out=qnb[j], in_=qn[j], func=AF.Copy), R=[b_qn[j]], W=[b_qnb[j]])
            P.pe(lambda e, j=j: e.matmul(pr[j][:, :TN], lhsT=Rm, rhs=qnb[j], start=True, stop=True),
                 R=[b_qnb[j], b_wl], W=[bpr[j]])
            P.dve(lambda e, j=j, bi=bi: e.tensor_tensor(out=t1[j], in0=qn[j], in1=cs[bi][:, 0, :], op=ALU.mult),
                  R=[b_qn[j], b_cs[bi]], W=[b_t1[j]])
            P.dve(lambda e, j=j, bi=bi: e.tensor_tensor(out=qn[j], in0=pr[j][:, :TN], in1=cs[bi][:, 1, :], op=ALU.mult),
                  R=[bpr[j], b_cs[bi]], W=[b_qn[j]])
            if isq:
                dstv, db = qT[:, ft, (i - 1) * TN:i * TN], b_qT[ft][i]
            else:
                dstv, db = kT[:, ft - 8, i * TN:(i + 1) * TN], b_kT[ft - 8][i]
            P.dve(lambda e, j=j, dstv=dstv: e.tensor_tensor(out=dstv, in0=t1[j], in1=qn[j], op=ALU.add),
                  R=[b_t1[j], b_qn[j]], W=[db])
        for blk in range(2):
            j = cn[0] % 2
            cn[0] += 1
            for kt in range(KT):
                P.pe(lambda e, kt=kt, blk=blk, j=j, bi=bi: e.matmul(
                    pv[j][:, :256], lhsT=xm[bi][:, kt, blk * 128:(blk + 1) * 128], rhs=win[:, kt, 1280:1536],
                    start=(kt == 0), stop=(kt == KT - 1)), R=[b_wl, b_xm[bi]], W=[bpv[j]])
            P.act(lambda e, j=j, i=i, blk=blk: e.activation(out=V[:, i * 2 + blk, :], in_=pv[j][:, :256], func=AF.Copy),
                  R=[bpv[j]], W=[b_V[i]])
    A.off = mark_a
    P.barrier(lambda e: e.memset(k.scr, 0.0))
    pt = [A.alloc([512], BF16) for _ in range(3)]
    rec = [A.alloc([512], F32) for _ in range(2)]
    b_pt, b_rec = [Buf() for _ in range(3)], [Buf(), Buf()]
    pS, pO, pZ = k.ps[0:3], k.ps[3:5], k.ps[5:7]
    bpS, bpO, bpZ = k.psb[0:3], k.psb[3:5], k.psb[5:7]
    b_o = [[Buf() for _ in range(NQ)] for _ in range(8)]
    SCALE = 128 ** -0.5
    SHIFT = 8.0
    it = 0
    cs_ = 0
    NKB = T // 128
    for h in range(8):
        g = h // 4
        for qt in range(NQ):
            ob = it % 2
            it += 1
            qbufs = [b_qT[h][1 + 2 * qt], b_qT[h][2 + 2 * qt]]
            for kb in range(NKB):
                sb = cs_ % 3
                cs_ += 1
                P.pe(lambda e, g=g, kb=kb, h=h, qt=qt, sb=sb: e.matmul(
                    pS[sb][:, :], lhsT=kT[:, g, kb * 128:(kb + 1) * 128], rhs=qT[:, h, qt * 512:(qt + 1) * 512],
                    start=True, stop=True), R=[b_kT[g][kb // 2]] + qbufs, W=[bpS[sb]])
                P.act(lambda e, sb=sb: e.activation(out=pt[sb], in_=pS[sb][:, :], func=AF.Exp, scale=SCALE, bias=-SHIFT),
                      R=[bpS[sb]], W=[b_pt[sb]])
                P.pe(lambda e, g=g, kb=kb, sb=sb, ob=ob: e.matmul(
                    pO[ob][:, :], lhsT=V[:, kb, g * 128:(g + 1) * 128], rhs=pt[sb],
                    start=(kb == 0), stop=(kb == NKB - 1)), R=[b_V[kb // 2], b_pt[sb]], W=[bpO[ob]])
                P.pe(lambda e, kb=kb, sb=sb, ob=ob: e.matmul(
                    pZ[ob][:, :], lhsT=k.ones_bf, rhs=pt[sb],
                    start=(kb == 0), stop=(kb == NKB - 1)), R=[b_pt[sb], k.b_ones], W=[bpZ[ob]])
            P.dve(lambda e, ob=ob: e.reciprocal(out=rec[ob], in_=pZ[ob][:, :]), R=[bpZ[ob]], W=[b_rec[ob]])
            P.dve(lambda e, ob=ob, h=h, qt=qt: e.tensor_tensor(
                out=qT[:, h, qt * 512:(qt + 1) * 512], in0=pO[ob][:, :], in1=rec[ob], op=ALU.mult),
                R=[bpO[ob], b_rec[ob]], W=qbufs + [b_o[h][qt]])
    emit_proj_ln(k, l, l * 3 + 1, 5, 8, wout, b_wl,
                 lambda i, f, idx: qT[:, f, (i - 1) * TN:i * TN],
                 lambda i, f, idx: [b_o[f][(i - 1) // 2]],
                 src, src_bufs, dst, dst_bufs, list(range(1, NTILE)))


_CACHE = {}

ALL_STAGES = ("mods", "l0f1", "l0mix", "l0f2", "l1f1", "l1mix", "l1f2")


def prep_inputs(inputs, stages=ALL_STAGES, ncores=8):
    f32 = lambda n: np.asarray(inputs[n], dtype=np.float32)
    pl = np.ascontiguousarray
    x, ctx, c, c_ctx = f32("x"), f32("ctx"), f32("c"), f32("c_ctx")
    shared = {}
    shared["ada_b"] = pl(f32("ada_b").reshape(DEPTH, 72, 128).transpose(2, 0, 1))
    shared["ln_g"] = pl(f32("ln_g").reshape(DEPTH * 3, KT, 128).transpose(2, 0, 1))
    shared["ln_b"] = pl(f32("ln_b").reshape(DEPTH * 3, KT, 128).transpose(2, 0, 1))
    for l in range(DEPTH):
        if "mods" in stages:
            shared["ada_w%d" % l] = pl(f32("ada_w")[l])
        for s in range(2):
            if ("l%df%d" % (l, s + 1)) in stages:
                shared["w_gu%d%d" % (l, s)] = pl(f32("ffn_w_gu")[l, s])
                shared["w_dn%d%d" % (l, s)] = pl(f32("ffn_w_down")[l, s])
    if "l1mix" in stages:
        shared["attn_w_in"] = pl(f32("attn_w_in")[0])
        shared["attn_w_out"] = pl(f32("attn_w_out")[0])
        shared["qk_gain"] = pl(np.stack([f32("attn_q_norm")[0], f32("attn_k_norm")[0]], axis=1))
        rot = np.zeros((128, 128), np.float32)
        for i in range(64):
            rot[2 * i + 1, 2 * i] = -1.0
            rot[2 * i, 2 * i + 1] = 1.0
        shared["rotm"] = rot
        rows = LLAT // 64
        rowp = np.repeat(np.arange(rows, dtype=np.float32), 64)
        colp = np.tile(np.arange(64, dtype=np.float32), rows)
        inv = (np.float32(10000.0) ** (-np.arange(32, dtype=np.float32) / np.float32(32))).astype(np.float32)
        ang = np.concatenate([rowp[:, None] * inv, colp[:, None] * inv], axis=-1).astype(np.float32)
        ang2 = np.repeat(ang, 2, axis=1).T
        shared["rope"] = pl(np.stack([np.cos(ang2), np.sin(ang2)], axis=1).astype(np.float32))
    if "l0mix" in stages:
        shared["even_w_in"] = pl(f32("even_w_in")[0])
        shared["even_w_out"] = pl(f32("even_w_out")[0])
        jj, ii = np.meshgrid(np.arange(128), np.arange(128), indexing="ij")
        gd = np.zeros((8, 128, 128), np.float32)
        gd[0] = np.eye(128)
        gd[1] = 1.0
        gd[2] = (jj <= ii)
        gd[3] = (jj >= ii)
        gd[4] = np.where(ii >= jj, 0.0, -30000.0)
        gd[5] = np.where(ii <= jj, 0.0, -30000.0)
        gd[6] = (ii > jj)
        gd[7] = (ii < jj)
        shared["gdnc"] = pl(gd.transpose(1, 0, 2))
        shared["cw5"] = pl(f32("even_qkv_conv")[0].reshape(5, 12, 128).transpose(2, 1, 0))
        shared["cw31"] = pl(f32("cf_dw_conv")[0].reshape(31, 4, 128).transpose(2, 1, 0))
        shared["cvec"] = pl(np.stack([f32("cf_dw_bias")[0], f32("cf_ln_g")[0], f32("cf_ln_b")[0]], axis=0)
                            .reshape(3, 4, 128).transpose(2, 1, 0))
        rc = np.stack([f32("gdn_dt_bias")[0].reshape(8), f32("gdn_a_log")[0].reshape(8)], axis=0)
        shared["rowc"] = pl(np.broadcast_to(rc[None], (128, 2, 8)))
        shared["gnorm"] = pl(np.broadcast_to(np.tile(f32("gdn_out_norm")[0], 4)[None], (128, 512)))
    maps = []
    for b in range(ncores):
        m = dict(shared)
        m["xT"] = pl(x[b].T)
        m["ctxT"] = pl(ctx[b].T)
        m["cond"] = pl(np.stack([c[b], c_ctx], axis=1).reshape(KT, 128, 2).transpose(1, 0, 2))
        maps.append(m)
    return maps


def _silu_inplace(P, ap_f32, b, psum_src=None, bsrc=None):
    pass


def emit_even(k, l, src_cols, src_bufs, src, dst, dst_bufs):
    nc, P, A = k.nc, k.P, k.A
    A.reset()
    P.barrier(lambda e: e.memset(k.scr, 0.0))
    NCH = T // 128
    HALO = 15
    NW = TN + 2 * HALO
    QN = A.alloc([4, T], BF16)
    KN = A.alloc([4, T], BF16)
    VV = A.alloc([4, T], BF16)
    GT = A.alloc([NCH, 16], F32)
    cst = A.alloc([9, 128], F32)
    identb = A.alloc([128], BF16)
    cw5 = A.alloc([12, 5], F32)
    cw31 = A.alloc([4, 31], F32)
    cvec = A.alloc([4, 3], F32)
    rowc = A.alloc([3, 8], F32)
    gnorm = A.alloc([512], F32)
    b_cst = Buf()
    P.dma("sp", lambda e: e.dma_start(out=cst[:, 0:8, :], in_=k.gdnc), W=[b_cst], semkey="ec")
    P.dma("sp", lambda e: e.dma_start(out=cw5, in_=k.cw5), W=[b_cst], semkey="ec")
    P.dma("sp", lambda e: e.dma_start(out=cw31, in_=k.cw31), W=[b_cst], semkey="ec")
    P.dma("sp", lambda e: e.dma_start(out=cvec, in_=k.cvec), W=[b_cst], semkey="ec")
    P.dma("sp", lambda e: e.dma_start(out=rowc[:, 0:2, :], in_=k.rowc), W=[b_cst], semkey="ec")
    P.dma("sp", lambda e: e.dma_start(out=gnorm, in_=k.gnorm), W=[b_cst], semkey="ec")
    P.act(lambda e: e.activation(out=rowc[:, 2, :], in_=rowc[:, 1, :], func=AF.Exp), R=[b_cst], W=[b_cst])
    P.dve(lambda e: e.tensor_scalar_mul(out=rowc[:, 2, :], in0=rowc[:, 2, :], scalar1=-1.0), R=[b_cst], W=[b_cst])
    P.dve(lambda e: e.tensor_copy(out=identb, in_=cst[:, 0, :]), R=[b_cst], W=[b_cst])
    ident, onesF, UT, LT = cst[:, 0, :], cst[:, 1, :], cst[:, 2, :], cst[:, 3, :]
    negm = [cst[:, 4, :], cst[:, 5, :]]
    smask = [cst[:, 6, :], cst[:, 7, :]]
    A.set_mark2 = A.off
    win = A.alloc([KT, 3088], BF16)
    b_win = Buf()
    for kt in range(KT):
        P.dma("pool", lambda e, kt=kt: e.dma_start(out=win[:, kt, :], in_=k.even_w_in[kt * 128:(kt + 1) * 128, :]),
              W=[Buf()], semkey="ewin")
    b_wl = Buf()
    P.act(lambda e: e.activation(out=k.scr[:, 0:1], in_=k.scr[:, 1:2], func=AF.Copy), R=[b_cst], W=[b_wl])
    k.P.ops[-1].dw["ewin"] = k.P.dcnt["ewin"]
    xs = [A.alloc([KT, NW], F32)] * 2
    xm = [A.alloc([KT, NW], BF16)] * 2
    acc = [A.alloc([TN], F32) for _ in range(2)]
    sg = [A.alloc([NW], F32) for _ in range(2)]
    sqb = [A.alloc([TN], BF16) for _ in range(2)]
    rs = [A.alloc([TN], F32) for _ in range(2)]
    uu = [A.alloc([NW], F32) for _ in range(2)]
    yy = A.alloc([4, TN], F32)
    yb = [A.alloc([TN], BF16) for _ in range(2)]
    yq = [A.alloc([TN], BF16) for _ in range(2)]
    st_m, st_r, st_v = A.alloc([TN], F32), A.alloc([TN], F32), A.alloc([TN], F32)
    cft = [A.alloc([4, TN], BF16)] * 2
    zs = [A.alloc([512], F32)] * 2
    zb = [A.alloc([512], BF16)] * 2
    gt1 = [A.alloc([16], F32) for _ in range(2)]
    b_xs, b_xm = [Buf()] * 2, [Buf()] * 2
    b_acc, b_sg, b_sqb, b_rs, b_uu = ([Buf(), Buf()] for _ in range(5))
    b_yy = [Buf() for _ in range(4)]
    b_yb, b_yq, b_gt1 = ([Buf(), Buf()] for _ in range(3))
    b_cft, b_zs, b_zb = [Buf()] * 2, [Buf()] * 2, [Buf()] * 2
    b_st = Buf()
    b_QN = [[Buf() for _ in range(NCH)] for _ in range(4)]
    b_KN = [[Buf() for _ in range(NCH)] for _ in range(4)]
    b_VV = [[Buf() for _ in range(NCH)] for _ in range(4)]
    b_GT = [Buf() for _ in range(NCH)]
    b_CFd = [Buf() for _ in range(NTILE)]
    b_ZSd = [Buf() for _ in range(NCH)]
    pp, pq, pm = k.ps[0:2], k.ps[2:4], k.ps[4:6]
    bpp, bpq, bpm = k.psb[0:2], k.psb[2:4], k.psb[4:6]
    pS1, pS2 = k.ps[6], k.ps[7]
    bS1, bS2 = k.psb[6], k.psb[7]
    cn = [0]

    def sigmoid_from(psrc, bsrc, dstap, bdst, n):
        P.act(lambda e: e.activation(out=dstap, in_=psrc, func=AF.Exp, scale=-1.0), R=[bsrc], W=[bdst])
        P.act(lambda e: e.activation(out=dstap, in_=dstap, func=AF.Ln, bias=1.0), R=[bdst], W=[bdst])
        P.act(lambda e: e.activation(out=dstap, in_=dstap, func=AF.Exp, scale=-1.0), R=[bdst], W=[bdst])

    for i in range(NTILE):
        bi = i % 2
        c = 1 if i == 0 else 0
        t0 = i * TN
        s0, s1 = (0, LCTX) if i == 0 else (LCTX, T)
        lo = max(0, HALO - (t0 - s0))
        hi = min(NW, HALO + (s1 - t0))
        g0, g1 = t0 - HALO + lo, t0 - HALO + hi
        P.dma("sp", lambda e, bi=bi, lo=lo, hi=hi, g0=g0, g1=g1: e.dma_start(
            out=xs[bi][:, :, lo:hi], in_=src_cols(g0, g1)), R=[src_bufs[j] for j in range(max(0, i - 1), min(NTILE, i + 2))],
            W=[b_xs[bi]], semkey="exs%d" % bi)
        if lo > 0 or hi < NW:
            P.dve(lambda e, bi=bi: e.memset(xm[bi], 0.0), W=[b_xm[bi]])
        for kt in range(KT):
            P.act(lambda e, kt=kt, bi=bi, c=c, lo=lo, hi=hi: e.activation(
                out=xm[bi][:, kt, lo:hi], in_=xs[bi][:, kt, lo:hi], func=AF.Identity,
                scale=k.mod1p[:, l, 4 * 8 + kt, c:c + 1], bias=k.mods[:, l, 3 * 8 + kt, c:c + 1]),
                R=[b_xs[bi], k.b_mods], W=[b_xm[bi]])
        for ft in range(12):
            j = cn[0] % 2
            cn[0] += 1
            for kt in range(KT):
                P.pe(lambda e, kt=kt, ft=ft, j=j, bi=bi: e.matmul(
                    pp[j][:, :260], lhsT=win[:, kt, ft * 128:(ft + 1) * 128], rhs=xm[bi][:, kt, 13:273],
                    start=(kt == 0), stop=(kt == KT - 1)), R=[b_wl, b_xm[bi]], W=[bpp[j]])
            P.dve(lambda e, j=j, ft=ft: e.tensor_scalar_mul(out=acc[j], in0=pp[j][:, 0:TN], scalar1=cw5[:, ft, 0:1]),
                  R=[bpp[j], b_cst], W=[b_acc[j]])
            for tap in range(1, 5):
                P.dve(lambda e, j=j, ft=ft, tap=tap: e.scalar_tensor_tensor(
                    out=acc[j], in0=pp[j][:, tap:tap + TN], scalar=cw5[:, ft, tap:tap + 1], in1=acc[j],
                    op0=ALU.mult, op1=ALU.add), R=[bpp[j], b_cst, b_acc[j]], W=[b_acc[j]])
            sigmoid_from(acc[j], b_acc[j], sg[j][:, :TN], b_sg[j], TN)
            P.dve(lambda e, j=j: e.tensor_tensor(out=acc[j], in0=acc[j], in1=sg[j][:, :TN], op=ALU.mult),
                  R=[b_sg[j], b_acc[j]], W=[b_acc[j]])
            hh = ft % 4
            if ft >= 8:
                P.dve(lambda e, j=j, hh=hh, t0=t0: e.tensor_copy(out=VV[:, hh, t0:t0 + TN], in_=acc[j]),
                      R=[b_acc[j]], W=[b_VV[hh][2 * i], b_VV[hh][2 * i + 1]])
                continue
            P.act(lambda e, j=j: e.activation(out=sqb[j], in_=acc[j], func=AF.Square), R=[b_acc[j]], W=[b_sqb[j]])
            P.pe(lambda e, j=j: e.matmul(pm[j][:, :TN], lhsT=k.ones_bf, rhs=sqb[j], start=True, stop=True),
                 R=[b_sqb[j], k.b_ones], W=[bpm[j]])
            P.act(lambda e, j=j: e.activation(out=rs[j], in_=pm[j][:, :TN], func=AF.Ln, bias=RMS_EPS),
                  R=[bpm[j]], W=[b_rs[j]])
            P.act(lambda e, j=j: e.activation(out=rs[j], in_=rs[j], func=AF.Exp, scale=-0.5), R=[b_rs[j]], W=[b_rs[j]])
            if ft < 4:
                P.dve(lambda e, j=j, hh=hh, t0=t0: e.scalar_tensor_tensor(
                    out=QN[:, hh, t0:t0 + TN], in0=acc[j], scalar=128 ** -0.5, in1=rs[j], op0=ALU.mult, op1=ALU.mult),
                    R=[b_acc[j], b_rs[j]], W=[b_QN[hh][2 * i], b_QN[hh][2 * i + 1]])
            else:
                P.dve(lambda e, j=j, hh=hh, t0=t0: e.tensor_tensor(
                    out=KN[:, hh, t0:t0 + TN], in0=acc[j], in1=rs[j], op=ALU.mult),
                    R=[b_acc[j], b_rs[j]], W=[b_KN[hh][2 * i], b_KN[hh][2 * i + 1]])
        for ct in range(4):
            j = cn[0] % 2
            cn[0] += 1
            for kt in range(KT):
                P.pe(lambda e, kt=kt, ct=ct, j=j, bi=bi: e.matmul(
                    pp[j][:, :NW], lhsT=win[:, kt, 2064 + ct * 128:2064 + (ct + 1) * 128], rhs=xm[bi][:, kt, :],
                    start=(kt == 0), stop=(kt == KT - 1)), R=[b_wl, b_xm[bi]], W=[bpp[j]])
            for kt in range(KT):
                P.pe(lambda e, kt=kt, ct=ct, j=j, bi=bi: e.matmul(
                    pq[j][:, :NW], lhsT=win[:, kt, 2576 + ct * 128:2576 + (ct + 1) * 128], rhs=xm[bi][:, kt, :],
                    start=(kt == 0), stop=(kt == KT - 1)), R=[b_wl, b_xm[bi]], W=[bpq[j]])
            sigmoid_from(pq[j][:, :NW], bpq[j], sg[j], b_sg[j], NW)
            P.dve(lambda e, j=j: e.tensor_tensor(out=uu[j], in0=pp[j][:, :NW], in1=sg[j], op=ALU.mult),
                  R=[bpp[j], b_sg[j]], W=[b_uu[j]])
            P.dve(lambda e, j=j, ct=ct: e.tensor_scalar(out=yy[:, ct, :], in0=uu[j][:, 0:TN], scalar1=cw31[:, ct, 0:1],
                                                       scalar2=cvec[:, ct, 0:1], op0=ALU.mult, op1=ALU.add),
                  R=[b_uu[j], b_cst], W=[b_yy[ct]])
            for tap in range(1, 31):
                P.dve(lambda e, j=j, ct=ct, tap=tap: e.scalar_tensor_tensor(
                    out=yy[:, ct, :], in0=uu[j][:, tap:tap + TN], scalar=cw31[:, ct, tap:tap + 1], in1=yy[:, ct, :],
                    op0=ALU.mult, op1=ALU.add), R=[b_uu[j], b_cst, b_yy[ct]], W=[b_yy[ct]])
            P.act(lambda e, j=j, ct=ct: e.activation(out=yq[j], in_=yy[:, ct, :], func=AF.Square), R=[b_yy[ct]], W=[b_yq[j]])
            P.dve(lambda e, j=j, ct=ct: e.tensor_copy(out=yb[j], in_=yy[:, ct, :]), R=[b_yy[ct]], W=[b_yb[j]])
            P.pe(lambda e, j=j, ct=ct: e.matmul(pS1[:, :TN], lhsT=k.ones_bf, rhs=yb[j], start=(ct == 0), stop=(ct == 3)),
                 R=[b_yb[j], k.b_ones], W=[bS1])
            P.pe(lambda e, j=j, ct=ct: e.matmul(pS2[:, :TN], lhsT=k.ones_bf, rhs=yq[j], start=(ct == 0), stop=(ct == 3)),
                 R=[b_yq[j], k.b_ones], W=[bS2])
        P.dve(lambda e: e.tensor_scalar_mul(out=st_m, in0=pS1[:, :TN], scalar1=1.0 / 512), R=[bS1], W=[b_st])
        P.dve(lambda e: e.tensor_tensor(out=st_v, in0=st_m, in1=st_m, op=ALU.mult), R=[b_st], W=[b_st])
        P.dve(lambda e: e.scalar_tensor_tensor(out=st_v, in0=pS2[:, :TN], scalar=1.0 / 512, in1=st_v,
                                               op0=ALU.mult, op1=ALU.subtract), R=[bS2, b_st], W=[b_st])
        P.act(lambda e: e.activation(out=st_r, in_=st_v, func=AF.Ln, bias=LN_EPS), R=[b_st], W=[b_st])
        P.act(lambda e: e.activation(out=st_r, in_=st_r, func=AF.Exp, scale=-0.5), R=[b_st], W=[b_st])
        for ct in range(4):
            j = ct % 2
            P.dve(lambda e, ct=ct: e.tensor_tensor(out=yy[:, ct, :], in0=yy[:, ct, :], in1=st_m, op=ALU.subtract),
                  R=[b_st, b_yy[ct]], W=[b_yy[ct]])
            P.dve(lambda e, ct=ct: e.tensor_tensor(out=yy[:, ct, :], in0=yy[:, ct, :], in1=st_r, op=ALU.mult),
                  R=[b_st, b_yy[ct]], W=[b_yy[ct]])
            P.act(lambda e, ct=ct: e.activation(out=yy[:, ct, :], in_=yy[:, ct, :], func=AF.Identity,
                                              scale=cvec[:, ct, 1:2], bias=cvec[:, ct, 2:3]),
                  R=[b_yy[ct], b_cst], W=[b_yy[ct]])
            sigmoid_from(yy[:, ct, :], b_yy[ct], sg[j][:, :TN], b_sg[j], TN)
            P.dve(lambda e, ct=ct, j=j, bi=bi: e.tensor_tensor(out=cft[bi][:, ct, :], in0=yy[:, ct, :], in1=sg[j][:, :TN],
                                                             op=ALU.mult), R=[b_yy[ct], b_sg[j]], W=[b_cft[bi]])
        P.dma("sp", lambda e, bi=bi, t0=t0: e.dma_start(
            out=k.CFd[:, t0:t0 + TN].rearrange("(ct p) t -> p ct t", p=128), in_=cft[bi]),
            R=[b_cft[bi]], W=[b_CFd[i]], semkey="ecf%d" % bi)
        for blk in range(2):
            j = cn[0] % 2
            cn[0] += 1
            ch = 2 * i + blk
            c0 = HALO + blk * 128
            for kt in range(KT):
                P.pe(lambda e, kt=kt, j=j, bi=bi, c0=c0: e.matmul(
                    pp[j][:, :512], lhsT=xm[bi][:, kt, c0:c0 + 128], rhs=win[:, kt, 1536:2048],
                    start=(kt == 0), stop=(kt == KT - 1)), R=[b_wl, b_xm[bi]], W=[bpp[j]])
            for kt in range(KT):
                P.pe(lambda e, kt=kt, j=j, bi=bi, c0=c0: e.matmul(
                    pq[j][:, :16], lhsT=xm[bi][:, kt, c0:c0 + 128], rhs=win[:, kt, 2048:2064],
                    start=(kt == 0), stop=(kt == KT - 1)), R=[b_wl, b_xm[bi]], W=[bpq[j]])
            sigmoid_from(pp[j][:, :512], bpp[j], zs[j], b_zs[j], 512)
            P.dve(lambda e, j=j: e.tensor_tensor(out=zb[j], in0=pp[j][:, :512], in1=zs[j], op=ALU.mult),
                  R=[bpp[j], b_zs[j]], W=[b_zb[j]])
            P.dma("sp", lambda e, j=j, ch=ch: e.dma_start(out=k.ZSd[ch * 128:(ch + 1) * 128, :], in_=zb[j]),
                  R=[b_zb[j]], W=[b_ZSd[ch]], semkey="ezs%d" % j)
            P.dve(lambda e, j=j: e.tensor_tensor(out=gt1[j][:, 0:8], in0=pq[j][:, 0:8], in1=rowc[:, 0, :], op=ALU.add),
                  R=[bpq[j], b_cst], W=[b_gt1[j]])
            P.act(lambda e, j=j: e.activation(out=gt1[j][:, 0:8], in_=gt1[j][:, 0:8], func=AF.Exp), R=[b_gt1[j]], W=[b_gt1[j]])
            P.act(lambda e, j=j: e.activation(out=gt1[j][:, 0:8], in_=gt1[j][:, 0:8], func=AF.Ln, bias=1.0),
                  R=[b_gt1[j]], W=[b_gt1[j]])
            P.dve(lambda e, j=j, ch=ch: e.tensor_tensor(out=GT[:, ch, 0:8], in0=gt1[j][:, 0:8], in1=rowc[:, 2, :], op=ALU.mult),
                  R=[b_gt1[j], b_cst], W=[b_GT[ch]])
            P.act(lambda e, j=j: e.activation(out=gt1[j][:, 8:16], in_=pq[j][:, 8:16], func=AF.Exp, scale=-1.0),
                  R=[bpq[j]], W=[b_gt1[j]])
            P.act(lambda e, j=j: e.activation(out=gt1[j][:, 8:16], in_=gt1[j][:, 8:16], func=AF.Ln, bias=1.0),
                  R=[b_gt1[j]], W=[b_gt1[j]])
            P.act(lambda e, j=j, ch=ch: e.activation(out=GT[:, ch, 8:16], in_=gt1[j][:, 8:16], func=AF.Exp, scale=-1.0),
                  R=[b_gt1[j]], W=[b_GT[ch]])
    A.off = A.set_mark2
    P.barrier(lambda e: e.memset(k.scr, 0.0))
    insts = [(d, h) for d in range(2) for h in range(4)]
    order = [list(range(NCH)), [1, 0] + list(range(NCH - 1, 1, -1))]

    def mk(n, shape, dt):
        return [A.alloc(shape, dt) for _ in range(n)]
    S_ = mk(8, [128], F32)
    Sb = mk(8, [128], BF16)
    cols = mk(8, [16], F32)
    ktl = mk(8, [128], BF16)
    vtk = mk(8, [128], F32)
    dg = mk(8, [128], F32)
    DT = mk(8, [128], F32)
    AT = mk(8, [128], BF16)
    WA = mk(8, [128], F32)
    WB = mk(8, [128], F32)
    PP = mk(8, [128], F32)
    r2 = mk(8, [128], F32)
    vn = mk(8, [128], BF16)
    o1 = mk(8, [128], F32)
    oo = mk(8, [128], F32)
    bI = [Buf() for _ in range(8)]
    bO = [Buf() for _ in range(8)]
    b_Od = [[Buf() for _ in range(NCH)] for _ in range(2)]
    for n in range(8):
        P.dve(lambda e, n=n: e.memset(S_[n], 0.0), W=[bI[n]])
        P.dve(lambda e, n=n: e.memset(Sb[n], 0.0), W=[bI[n]])
    bank = k.ps
    bB = k.psb

    def reg(n, r):
        return bank[n][:, r * 128:(r + 1) * 128]

    def regb(n):
        return bank[n][:, 384:448].bitcast(BF16)

    for step in range(NCH):
        chs = [order[d][step] for (d, h) in insts]

        def each(fn):
            for n, (d, h) in enumerate(insts):
                fn(n, d, h, chs[n], chs[n] * 128)
        def sa(n, d, h, ch, tk):
            gcol = GT[:, ch, d * 4 + h:d * 4 + h + 1]
            bcol = GT[:, ch, 8 + d * 4 + h:8 + d * 4 + h + 1]
            P.pe(lambda e: e.matmul(reg(n, 0)[:, 0:1], lhsT=(UT if d == 0 else LT), rhs=gcol, start=True, stop=True),
                 R=[b_GT[ch], b_cst], W=[bB[n]])
            P.pe(lambda e: e.matmul(reg(n, 0)[:, 1:2], lhsT=onesF, rhs=gcol, start=True, stop=True),
                 R=[b_GT[ch], b_cst], W=[bB[n]])
            cl = cols[n]
            P.dve(lambda e: e.tensor_copy(out=cl[:, 0:2], in_=reg(n, 0)[:, 0:2]), R=[bB[n]], W=[bI[n]])
            P.dve(lambda e: e.tensor_scalar_mul(out=cl[:, 2:3], in0=cl[:, 0:1], scalar1=-1.0), R=[bI[n]], W=[bI[n]])
            P.act(lambda e: e.activation(out=cl[:, 3:4], in_=cl[:, 0:1], func=AF.Exp), R=[bI[n]], W=[bI[n]])
            P.act(lambda e: e.activation(out=cl[:, 4:5], in_=cl[:, 0:1], func=AF.Exp, scale=-1.0, bias=cl[:, 1:2]),
                  R=[bI[n]], W=[bI[n]])
            P.act(lambda e: e.activation(out=cl[:, 5:6], in_=cl[:, 1:2], func=AF.Exp), R=[bI[n]], W=[bI[n]])
            P.dve(lambda e: e.scalar_tensor_tensor(out=cl[:, 6:7], in0=bcol, scalar=-1.0, in1=cl[:, 3:4],
                                                   op0=ALU.mult, op1=ALU.mult), R=[bI[n], b_GT[ch]], W=[bI[n]])
            P.dve(lambda e: e.tensor_scalar_mul(out=cl[:, 7:8], in0=cl[:, 3:4], scalar1=-1.0), R=[bI[n]], W=[bI[n]])
        each(sa)
        def sb(n, d, h, ch, tk):
            P.pe(lambda e: e.transpose(regb(n), KN[:, h, tk:tk + 128], identb), R=[b_KN[h][ch], b_cst], W=[bB[n]])
            P.dve(lambda e: e.tensor_scalar_mul(out=ktl[n], in0=regb(n), scalar1=cols[n][:, 4:5]),
                  R=[bB[n], bI[n]], W=[bI[n]])
        each(sb)

        def sb2(n, d, h, ch, tk):
            P.pe(lambda e: e.transpose(regb(n), VV[:, h, tk:tk + 128], identb), R=[b_VV[h][ch], b_cst], W=[bB[n]])
            P.act(lambda e: e.activation(out=vtk[n], in_=regb(n), func=AF.Copy), R=[bB[n]], W=[bI[n]])
        each(sb2)
        def sc(n, d, h, ch, tk):
            P.dve(lambda e: e.tensor_scalar_mul(out=dg[n], in0=ident, scalar1=cols[n][:, 0:1]), R=[bI[n], b_cst], W=[bI[n]])
            P.pe(lambda e: e.matmul(reg(n, 0), lhsT=onesF, rhs=dg[n], start=True, stop=True), R=[bI[n], b_cst], W=[bB[n]])
            P.pe(lambda e: e.matmul(reg(n, 1), lhsT=KN[:, h, tk:tk + 128], rhs=KN[:, h, tk:tk + 128], start=True, stop=True),
                 R=[b_KN[h][ch]], W=[bB[n]])
            P.pe(lambda e: e.matmul(reg(n, 2), lhsT=KN[:, h, tk:tk + 128], rhs=QN[:, h, tk:tk + 128], start=True, stop=True),
                 R=[b_KN[h][ch], b_QN[h][ch]], W=[bB[n]])
            P.dve(lambda e: e.tensor_tensor(out=DT[n], in0=reg(n, 0), in1=negm[d], op=ALU.add), R=[bB[n], b_cst], W=[bI[n]])
            P.act(lambda e: e.activation(out=DT[n], in_=DT[n], func=AF.Exp, bias=cols[n][:, 2:3]), R=[bI[n]], W=[bI[n]])
            P.dve(lambda e: e.tensor_tensor(out=AT[n], in0=reg(n, 2), in1=DT[n], op=ALU.mult), R=[bB[n], bI[n]], W=[bI[n]])
            P.dve(lambda e: e.tensor_tensor(out=WA[n], in0=reg(n, 1), in1=DT[n], op=ALU.mult), R=[bB[n], bI[n]], W=[bI[n]])
            bcol = GT[:, ch, 8 + d * 4 + h:8 + d * 4 + h + 1]
            P.dve(lambda e: e.scalar_tensor_tensor(out=WA[n], in0=WA[n], scalar=bcol, in1=smask[d],
                                                   op0=ALU.mult, op1=ALU.mult), R=[bI[n], b_GT[ch], b_cst], W=[bI[n]])
        each(sc)
        def se(n, d, h, ch, tk):
            P.pe(lambda e: e.transpose(reg(n, 3), WA[n], ident), R=[bI[n], b_cst], W=[bB[n]])
            P.act(lambda e: e.activation(out=WB[n], in_=reg(n, 3), func=AF.Copy), R=[bB[n]], W=[bI[n]])
            P.dve(lambda e: e.tensor_tensor(out=PP[n], in0=ident, in1=WA[n], op=ALU.subtract), R=[bI[n], b_cst], W=[bI[n]])
        each(se)
        for lev in range(6):
            last = lev == 5

            def sf(n, d, h, ch, tk):
                P.pe(lambda e: e.matmul(reg(n, 1), lhsT=WA[n], rhs=WB[n], start=True, stop=True), R=[bI[n]], W=[bB[n]])
                if not last:
                    P.pe(lambda e: e.matmul(reg(n, 2), lhsT=WB[n], rhs=WA[n], start=True, stop=True), R=[bI[n]], W=[bB[n]])
                P.act(lambda e: e.activation(out=WB[n], in_=reg(n, 1), func=AF.Copy), R=[bB[n]], W=[bI[n]])
                if not last:
                    P.dve(lambda e: e.tensor_copy(out=WA[n], in_=reg(n, 2)), R=[bB[n]], W=[bI[n]])
                P.pe(lambda e: e.matmul(reg(n, 3), lhsT=WB[n], rhs=PP[n], start=True, stop=True), R=[bI[n]], W=[bB[n]])
                P.dve(lambda e: e.tensor_tensor(out=PP[n], in0=reg(n, 3), in1=PP[n], op=ALU.add), R=[bB[n], bI[n]], W=[bI[n]])
            each(sf)
        def sg1(n, d, h, ch, tk):
            P.pe(lambda e: e.matmul(reg(n, 0), lhsT=KN[:, h, tk:tk + 128], rhs=Sb[n], start=True, stop=True),
                 R=[b_KN[h][ch], bI[n]], W=[bB[n]])
            P.pe(lambda e: e.matmul(reg(n, 2), lhsT=QN[:, h, tk:tk + 128], rhs=Sb[n], start=True, stop=True),
                 R=[b_QN[h][ch], bI[n]], W=[bB[n]])
            P.dve(lambda e: e.scalar_tensor_tensor(out=r2[n], in0=reg(n, 0), scalar=cols[n][:, 7:8], in1=vtk[n],
                                                   op0=ALU.mult, op1=ALU.add), R=[bB[n], bI[n]], W=[bI[n]])
            P.pe(lambda e: e.matmul(reg(n, 1), lhsT=PP[n], rhs=r2[n], start=True, stop=True), R=[bI[n]], W=[bB[n]])
            bcol = GT[:, ch, 8 + d * 4 + h:8 + d * 4 + h + 1]
            P.dve(lambda e: e.tensor_scalar_mul(out=vn[n], in0=reg(n, 1), scalar1=bcol), R=[bB[n], b_GT[ch]], W=[bI[n]])
        each(sg1)

        def sg2(n, d, h, ch, tk):
            P.pe(lambda e: e.matmul(reg(n, 3), lhsT=AT[n], rhs=vn[n], start=True, stop=True), R=[bI[n]], W=[bB[n]])
            P.pe(lambda e: e.matmul(reg(n, 0), lhsT=ktl[n], rhs=vn[n], start=True, stop=True), R=[bI[n]], W=[bB[n]])
            P.act(lambda e: e.activation(out=o1[n], in_=reg(n, 3), func=AF.Copy), R=[bB[n]], W=[bI[n]])
            P.dve(lambda e: e.scalar_tensor_tensor(out=oo[n], in0=reg(n, 2), scalar=cols[n][:, 3:4], in1=o1[n],
                                                   op0=ALU.mult, op1=ALU.add), R=[bB[n], bI[n]], W=[bO[n]])
            P.dma("sp", lambda e: e.dma_start(out=k.Od[d][tk:tk + 128, h * 128:(h + 1) * 128], in_=oo[n]),
                  R=[bO[n]], W=[b_Od[d][ch]], semkey="eo%d" % n)
            P.dve(lambda e: e.scalar_tensor_tensor(out=S_[n], in0=S_[n], scalar=cols[n][:, 5:6], in1=reg(n, 0),
                                                   op0=ALU.mult, op1=ALU.add), R=[bB[n], bI[n]], W=[bI[n]])
            P.act(lambda e: e.activation(out=Sb[n], in_=S_[n], func=AF.Copy), R=[bI[n]], W=[bI[n]])
        each(sg2)
    A.off = A.set_mark2
    P.barrier(lambda e: e.memset(k.scr, 0.0))
    wout = A.alloc([KT, D], BF16)
    for kt in range(KT):
        P.dma("pool", lambda e, kt=kt: e.dma_start(out=wout[:, kt, :], in_=k.even_w_out[kt * 128:(kt + 1) * 128, :]),
              W=[Buf()], semkey="ewout")
    b_wo = Buf()
    P.act(lambda e: e.activation(out=k.scr[:, 0:1], in_=k.scr[:, 1:2], func=AF.Copy), W=[b_wo])
    k.P.ops[-1].dw["ewout"] = k.P.dcnt["ewout"]
    of_ = [A.alloc([512], F32) for _ in range(2)]
    ob_ = [A.alloc([512], F32) for _ in range(2)]
    zl = [A.alloc([512], BF16) for _ in range(2)]
    sqj = A.alloc([512], F32)
    ss = [A.alloc([4], F32) for _ in range(2)]
    gmb = [A.alloc([512], BF16) for _ in range(2)]
    gTt = [A.alloc([4, TN], BF16) for _ in range(2)]
    cfl = [A.alloc([4, TN], BF16) for _ in range(2)]
    b_of, b_ob, b_zl, b_ss, b_gmb, b_gT, b_cfl = ([Buf(), Buf()] for _ in range(7))
    b_sqj = Buf()
    ptr = k.ps[0:2]
    bptr = k.psb[0:2]
    cnm = [0]

    def pre_tile(i, idx):
        bi = idx % 2
        for blk in range(2):
            j = cnm[0] % 2
            cnm[0] += 1
            ch = 2 * i + blk
            P.dma("sp", lambda e, j=j, ch=ch: e.dma_start(out=of_[j], in_=k.Od[0][ch * 128:(ch + 1) * 128, :]),
                  R=[b_Od[0][ch]], W=[b_of[j]], semkey="mof%d" % j)
            P.dma("sp", lambda e, j=j, ch=ch: e.dma_start(out=ob_[j], in_=k.Od[1][ch * 128:(ch + 1) * 128, :]),
                  R=[b_Od[1][ch]], W=[b_ob[j]], semkey="mob%d" % j)
            P.dma("sp", lambda e, j=j, ch=ch: e.dma_start(out=zl[j], in_=k.ZSd[ch * 128:(ch + 1) * 128, :]),
                  R=[b_ZSd[ch]], W=[b_zl[j]], semkey="mzl%d" % j)
            P.dve(lambda e, j=j: e.tensor_tensor(out=of_[j], in0=of_[j], in1=ob_[j], op=ALU.add),
                  R=[b_of[j], b_ob[j]], W=[b_of[j]])
            P.dve(lambda e, j=j: e.memset(ss[j], 0.0), W=[b_ss[j]])
            for hh in range(4):
                P.act(lambda e, j=j, hh=hh: e.activation(out=sqj[:, hh * 128:(hh + 1) * 128],
                                                        in_=of_[j][:, hh * 128:(hh + 1) * 128], func=AF.Square,
                                                        accum_out=ss[j][:, hh:hh + 1]),
                      R=[b_of[j]], W=[b_sqj, b_ss[j]])
            P.act(lambda e, j=j: e.activation(out=ss[j], in_=ss[j], func=AF.Ln, scale=1.0 / 128, bias=RMS_EPS),
                  R=[b_ss[j]], W=[b_ss[j]])
            P.act(lambda e, j=j: e.activation(out=ss[j], in_=ss[j], func=AF.Exp, scale=-0.5), R=[b_ss[j]], W=[b_ss[j]])
            for hh in range(4):
                P.dve(lambda e, j=j, hh=hh: e.scalar_tensor_tensor(
                    out=of_[j][:, hh * 128:(hh + 1) * 128], in0=of_[j][:, hh * 128:(hh + 1) * 128],
                    scalar=ss[j][:, hh:hh + 1], in1=gnorm[:, hh * 128:(hh + 1) * 128], op0=ALU.mult, op1=ALU.mult),
                    R=[b_of[j], b_ss[j], b_cst], W=[b_of[j]])
            P.dve(lambda e, j=j: e.tensor_tensor(out=gmb[j], in0=of_[j], in1=zl[j], op=ALU.mult),
                  R=[b_of[j], b_zl[j]], W=[b_gmb[j]])
            for hh in range(4):
                jj = (hh + blk) % 2
                P.pe(lambda e, j=j, hh=hh, jj=jj: e.transpose(ptr[jj][:, 0:64].bitcast(BF16),
                                                              gmb[j][:, hh * 128:(hh + 1) * 128], identb),
                     R=[b_gmb[j], b_cst], W=[bptr[jj]])
                P.act(lambda e, hh=hh, jj=jj, bi=bi, blk=blk: e.activation(
                    out=gTt[bi][:, hh, blk * 128:(blk + 1) * 128], in_=ptr[jj][:, 0:64].bitcast(BF16), func=AF.Copy),
                    R=[bptr[jj]], W=[b_gT[bi]])
        P.dma("sp", lambda e, bi=bi, i=i: e.dma_start(
            out=cfl[bi], in_=k.CFd[:, i * TN:(i + 1) * TN].rearrange("(ct p) t -> p ct t", p=128)),
            R=[b_CFd[i]], W=[b_cfl[bi]], semkey="mcf%d" % bi)

    def rhs_fn(i, f, idx):
        bi = idx % 2
        return gTt[bi][:, f, :] if f < 4 else cfl[bi][:, f - 4, :]

    def rhs_bufs(i, f, idx):
        bi = idx % 2
        return [b_gT[bi]] if f < 4 else [b_cfl[bi]]

    emit_proj_ln(k, l, l * 3 + 1, 5, 8, wout, b_wo, rhs_fn, rhs_bufs, src, src_bufs, dst, dst_bufs,
                 list(range(NTILE)), pre_tile=pre_tile)


def kernel(**inputs):
    if "nc" not in _CACHE:
        _CACHE["nc"] = build_program()[0]
    nc = _CACHE["nc"]
    maps = prep_inputs(inputs)
    res = run_bass_kernel_spmd(nc, maps, core_ids=list(range(8)))
    out = np.stack([np.ascontiguousarray(r["yT"].T) for r in res.results], axis=0)
    return out.astype(np.float32)
```

```python
import contextlib
import os
import numpy as np
import concourse.bass as bass
import concourse.mybir as mybir
from concourse.bass_utils import run_bass_kernel_spmd

F32 = mybir.dt.float32
BF16 = mybir.dt.bfloat16
AF = mybir.ActivationFunctionType
ALU = mybir.AluOpType
AX = mybir.AxisListType

ENGS = ("pe", "act", "dve", "pool", "sp")
SAME_ENGINE_SYNC = False

D = 1024
KT = 8
DFF = 2816
FT = 22
LCTX = 256
LLAT = 4096
T = LCTX + LLAT
TN = 256
NTILE = T // TN
DEPTH = 2
ALPHA = (2.0 * DEPTH) ** 0.25
LN_EPS = 1e-5
RMS_EPS = 1e-6


class Buf:
    __slots__ = ("name", "lw", "rd")

    def __init__(self, name=""):
        self.name = name
        self.lw = None
        self.rd = {}


class Op:
    __slots__ = ("eng", "fn", "cw", "dw", "is_dma", "need_inc", "ticket", "semkey", "semval", "small")

    def __init__(self, eng, fn, is_dma, semkey):
        self.eng = eng
        self.fn = fn
        self.cw = {}
        self.dw = {}
        self.is_dma = is_dma
        self.need_inc = False
        self.ticket = 0
        self.semkey = semkey
        self.semval = 0
        self.small = False


class Prog:
    def __init__(self):
        self.small_on = False
        self.ops = []
        self.dcnt = {}
        self.last = {}
        self.barrier_idx = None

    def _dep(self, op, d):
        dop = self.ops[d]
        if dop.is_dma:
            k = dop.semkey
            v = self.dcnt[k]
            if v > op.dw.get(k, 0):
                op.dw[k] = v
        else:
            if dop.eng == op.eng and not op.is_dma:
                if dop.eng == "pe":
                    return
                if not SAME_ENGINE_SYNC and not (dop.small or op.small):
                    return
            if d > op.cw.get(dop.eng, -1):
                op.cw[dop.eng] = d

    def add(self, eng, fn, R=(), W=(), dma=False, semkey=None, small=None):
        i = len(self.ops)
        op = Op(eng, fn, dma, semkey)
        op.small = self.small_on if small is None else small
        self.ops.append(op)
        for b in R:
            if b.lw is not None:
                self._dep(op, b.lw)
        for b in W:
            if b.lw is not None:
                self._dep(op, b.lw)
            for r in b.rd.values():
                self._dep(op, r)
        if self.barrier_idx is not None:
            self._dep(op, self.barrier_idx)
        rk = ("d", semkey) if dma else eng
        for b in R:
            b.rd[rk] = i
        for b in W:
            b.lw = i
            b.rd = {}
        if dma:
            self.dcnt[semkey] = self.dcnt.get(semkey, 0) + 16
            op.semval = self.dcnt[semkey]
            self.last[("d", semkey)] = i
        else:
            self.last[eng] = i
        return i

    def pe(self, fn, R=(), W=(), small=None):
        return self.add("pe", fn, R, W, small=small)

    def act(self, fn, R=(), W=(), small=None):
        return self.add("act", fn, R, W, small=small)

    def dve(self, fn, R=(), W=(), small=None):
        return self.add("dve", fn, R, W, small=small)

    def pool(self, fn, R=(), W=()):
        return self.add("pool", fn, R, W)

    def dma(self, q, fn, R=(), W=(), semkey=None):
        return self.add(q, fn, R, W, dma=True, semkey=semkey)

    def barrier(self, fn):
        i = len(self.ops)
        op = Op("dve", fn, False, None)
        op.small = True
        self.ops.append(op)
        for k, d in self.last.items():
            dop = self.ops[d]
            if dop.is_dma:
                op.dw[dop.semkey] = self.dcnt[dop.semkey]
            elif d > op.cw.get(dop.eng, -1):
                op.cw[dop.eng] = d
        self.last["dve"] = i
        self.barrier_idx = i
        return i

    def emit(self, nc, final_wait_eng="sp"):
        ops = self.ops
        for op in ops:
            for e, d in op.cw.items():
                ops[d].need_inc = True
        cnt = {e: 0 for e in ENGS}
        for op in ops:
            if not op.is_dma and op.need_inc:
                cnt[op.eng] += 1
                op.ticket = cnt[op.eng]
        dcnt = self.dcnt
        self.stats = dict(cnt=dict(cnt), n_ops=len(ops), n_dma_sems=len(dcnt))
        per_eng = {e: [] for e in ENGS}
        for op in ops:
            per_eng[op.eng].append(op)
        with contextlib.ExitStack() as es:
            esem = {e: es.enter_context(nc.semaphore("s_" + e)) for e in ENGS}
            dsem = {k: es.enter_context(nc.semaphore("d_%s" % (k,))) for k in dcnt}
            block = es.enter_context(nc.Block())

            def make(e):
                def body(eng):
                    known = {f: 0 for f in ENGS}
                    kd = {}
                    for op in per_eng[e]:
                        for f, d in op.cw.items():
                            t = ops[d].ticket
                            if t > known[f]:
                                eng.wait_ge(esem[f], t)
                                known[f] = t
                        for k, v in op.dw.items():
                            if v > kd.get(k, 0):
                                eng.wait_ge(dsem[k], v)
                                kd[k] = v
                        ins = op.fn(eng)
                        if op.is_dma:
                            ins.then_inc(dsem[op.semkey], 16)
                        elif op.need_inc:
                            ins.then_inc(esem[e], 1)
                    if e == final_wait_eng:
                        for f in ENGS:
                            if f != e and cnt[f] > 0:
                                eng.wait_ge(esem[f], cnt[f])
                        for k, v in dcnt.items():
                            eng.wait_ge(dsem[k], v)
                return body

            block.tensor(make("pe"))
            block.scalar(make("act"))
            block.vector(make("dve"))
            block.gpsimd(make("pool"))
            block.sync(make("sp"))


def run_window(gens, width=2):
    it = iter(gens)
    active = []
    done = False
    while True:
        while len(active) < width and not done:
            try:
                active.append(next(it))
            except StopIteration:
                done = True
        if not active:
            break
        for g_ in list(active):
            try:
                next(g_)
            except StopIteration:
                active.remove(g_)


class Arena:
    def __init__(self, t, nwords):
        self.t = t
        self.n = nwords
        self.off = 0
        self.mark = 0

    def alloc(self, shape, dtype):
        n = 1
        for s in shape:
            n *= s
        if dtype == BF16:
            w = (n + 1) // 2
        else:
            w = n
        assert self.off + w <= self.n, ("SBUF arena overflow", self.off, w, self.n)
        v = self.t[:, self.off:self.off + w]
        self.off += w
        if dtype == BF16:
            v = v.bitcast(BF16)[:, :n]
        if len(shape) == 1:
            return v
        if len(shape) == 2:
            return v.rearrange("p (a b) -> p a b", a=shape[0])
        if len(shape) == 3:
            return v.rearrange("p (a b c) -> p a b c", a=shape[0], b=shape[1])
        raise ValueError(shape)

    def set_mark(self):
        self.mark = self.off

    def reset(self):
        self.off = self.mark


class K:
    pass


def build_program(stages=("mods", "l0f1", "l0mix", "l0f2", "l1f1", "l1mix", "l1f2"), dbg=None):
    nc = bass.Bass("TRN2", target_bir_lowering=False)
    k = K()
    k.nc = nc
    k.P = Prog()
    P = k.P

    def din(name, shape):
        return nc.dram_tensor(name, list(shape), F32, kind="ExternalInput").ap()

    k.xT = din("xT", [D, LLAT])
    k.ctxT = din("ctxT", [D, LCTX])
    k.cond = din("cond", [128, KT, 2])
    k.ada_w = [din("ada_w%d" % l, [D, 9 * D]) if "mods" in stages else None for l in range(DEPTH)]
    k.ada_b = din("ada_b", [128, DEPTH, 72])
    k.ln_g = din("ln_g", [128, DEPTH * 3, KT])
    k.ln_b = din("ln_b", [128, DEPTH * 3, KT])
    k.w_gu = [[din("w_gu%d%d" % (l, s), [D, 2 * DFF]) if ("l%df%d" % (l, s + 1)) in stages else None
               for s in range(2)] for l in range(DEPTH)]
    k.w_dn = [[din("w_dn%d%d" % (l, s), [DFF, D]) if ("l%df%d" % (l, s + 1)) in stages else None
               for s in range(2)] for l in range(DEPTH)]
    if "l1mix" in stages:
        k.attn_w_in = din("attn_w_in", [D, 1536])
        k.attn_w_out = din("attn_w_out", [D, D])
        k.rotm = din("rotm", [128, 128])
        k.qk_gain = din("qk_gain", [128, 2])
        k.rope = din("rope", [128, 2, LLAT])
    if "l0mix" in stages:
        k.even_w_in = din("even_w_in", [D, 3088])
        k.even_w_out = din("even_w_out", [D, D])
        k.gdnc = din("gdnc", [128, 8, 128])
        k.cw5 = din("cw5", [128, 12, 5])
        k.cw31 = din("cw31", [128, 4, 31])
        k.cvec = din("cvec", [128, 4, 3])
        k.rowc = din("rowc", [128, 2, 8])
        k.gnorm = din("gnorm", [128, 512])
        k.CFd = nc.dram_tensor("CFd", [512, T], BF16, kind="Internal").ap()
        k.ZSd = nc.dram_tensor("ZSd", [T, 512], BF16, kind="Internal").ap()
        k.Od = [nc.dram_tensor("Od%d" % i, [T, 512], F32, kind="Internal").ap() for i in range(2)]
    k.out = nc.dram_tensor("yT", [D, LLAT], F32, kind="ExternalOutput").ap()
    k.S = [nc.dram_tensor("stream%d" % i, [D, T], F32, kind=("ExternalOutput" if dbg else "Internal")).ap() for i in range(2)]

    with contextlib.ExitStack() as es:
        arena_t = es.enter_context(nc.sbuf_tensor("arena", [128, 53200], F32))
        k.A = Arena(arena_t, 53200)
        k.ps = [es.enter_context(nc.psum_tensor("ps%d" % i, [128, 512], F32)) for i in range(8)]
        k.psb = [Buf("ps%d" % i) for i in range(8)]
        setup_consts(k)
        if "mods" in stages:
            emit_mods(k)
        if dbg == "mods":
            dm = nc.dram_tensor("dbg_mods", [128, DEPTH * 72 * 2], F32, kind="ExternalOutput").ap()
            P.dma("sp", lambda e: e.dma_start(out=dm, in_=k.mods.rearrange("p l n c -> p (l n c)")), R=[k.b_mods], semkey="dbg")
        def in_tile(i):
            if i == 0:
                return k.ctxT.rearrange("(kt p) t -> p kt t", p=128)
            return k.xT[:, (i - 1) * TN:i * TN].rearrange("(kt p) t -> p kt t", p=128)

        def s_tile(s):
            return lambda i: k.S[s][:, i * TN:(i + 1) * TN].rearrange("(kt p) t -> p kt t", p=128)

        def out_tile(i):
            return k.out[:, (i - 1) * TN:i * TN].rearrange("(kt p) t -> p kt t", p=128)

        k.sbuf_S = [[Buf("S%d_%d" % (s_, i)) for i in range(NTILE)] for s_ in range(2)]
        k.b_in = [Buf("in%d" % i) for i in range(NTILE)]
        k.b_out = [Buf("out%d" % i) for i in range(NTILE)]
        def in_cols(g0, g1):
            if g1 <= LCTX:
                return k.ctxT[:, g0:g1].rearrange("(kt p) t -> p kt t", p=128)
            assert g0 >= LCTX
            return k.xT[:, g0 - LCTX:g1 - LCTX].rearrange("(kt p) t -> p kt t", p=128)
        cur = (in_tile, k.b_in, in_cols)
        nxt = [0]

        def scratch():
            j = nxt[0]
            nxt[0] = 1 - j
            return (s_tile(j), k.sbuf_S[j],
                    lambda g0, g1, j=j: k.S[j][:, g0:g1].rearrange("(kt p) t -> p kt t", p=128))
        for st in ("l0f1", "l0mix", "l0f2", "l1f1", "l1mix", "l1f2"):
            if st not in stages:
                continue
            l = int(st[1])
            if st == "l1f2":
                dstp = (out_tile, k.b_out, None)
                emit_ffn(k, 1, 1, cur[0], cur[1], dstp[0], dstp[1], range(1, NTILE))
            elif st.endswith("f1") or st.endswith("f2"):
                dstp = scratch()
                tl = range(NTILE) if st != "l1f2" else range(1, NTILE)
                emit_ffn(k, l, 0 if st.endswith("f1") else 1, cur[0], cur[1], dstp[0], dstp[1], tl)
            elif st == "l1mix":
                dstp = scratch()
                emit_attn(k, 1, cur[0], cur[1], dstp[0], dstp[1])
            elif st == "l0mix":
                dstp = scratch()
                emit_even(k, 0, cur[2], cur[1], cur[0], dstp[0], dstp[1])
            cur = dstp
        P.emit(nc)
    k.stats = P.stats
    return nc, k


def setup_consts(k):
    nc, P, A = k.nc, k.P, k.A
    P.small_on = True
    k.ones_bf = A.alloc([128], BF16)
    k.b_ones = Buf("ones")
    P.dve(lambda e: e.memset(k.ones_bf, 1.0), W=[k.b_ones])
    k.scr = A.alloc([8], F32)
    k.mods = A.alloc([DEPTH, 72, 2], F32)
    k.b_mods = Buf("mods")
    k.mod1p = A.alloc([DEPTH, 72, 2], F32)
    k.lng = A.alloc([DEPTH * 3, KT], F32)
    k.lnb = A.alloc([DEPTH * 3, KT], F32)
    k.b_ln = Buf("ln")
    P.dma("sp", lambda e: e.dma_start(out=k.lng, in_=k.ln_g),
          W=[k.b_ln], semkey="c0")
    P.dma("sp", lambda e: e.dma_start(out=k.lnb, in_=k.ln_b),
          W=[k.b_ln], semkey="c0")
    A.set_mark()


def emit_mods(k):
    nc, P, A = k.nc, k.P, k.A
    A.reset()
    P.small_on = True
    cs = A.alloc([KT, 2], F32)
    sc = A.alloc([KT, 2], F32)
    bsb = A.alloc([DEPTH, 72], F32)
    NCH = 1024
    wbuf = [A.alloc([KT, NCH], F32) for _ in range(2)]
    b_c, b_sc, b_b = Buf(), Buf(), Buf()
    b_w = [Buf(), Buf()]
    P.dma("sp", lambda e: e.dma_start(out=cs, in_=k.cond), W=[b_c], semkey="c1")
    P.dma("sp", lambda e: e.dma_start(out=bsb, in_=k.ada_b), W=[b_b], semkey="c2")
    P.act(lambda e: e.activation(out=sc, in_=cs, func=AF.Exp, scale=-1.0), R=[b_c], W=[b_sc])
    P.act(lambda e: e.activation(out=sc, in_=sc, func=AF.Ln, bias=1.0), R=[b_sc], W=[b_sc])
    P.act(lambda e: e.activation(out=sc, in_=sc, func=AF.Exp, scale=-1.0), R=[b_sc], W=[b_sc])
    P.dve(lambda e: e.tensor_tensor(out=sc, in0=sc, in1=cs, op=ALU.mult), R=[b_sc, b_c], W=[b_sc])
    pst = k.ps[0]
    for l in range(DEPTH):
        for ch in range(9):
            wb = wbuf[(l * 9 + ch) % 2]
            bw = b_w[(l * 9 + ch) % 2]
            for kt in range(KT):
                P.dma("sp" if (kt % 2 == 0 or os.environ.get("NOACTQ")) else "act",
                      lambda e, l=l, ch=ch, kt=kt, wb=wb: e.dma_start(
                          out=wb[:, kt, :], in_=k.ada_w[l][kt * 128:(kt + 1) * 128, ch * NCH:(ch + 1) * NCH]),
                      W=[bw], semkey="mw%d" % ((l * 9 + ch) % 2))
            for nt in range(8):
                n = ch * 8 + nt
                for kt in range(KT):
                    P.pe(lambda e, wb=wb, kt=kt, nt=nt, n=n: e.matmul(
                        pst[:, 2 * n:2 * n + 2], lhsT=wb[:, kt, nt * 128:(nt + 1) * 128], rhs=sc[:, kt, :],
                        start=(kt == 0), stop=(kt == KT - 1)), R=[bw, b_sc], W=[k.psb[0]])
        P.dve(lambda e, l=l: e.tensor_tensor(
            out=k.mods[:, l, :, :], in0=pst[:, 0:144].rearrange("p (n c) -> p n c", c=2),
            in1=bsb[:, l, :].unsqueeze(2).to_broadcast([128, 72, 2]), op=ALU.add),
            R=[k.psb[0], b_b], W=[k.b_mods])
    P.dve(lambda e: e.tensor_scalar_add(out=k.mod1p, in0=k.mods, scalar1=1.0), R=[k.b_mods], W=[k.b_mods])
    for j in (2, 8):
        P.dve(lambda e, j=j: e.tensor_scalar_mul(out=k.mod1p[:, :, j * 8:(j + 1) * 8, :],
                                                 in0=k.mods[:, :, j * 8:(j + 1) * 8, :], scalar1=0.5),
              R=[k.b_mods], W=[k.b_mods])
    P.small_on = False


def emit_ffn(k, l, s, src, src_bufs, dst, dst_bufs, tiles, final_out=None):
    nc, P, A = k.nc, k.P, k.A
    A.reset()
    P.small_on = False
    P.barrier(lambda e: e.memset(k.scr, 0.0))
    j0 = 0 if s == 0 else 6
    lnrow = l * 3 + (0 if s == 0 else 2)
    NG = 2
    wgu = A.alloc([KT, 2 * DFF], BF16)
    wdn = A.alloc([FT, D], BF16)
    b_wgu = [[[Buf() for _ in range(KT)] for _ in range(NG)] for _ in range(2)]
    b_wdn = [Buf() for _ in range(FT)]
    for g in range(NG):
        for gu in range(2):
            for kt in range(KT):
                c0 = gu * DFF + g * 11 * 128
                P.dma("pool", lambda e, kt=kt, c0=c0: e.dma_start(
                    out=wgu[:, kt, c0:c0 + 11 * 128], in_=k.w_gu[l][s][kt * 128:(kt + 1) * 128, c0:c0 + 11 * 128]),
                    W=[b_wgu[gu][g][kt]], semkey="wgu%d" % (gu * NG + g))
    for f in range(FT):
        P.dma("pool", lambda e, f=f: e.dma_start(out=wdn[:, f, :], in_=k.w_dn[l][s][f * 128:(f + 1) * 128, :]),
              W=[b_wdn[f]], semkey="wdn%d" % (f // 11))
    xs = [A.alloc([KT, TN], F32) for _ in range(2)]
    xm = [A.alloc([KT, TN], BF16) for _ in range(2)]
    h = A.alloc([FT, TN], BF16)
    rr = [A.alloc([KT, TN], F32) for _ in range(2)]
    tmp = [A.alloc([TN], F32) for _ in range(2)]
    sg = [A.alloc([TN], F32) for _ in range(4)]
    rb = [A.alloc([TN], BF16) for _ in range(2)]
    rq = [A.alloc([TN], BF16) for _ in range(2)]
    st_m = A.alloc([TN], F32)
    st_r = A.alloc([TN], F32)
    st_v = A.alloc([TN], F32)
    b_xs = [Buf(), Buf()]
    b_xm = [Buf(), Buf()]
    b_h = [Buf() for _ in range(FT)]
    b_rr = [[Buf() for _ in range(KT)] for _ in range(2)]
    b_rl = [Buf(), Buf()]
    b_tmp = [Buf(), Buf()]
    b_sg = [Buf() for _ in range(4)]
    b_rb = [Buf(), Buf()]
    b_rq = [Buf(), Buf()]
    b_st = Buf()
    pgu, pd, pS1, pS2 = k.ps[0:4], k.ps[4:6], k.ps[6], k.ps[7]
    bpgu, bpd, bS1, bS2 = k.psb[0:4], k.psb[4:6], k.psb[6], k.psb[7]
    pg = [p[:, 0:TN] for p in pgu]
    pu = [p[:, TN:2 * TN] for p in pgu]
    bpg = bpgu
    bpu = bpgu
    tiles = list(tiles)
    cnt_f = [0]
    cnt_d = [0]

    def load_x(idx):
        i = tiles[idx]
        bi = idx % 2
        P.dma("sp", lambda e: e.dma_start(out=xs[bi], in_=src(i)), R=[src_bufs[i]], W=[b_xs[bi]], semkey="xs%d" % bi)

    def load_r(idx):
        i = tiles[idx]
        bi = idx % 2
        P.dma("sp", lambda e: e.dma_start(out=rr[bi], in_=src(i)), R=[src_bufs[i]], W=b_rr[bi] + [b_rl[bi]],
              semkey="rr%d" % bi)

    def modulate(idx):
        i = tiles[idx]
        bi = idx % 2
        c = 1 if i == 0 else 0
        for kt in range(KT):
            P.act(lambda e, kt=kt: e.activation(
                out=xm[bi][:, kt, :], in_=xs[bi][:, kt, :], func=AF.Identity,
                scale=k.mod1p[:, l, (j0 + 1) * 8 + kt, c:c + 1], bias=k.mods[:, l, j0 * 8 + kt, c:c + 1]),
                R=[b_xs[bi], k.b_mods], W=[b_xm[bi]])

    def tail_steps(idx):
        i = tiles[idx]
        bi = idx % 2
        steps = []

        def stat_math():
            P.dve(lambda e: e.tensor_scalar_mul(out=st_m, in0=pS1[:, :TN], scalar1=1.0 / D), R=[bS1], W=[b_st])
            P.dve(lambda e: e.tensor_tensor(out=st_v, in0=st_m, in1=st_m, op=ALU.mult), R=[b_st], W=[b_st])
            P.dve(lambda e: e.scalar_tensor_tensor(out=st_v, in0=pS2[:, :TN], scalar=1.0 / D, in1=st_v,
                                                   op0=ALU.mult, op1=ALU.subtract), R=[bS2, b_st], W=[b_st])
            P.act(lambda e: e.activation(out=st_r, in_=st_v, func=AF.Ln, bias=LN_EPS), R=[b_st], W=[b_st])
            P.act(lambda e: e.activation(out=st_r, in_=st_r, func=AF.Exp, scale=-0.5), R=[b_st], W=[b_st])
        steps.append(stat_math)
        for d in range(KT):
            def norm(d=d):
                P.dve(lambda e: e.tensor_tensor(out=rr[bi][:, d, :], in0=rr[bi][:, d, :], in1=st_m, op=ALU.subtract),
                      R=[b_st, b_rr[bi][d]], W=[b_rr[bi][d]])
                P.dve(lambda e: e.tensor_tensor(out=rr[bi][:, d, :], in0=rr[bi][:, d, :], in1=st_r, op=ALU.mult),
                      R=[b_st, b_rr[bi][d]], W=[b_rr[bi][d]])
            steps.append(norm)

            def affine(d=d):
                P.act(lambda e: e.activation(out=rr[bi][:, d, :], in_=rr[bi][:, d, :], func=AF.Identity,
                                             scale=k.lng[:, lnrow, d:d + 1], bias=k.lnb[:, lnrow, d:d + 1]),
                      R=[b_rr[bi][d], k.b_ln], W=[b_rr[bi][d]])
            steps.append(affine)

        def store():
            if final_out is not None and i >= 1:
                P.dma("sp", lambda e: e.dma_start(out=final_out[0](i), in_=rr[bi]), R=b_rr[bi], W=[final_out[1][i]],
                      semkey="st%d" % bi)
            else:
                P.dma("sp", lambda e: e.dma_start(out=dst(i), in_=rr[bi]), R=b_rr[bi], W=[dst_bufs[i]],
                      semkey="st%d" % bi)
        steps.append(store)
        return steps

    def tile_body(idx, deferred):
        i = tiles[idx]
        bi = idx % 2
        c = 1 if i == 0 else 0
        if idx + 1 < len(tiles):
            load_x(idx + 1)
        load_r(idx)
        if idx == 0:
            modulate(0)
        for f in range(FT):
            fb = cnt_f[0] % 4
            cnt_f[0] += 1
            g = f // 11
            for kt in range(KT):
                P.pe(lambda e, kt=kt, f=f, fb=fb: e.matmul(
                    pg[fb], lhsT=wgu[:, kt, f * 128:(f + 1) * 128], rhs=xm[bi][:, kt, :],
                    start=(kt == 0), stop=(kt == KT - 1)), R=[b_wgu[0][g][kt], b_xm[bi]], W=[bpg[fb]])
            for kt in range(KT):
                P.pe(lambda e, kt=kt, f=f, fb=fb: e.matmul(
                    pu[fb], lhsT=wgu[:, kt, DFF + f * 128:DFF + (f + 1) * 128], rhs=xm[bi][:, kt, :],
                    start=(kt == 0), stop=(kt == KT - 1)), R=[b_wgu[1][g][kt], b_xm[bi]], W=[bpu[fb]])
            if deferred and f >= 1:
                deferred.pop(0)()
            P.act(lambda e, fb=fb: e.activation(out=sg[fb], in_=pg[fb], func=AF.Exp, scale=-1.0),
                  R=[bpg[fb]], W=[b_sg[fb]])
            P.act(lambda e, fb=fb: e.activation(out=sg[fb], in_=sg[fb], func=AF.Ln, bias=1.0),
                  R=[b_sg[fb]], W=[b_sg[fb]])
            P.act(lambda e, fb=fb: e.activation(out=sg[fb], in_=sg[fb], func=AF.Exp, scale=-1.0),
                  R=[b_sg[fb]], W=[b_sg[fb]])
            P.dve(lambda e, fb=fb: e.tensor_tensor(out=sg[fb], in0=pg[fb], in1=sg[fb], op=ALU.mult),
                  R=[bpg[fb], b_sg[fb]], W=[b_sg[fb]])
            P.dve(lambda e, fb=fb, f=f: e.tensor_tensor(out=h[:, f, :], in0=pu[fb], in1=sg[fb], op=ALU.mult),
                  R=[bpu[fb], b_sg[fb]], W=[b_h[f]])
        while deferred:
            deferred.pop(0)()
        if idx + 1 < len(tiles):
            modulate(idx + 1)
        pend = None
        for d in range(KT):
            db = cnt_d[0] % 2
            cnt_d[0] += 1
            for f in range(FT):
                P.pe(lambda e, f=f, d=d, db=db: e.matmul(
                    pd[db][:, :TN], lhsT=wdn[:, f, d * 128:(d + 1) * 128], rhs=h[:, f, :],
                    start=(f == 0), stop=(f == FT - 1)), R=[b_wdn[f], b_h[f]], W=[bpd[db]])
            if pend is not None:
                pend()

            def epi(d=d, db=db):
                tb = d % 2
                P.act(lambda e: e.activation(out=tmp[tb], in_=pd[db][:, :TN], func=AF.Copy,
                                             scale=k.mod1p[:, l, (j0 + 2) * 8 + d, c:c + 1]),
                      R=[bpd[db], k.b_mods], W=[b_tmp[tb]])
                P.dve(lambda e: e.scalar_tensor_tensor(out=rr[bi][:, d, :], in0=rr[bi][:, d, :], scalar=ALPHA,
                                                       in1=tmp[tb], op0=ALU.mult, op1=ALU.add),
                      R=[b_tmp[tb], b_rr[bi][d]], W=[b_rr[bi][d]])
                P.act(lambda e: e.activation(out=rq[tb], in_=rr[bi][:, d, :], func=AF.Square),
                      R=[b_rr[bi][d]], W=[b_rq[tb]])
                P.dve(lambda e: e.tensor_copy(out=rb[tb], in_=rr[bi][:, d, :]), R=[b_rr[bi][d]], W=[b_rb[tb]])
            epi()

            def stats(d=d):
                tb = d % 2
                P.pe(lambda e: e.matmul(pS1[:, :TN], lhsT=k.ones_bf, rhs=rb[tb],
                                        start=(d == 0), stop=(d == KT - 1)), R=[b_rb[tb], k.b_ones], W=[bS1])
                P.pe(lambda e: e.matmul(pS2[:, :TN], lhsT=k.ones_bf, rhs=rq[tb],
                                        start=(d == 0), stop=(d == KT - 1)), R=[b_rq[tb], k.b_ones], W=[bS2])
            pend = stats
        return [pend] + tail_steps(idx)

    load_x(0)
    deferred = []
    for idx in range(len(tiles)):
        deferred = tile_body(idx, deferred)
    while deferred:
        deferred.pop(0)()


def emit_proj_ln(k, l, lnrow, gate_j, NK, wout, b_wout, rhs_fn, rhs_bufs_fn, src, src_bufs, dst, dst_bufs, tiles,
                 pre_tile=None):
    nc, P, A = k.nc, k.P, k.A
    rr = [A.alloc([KT, TN], F32) for _ in range(2)]
    tmp = [A.alloc([TN], F32) for _ in range(2)]
    rb = [A.alloc([TN], BF16) for _ in range(2)]
    rq = [A.alloc([TN], BF16) for _ in range(2)]
    st_m, st_r, st_v = A.alloc([TN], F32), A.alloc([TN], F32), A.alloc([TN], F32)
    b_rr = [[Buf() for _ in range(KT)] for _ in range(2)]
    b_tmp, b_rb, b_rq = [Buf(), Buf()], [Buf(), Buf()], [Buf(), Buf()]
    b_st = Buf()
    pd, pS1, pS2 = k.ps[4:6], k.ps[6], k.ps[7]
    bpd, bS1, bS2 = k.psb[4:6], k.psb[6], k.psb[7]
    cnt_d = [0]

    def tile(idx):
        i = tiles[idx]
        bi = idx % 2
        c = 1 if i == 0 else 0
        if pre_tile is not None:
            pre_tile(i, idx)
        P.dma("sp", lambda e: e.dma_start(out=rr[bi], in_=src(i)), R=[src_bufs[i]], W=b_rr[bi], semkey="prr%d" % bi)
        pend = []
        for d in range(KT):
            db = cnt_d[0] % 2
            cnt_d[0] += 1
            tb = d % 2
            for f in range(NK):
                P.pe(lambda e, f=f, d=d, db=db: e.matmul(
                    pd[db][:, :TN], lhsT=wout[:, f, d * 128:(d + 1) * 128], rhs=rhs_fn(i, f, idx),
                    start=(f == 0), stop=(f == NK - 1)), R=[b_wout] + rhs_bufs_fn(i, f, idx), W=[bpd[db]])
            while pend:
                pend.pop(0)()
            P.act(lambda e, d=d, db=db, tb=tb: e.activation(out=tmp[tb], in_=pd[db][:, :TN], func=AF.Copy,
                                                          scale=k.mods[:, l, gate_j * 8 + d, c:c + 1]),
                  R=[bpd[db], k.b_mods], W=[b_tmp[tb]])
            P.dve(lambda e, d=d, tb=tb: e.scalar_tensor_tensor(out=rr[bi][:, d, :], in0=rr[bi][:, d, :], scalar=ALPHA,
                                                             in1=tmp[tb], op0=ALU.mult, op1=ALU.add),
                  R=[b_tmp[tb], b_rr[bi][d]], W=[b_rr[bi][d]])
            P.act(lambda e, d=d, tb=tb: e.activation(out=rq[tb], in_=rr[bi][:, d, :], func=AF.Square),
                  R=[b_rr[bi][d]], W=[b_rq[tb]])
            P.dve(lambda e, d=d, tb=tb: e.tensor_copy(out=rb[tb], in_=rr[bi][:, d, :]), R=[b_rr[bi][d]], W=[b_rb[tb]])

            def stats(d=d, tb=tb):
                P.pe(lambda e: e.matmul(pS1[:, :TN], lhsT=k.ones_bf, rhs=rb[tb],
                                        start=(d == 0), stop=(d == KT - 1)), R=[b_rb[tb], k.b_ones], W=[bS1])
                P.pe(lambda e: e.matmul(pS2[:, :TN], lhsT=k.ones_bf, rhs=rq[tb],
                                        start=(d == 0), stop=(d == KT - 1)), R=[b_rq[tb], k.b_ones], W=[bS2])
            pend.append(stats)
        while pend:
            pend.pop(0)()
        P.dve(lambda e: e.tensor_scalar_mul(out=st_m, in0=pS1[:, :TN], scalar1=1.0 / D), R=[bS1], W=[b_st])
        P.dve(lambda e: e.tensor_tensor(out=st_v, in0=st_m, in1=st_m, op=ALU.mult), R=[b_st], W=[b_st])
        P.dve(lambda e: e.scalar_tensor_tensor(out=st_v, in0=pS2[:, :TN], scalar=1.0 / D, in1=st_v,
                                               op0=ALU.mult, op1=ALU.subtract), R=[bS2, b_st], W=[b_st])
        P.act(lambda e: e.activation(out=st_r, in_=st_v, func=AF.Ln, bias=LN_EPS), R=[b_st], W=[b_st])
        P.act(lambda e: e.activation(out=st_r, in_=st_r, func=AF.Exp, scale=-0.5), R=[b_st], W=[b_st])
        for d in range(KT):
            P.dve(lambda e, d=d: e.tensor_tensor(out=rr[bi][:, d, :], in0=rr[bi][:, d, :], in1=st_m, op=ALU.subtract),
                  R=[b_st, b_rr[bi][d]], W=[b_rr[bi][d]])
            P.dve(lambda e, d=d: e.tensor_tensor(out=rr[bi][:, d, :], in0=rr[bi][:, d, :], in1=st_r, op=ALU.mult),
                  R=[b_st, b_rr[bi][d]], W=[b_rr[bi][d]])
            P.act(lambda e, d=d: e.activation(out=rr[bi][:, d, :], in_=rr[bi][:, d, :], func=AF.Identity,
                                             scale=k.lng[:, lnrow, d:d + 1], bias=k.lnb[:, lnrow, d:d + 1]),
                  R=[b_rr[bi][d], k.b_ln], W=[b_rr[bi][d]])
        P.dma("sp", lambda e: e.dma_start(out=dst(i), in_=rr[bi]), R=b_rr[bi], W=[dst_bufs[i]], semkey="pst%d" % bi)

    for idx in range(len(tiles)):
        tile(idx)


def emit_attn(k, l, src, src_bufs, dst, dst_bufs):
    nc, P, A = k.nc, k.P, k.A
    A.reset()
    P.small_on = False
    P.barrier(lambda e: e.memset(k.scr, 0.0))
    NQ = LLAT // 512
    win = A.alloc([KT, 1536], BF16)
    wout = A.alloc([KT, D], BF16)
    qT = A.alloc([8, LLAT], BF16)
    kT = A.alloc([2, T], BF16)
    V = A.alloc([T // 128, 256], BF16)
    Rm = A.alloc([128], BF16)
    gq = A.alloc([2], F32)
    b_win, b_wout, b_c = Buf(), Buf(), Buf()
    for kt in range(KT):
        P.dma("pool", lambda e, kt=kt: e.dma_start(out=win[:, kt, :], in_=k.attn_w_in[kt * 128:(kt + 1) * 128, :]),
              W=[b_win], semkey="awin") if kt == 0 else P.dma(
            "pool", lambda e, kt=kt: e.dma_start(out=win[:, kt, :], in_=k.attn_w_in[kt * 128:(kt + 1) * 128, :]),
            R=[], W=[Buf()], semkey="awin")
    for kt in range(KT):
        P.dma("pool", lambda e, kt=kt: e.dma_start(out=wout[:, kt, :], in_=k.attn_w_out[kt * 128:(kt + 1) * 128, :]),
              W=[b_wout] if kt == 0 else [Buf()], semkey="awout")
    P.dma("pool", lambda e: e.dma_start(out=Rm, in_=k.rotm), W=[b_c], semkey="ac")
    P.dma("sp", lambda e: e.dma_start(out=gq, in_=k.qk_gain), W=[b_c], semkey="ac")
    b_wl = Buf()
    P.act(lambda e: e.activation(out=k.scr[:, 0:1], in_=k.scr[:, 1:2], func=AF.Copy), R=[b_win, b_wout, b_c], W=[b_wl])
    mark_a = A.off
    xs = [A.alloc([KT, TN], F32) for _ in range(2)]
    xm = [A.alloc([KT, TN], BF16) for _ in range(2)]
    cs = [A.alloc([2, TN], F32) for _ in range(2)]
    qf = [A.alloc([TN], F32) for _ in range(2)]
    sq = [A.alloc([TN], BF16) for _ in range(2)]
    rs = [A.alloc([TN], F32) for _ in range(2)]
    qn = [A.alloc([TN], F32) for _ in range(2)]
    qnb = [A.alloc([TN], BF16) for _ in range(2)]
    t1 = [A.alloc([TN], F32) for _ in range(2)]
    b_xs, b_xm, b_cs = [Buf(), Buf()], [Buf(), Buf()], [Buf(), Buf()]
    b_qf, b_sq, b_rs, b_qn, b_qnb, b_t1 = ([Buf(), Buf()] for _ in range(6))
    b_qT = [[Buf() for _ in range(NTILE)] for _ in range(8)]
    b_kT = [[Buf() for _ in range(NTILE)] for _ in range(2)]
    b_V = [Buf() for _ in range(NTILE)]
    pp, pm, pr, pv = k.ps[0:2], k.ps[2:4], k.ps[4:6], k.ps[6:8]
    bpp, bpm, bpr, bpv = k.psb[0:2], k.psb[2:4], k.psb[4:6], k.psb[6:8]
    cn = [0]
    for i in range(NTILE):
        bi = i % 2
        c = 1 if i == 0 else 0
        P.dma("sp", lambda e, i=i, bi=bi: e.dma_start(out=xs[bi], in_=src(i)), R=[src_bufs[i]], W=[b_xs[bi]],
              semkey="axs%d" % bi)
        if i >= 1:
            P.dma("sp", lambda e, i=i, bi=bi: e.dma_start(out=cs[bi], in_=k.rope[:, :, (i - 1) * TN:i * TN]),
                  W=[b_cs[bi]], semkey="acs%d" % bi)
        for kt in range(KT):
            P.act(lambda e, kt=kt, bi=bi, c=c: e.activation(
                out=xm[bi][:, kt, :], in_=xs[bi][:, kt, :], func=AF.Identity,
                scale=k.mod1p[:, l, 4 * 8 + kt, c:c + 1], bias=k.mods[:, l, 3 * 8 + kt, c:c + 1]),
                R=[b_xs[bi], k.b_mods], W=[b_xm[bi]])
        fts = ([] if i == 0 else list(range(8))) + [8, 9]

        def ft_gen(ft, i=i, bi=bi):
            j = cn[0] % 2
            cn[0] += 1
            isq = ft < 8
            for kt in range(KT):
                P.pe(lambda e, kt=kt: e.matmul(
                    pp[j][:, :TN], lhsT=win[:, kt, ft * 128:(ft + 1) * 128], rhs=xm[bi][:, kt, :],
                    start=(kt == 0), stop=(kt == KT - 1)), R=[b_wl, b_xm[bi]], W=[bpp[j]])
            yield
            P.act(lambda e: e.activation(out=qf[j], in_=pp[j][:, :TN], func=AF.Copy), R=[bpp[j]], W=[b_qf[j]])
            P.act(lambda e: e.activation(out=sq[j], in_=pp[j][:, :TN], func=AF.Square), R=[bpp[j]], W=[b_sq[j]])
            yield
            P.pe(lambda e: e.matmul(pm[j][:, :TN], lhsT=k.ones_bf, rhs=sq[j], start=True, stop=True),
                 R=[b_sq[j], k.b_ones], W=[bpm[j]])
            yield
            P.act(lambda e: e.activation(out=rs[j], in_=pm[j][:, :TN], func=AF.Ln, scale=1.0 / 128, bias=RMS_EPS),
                  R=[bpm[j]], W=[b_rs[j]])
            P.act(lambda e: e.activation(out=rs[j], in_=rs[j], func=AF.Exp, scale=-0.5), R=[b_rs[j]], W=[b_rs[j]])
            yield
            gcol = gq[:, 0:1] if isq else gq[:, 1:2]
            if i == 0:
                dstv = kT[:, ft - 8, 0:TN]
                P.dve(lambda e: e.scalar_tensor_tensor(
                    out=dstv, in0=qf[j], scalar=gcol, in1=rs[j], op0=ALU.mult, op1=ALU.mult),
                    R=[b_qf[j], b_rs[j], b_wl], W=[b_kT[ft - 8][i]])
                return
            P.dve(lambda e: e.scalar_tensor_tensor(
                out=qn[j], in0=qf[j], scalar=gcol, in1=rs[j], op0=ALU.mult, op1=ALU.mult),
                R=[b_qf[j], b_rs[j], b_wl], W=[b_qn[j]])
            yield
            P.act(lambda e: e.activation(out=qnb[j], in_=qn[j], func=AF.Copy), R=[b_qn[j]], W=[b_qnb[j]])
            yield
            P.pe(lambda e: e.matmul(pr[j][:, :TN], lhsT=Rm, rhs=qnb[j], start=True, stop=True),
                 R=[b_qnb[j], b_wl], W=[bpr[j]])
            yield
            P.dve(lambda e: e.tensor_tensor(out=t1[j], in0=qn[j], in1=cs[bi][:, 0, :], op=ALU.mult),
                  R=[b_qn[j], b_cs[bi]], W=[b_t1[j]])
            P.dve(lambda e: e.tensor_tensor(out=qn[j], in0=pr[j][:, :TN], in1=cs[bi][:, 1, :], op=ALU.mult),
                  R=[bpr[j], b_cs[bi]], W=[b_qn[j]])
            if isq:
                dstv, db = qT[:, ft, (i - 1) * TN:i * TN], b_qT[ft][i]
            else:
                dstv, db = kT[:, ft - 8, i * TN:(i + 1) * TN], b_kT[ft - 8][i]
            P.dve(lambda e: e.tensor_tensor(out=dstv, in0=t1[j], in1=qn[j], op=ALU.add),
                  R=[b_t1[j], b_qn[j]], W=[db])
        run_window((ft_gen(ft) for ft in fts), 2)
        for blk in range(2):
            j = cn[0] % 2
            cn[0] += 1
            for kt in range(KT):
                P.pe(lambda e, kt=kt, blk=blk, j=j, bi=bi: e.matmul(
                    pv[j][:, :256], lhsT=xm[bi][:, kt, blk * 128:(blk + 1) * 128], rhs=win[:, kt, 1280:1536],
                    start=(kt == 0), stop=(kt == KT - 1)), R=[b_wl, b_xm[bi]], W=[bpv[j]])
            P.act(lambda e, j=j, i=i, blk=blk: e.activation(out=V[:, i * 2 + blk, :], in_=pv[j][:, :256], func=AF.Copy),
                  R=[bpv[j]], W=[b_V[i]])
    A.off = mark_a
    P.barrier(lambda e: e.memset(k.scr, 0.0))
    pt = [A.alloc([512], BF16) for _ in range(3)]
    rec = [A.alloc([512], F32) for _ in range(2)]
    b_pt, b_rec = [Buf() for _ in range(3)], [Buf(), Buf()]
    pS, pO, pZ = k.ps[0:3], k.ps[3:5], k.ps[5:7]
    bpS, bpO, bpZ = k.psb[0:3], k.psb[3:5], k.psb[5:7]
    b_o = [[Buf() for _ in range(NQ)] for _ in range(8)]
    SCALE = 128 ** -0.5
    SHIFT = 8.0
    NKB = T // 128
    items = []
    it = 0
    for h in range(8):
        for qt in range(NQ):
            for kb in range(NKB):
                items.append((h, h // 4, qt, kb, it % 2))
            it += 1
    LOOK = 2

    def emit_S(idx):
        h, g, qt, kb, ob = items[idx]
        sb = idx % 3
        qbufs = [b_qT[h][1 + 2 * qt], b_qT[h][2 + 2 * qt]]
        P.pe(lambda e: e.matmul(
            pS[sb][:, :], lhsT=kT[:, g, kb * 128:(kb + 1) * 128], rhs=qT[:, h, qt * 512:(qt + 1) * 512],
            start=True, stop=True), R=[b_kT[g][kb // 2]] + qbufs, W=[bpS[sb]])
        P.act(lambda e: e.activation(out=pt[sb], in_=pS[sb][:, :], func=AF.Exp, scale=SCALE, bias=-SHIFT),
              R=[bpS[sb]], W=[b_pt[sb]])

    def emit_OZ(idx):
        h, g, qt, kb, ob = items[idx]
        sb = idx % 3
        qbufs = [b_qT[h][1 + 2 * qt], b_qT[h][2 + 2 * qt]]
        P.pe(lambda e: e.matmul(
            pO[ob][:, :], lhsT=V[:, kb, g * 128:(g + 1) * 128], rhs=pt[sb],
            start=(kb == 0), stop=(kb == NKB - 1)), R=[b_V[kb // 2], b_pt[sb]], W=[bpO[ob]])
        P.pe(lambda e: e.matmul(
            pZ[ob][:, :], lhsT=k.ones_bf, rhs=pt[sb],
            start=(kb == 0), stop=(kb == NKB - 1)), R=[b_pt[sb], k.b_ones], W=[bpZ[ob]])
        if kb == NKB - 1:
            P.dve(lambda e: e.reciprocal(out=rec[ob], in_=pZ[ob][:, :]), R=[bpZ[ob]], W=[b_rec[ob]])
            P.dve(lambda e: e.tensor_tensor(
                out=qT[:, h, qt * 512:(qt + 1) * 512], in0=pO[ob][:, :], in1=rec[ob], op=ALU.mult),
                R=[bpO[ob], b_rec[ob]], W=qbufs + [b_o[h][qt]])

    for idx in range(len(items) + LOOK):
        if idx < len(items):
            emit_S(idx)
        if idx - LOOK >= 0:
            emit_OZ(idx - LOOK)
    emit_proj_ln(k, l, l * 3 + 1, 5, 8, wout, b_wl,
                 lambda i, f, idx: qT[:, f, (i - 1) * TN:i * TN],
                 lambda i, f, idx: [b_o[f][(i - 1) // 2]],
                 src, src_bufs, dst, dst_bufs, list(range(1, NTILE)))


_CACHE = {}

ALL_STAGES = ("mods", "l0f1", "l0mix", "l0f2", "l1f1", "l1mix", "l1f2")


def prep_inputs(inputs, stages=ALL_STAGES, ncores=8):
    f32 = lambda n: np.asarray(inputs[n], dtype=np.float32)
    pl = np.ascontiguousarray
    x, ctx, c, c_ctx = f32("x"), f32("ctx"), f32("c"), f32("c_ctx")
    shared = {}
    shared["ada_b"] = pl(f32("ada_b").reshape(DEPTH, 72, 128).transpose(2, 0, 1))
    shared["ln_g"] = pl(f32("ln_g").reshape(DEPTH * 3, KT, 128).transpose(2, 0, 1))
    shared["ln_b"] = pl(f32("ln_b").reshape(DEPTH * 3, KT, 128).transpose(2, 0, 1))
    for l in range(DEPTH):
        if "mods" in stages:
            shared["ada_w%d" % l] = pl(f32("ada_w")[l])
        for s in range(2):
            if ("l%df%d" % (l, s + 1)) in stages:
                shared["w_gu%d%d" % (l, s)] = pl(f32("ffn_w_gu")[l, s])
                shared["w_dn%d%d" % (l, s)] = pl(f32("ffn_w_down")[l, s])
    if "l1mix" in stages:
        shared["attn_w_in"] = pl(f32("attn_w_in")[0])
        shared["attn_w_out"] = pl(f32("attn_w_out")[0])
        shared["qk_gain"] = pl(np.stack([f32("attn_q_norm")[0], f32("attn_k_norm")[0]], axis=1))
        rot = np.zeros((128, 128), np.float32)
        for i in range(64):
            rot[2 * i + 1, 2 * i] = -1.0
            rot[2 * i, 2 * i + 1] = 1.0
        shared["rotm"] = rot
        rows = LLAT // 64
        rowp = np.repeat(np.arange(rows, dtype=np.float32), 64)
        colp = np.tile(np.arange(64, dtype=np.float32), rows)
        inv = (np.float32(10000.0) ** (-np.arange(32, dtype=np.float32) / np.float32(32))).astype(np.float32)
        ang = np.concatenate([rowp[:, None] * inv, colp[:, None] * inv], axis=-1).astype(np.float32)
        ang2 = np.repeat(ang, 2, axis=1).T
        shared["rope"] = pl(np.stack([np.cos(ang2), np.sin(ang2)], axis=1).astype(np.float32))
    if "l0mix" in stages:
        shared["even_w_in"] = pl(f32("even_w_in")[0])
        shared["even_w_out"] = pl(f32("even_w_out")[0])
        jj, ii = np.meshgrid(np.arange(128), np.arange(128), indexing="ij")
        gd = np.zeros((8, 128, 128), np.float32)
        gd[0] = np.eye(128)
        gd[1] = 1.0
        gd[2] = (jj <= ii)
        gd[3] = (jj >= ii)
        gd[4] = np.where(ii >= jj, 0.0, -30000.0)
        gd[5] = np.where(ii <= jj, 0.0, -30000.0)
        gd[6] = (ii > jj)
        gd[7] = (ii < jj)
        shared["gdnc"] = pl(gd.transpose(1, 0, 2))
        shared["cw5"] = pl(f32("even_qkv_conv")[0].reshape(5, 12, 128).transpose(2, 1, 0))
        shared["cw31"] = pl(f32("cf_dw_conv")[0].reshape(31, 4, 128).transpose(2, 1, 0))
        shared["cvec"] = pl(np.stack([f32("cf_dw_bias")[0], f32("cf_ln_g")[0], f32("cf_ln_b")[0]], axis=0)
                            .reshape(3, 4, 128).transpose(2, 1, 0))
        rc = np.stack([f32("gdn_dt_bias")[0].reshape(8), f32("gdn_a_log")[0].reshape(8)], axis=0)
        shared["rowc"] = pl(np.broadcast_to(rc[None], (128, 2, 8)))
        shared["gnorm"] = pl(np.broadcast_to(np.tile(f32("gdn_out_norm")[0], 4)[None], (128, 512)))
    maps = []
    for b in range(ncores):
        m = dict(shared)
        m["xT"] = pl(x[b].T)
        m["ctxT"] = pl(ctx[b].T)
        m["cond"] = pl(np.stack([c[b], c_ctx], axis=1).reshape(KT, 128, 2).transpose(1, 0, 2))
        maps.append(m)
    return maps


def _silu_inplace(P, ap_f32, b, psum_src=None, bsrc=None):
    pass


def emit_even(k, l, src_cols, src_bufs, src, dst, dst_bufs):
    nc, P, A = k.nc, k.P, k.A
    A.reset()
    P.small_on = False
    P.barrier(lambda e: e.memset(k.scr, 0.0))
    NCH = T // 128
    HALO = 15
    NW = TN + 2 * HALO
    QN = A.alloc([4, T], BF16)
    KN = A.alloc([4, T], BF16)
    VV = A.alloc([4, T], BF16)
    GT = A.alloc([NCH, 16], F32)
    cst = A.alloc([9, 128], F32)
    identb = A.alloc([128], BF16)
    cw5 = A.alloc([12, 5], F32)
    cw31 = A.alloc([4, 31], F32)
    cvec = A.alloc([4, 3], F32)
    rowc = A.alloc([3, 8], F32)
    gnorm = A.alloc([512], F32)
    b_cst = Buf()
    P.dma("sp", lambda e: e.dma_start(out=cst[:, 0:8, :], in_=k.gdnc), W=[b_cst], semkey="ec")
    P.dma("sp", lambda e: e.dma_start(out=cw5, in_=k.cw5), W=[b_cst], semkey="ec")
    P.dma("sp", lambda e: e.dma_start(out=cw31, in_=k.cw31), W=[b_cst], semkey="ec")
    P.dma("sp", lambda e: e.dma_start(out=cvec, in_=k.cvec), W=[b_cst], semkey="ec")
    P.dma("sp", lambda e: e.dma_start(out=rowc[:, 0:2, :], in_=k.rowc), W=[b_cst], semkey="ec")
    P.dma("sp", lambda e: e.dma_start(out=gnorm, in_=k.gnorm), W=[b_cst], semkey="ec")
    P.small_on = True
    P.act(lambda e: e.activation(out=rowc[:, 2, :], in_=rowc[:, 1, :], func=AF.Exp), R=[b_cst], W=[b_cst])
    P.dve(lambda e: e.tensor_scalar_mul(out=rowc[:, 2, :], in0=rowc[:, 2, :], scalar1=-1.0), R=[b_cst], W=[b_cst])
    P.dve(lambda e: e.tensor_copy(out=identb, in_=cst[:, 0, :]), R=[b_cst], W=[b_cst])
    P.small_on = False
    ident, onesF, UT, LT = cst[:, 0, :], cst[:, 1, :], cst[:, 2, :], cst[:, 3, :]
    negm = [cst[:, 4, :], cst[:, 5, :]]
    smask = [cst[:, 6, :], cst[:, 7, :]]
    A.set_mark2 = A.off
    win = A.alloc([KT, 3088], BF16)
    b_win = Buf()
    for kt in range(KT):
        P.dma("pool", lambda e, kt=kt: e.dma_start(out=win[:, kt, :], in_=k.even_w_in[kt * 128:(kt + 1) * 128, :]),
              W=[Buf()], semkey="ewin")
    b_wl = Buf()
    P.act(lambda e: e.activation(out=k.scr[:, 0:1], in_=k.scr[:, 1:2], func=AF.Copy), R=[b_cst], W=[b_wl])
    k.P.ops[-1].dw["ewin"] = k.P.dcnt["ewin"]
    xs = [A.alloc([KT, NW], F32)] * 2
    xm = [A.alloc([KT, NW], BF16)] * 2
    acc = [A.alloc([TN], F32) for _ in range(2)]
    sg = [A.alloc([NW], F32) for _ in range(2)]
    sqb = [A.alloc([TN], BF16) for _ in range(2)]
    rs = [A.alloc([TN], F32) for _ in range(2)]
    uu = [A.alloc([NW], F32) for _ in range(2)]
    yy = A.alloc([4, TN], F32)
    yb = [A.alloc([TN], BF16) for _ in range(2)]
    yq = [A.alloc([TN], BF16) for _ in range(2)]
    st_m, st_r, st_v = A.alloc([TN], F32), A.alloc([TN], F32), A.alloc([TN], F32)
    cft = [A.alloc([4, TN], BF16)] * 2
    zs = [A.alloc([512], F32)] * 2
    zb = [A.alloc([512], BF16)] * 2
    gt1 = [A.alloc([16], F32) for _ in range(2)]
    b_xs, b_xm = [Buf()] * 2, [Buf()] * 2
    b_acc, b_sg, b_sqb, b_rs, b_uu = ([Buf(), Buf()] for _ in range(5))
    b_yy = [Buf() for _ in range(4)]
    b_yb, b_yq, b_gt1 = ([Buf(), Buf()] for _ in range(3))
    b_cft, b_zs, b_zb = [Buf()] * 2, [Buf()] * 2, [Buf()] * 2
    b_st = Buf()
    b_QN = [[Buf() for _ in range(NCH)] for _ in range(4)]
    b_KN = [[Buf() for _ in range(NCH)] for _ in range(4)]
    b_VV = [[Buf() for _ in range(NCH)] for _ in range(4)]
    b_GT = [Buf() for _ in range(NCH)]
    b_CFd = [Buf() for _ in range(NTILE)]
    b_ZSd = [Buf() for _ in range(NCH)]
    pp, pq, pm = k.ps[0:2], k.ps[2:4], k.ps[4:6]
    bpp, bpq, bpm = k.psb[0:2], k.psb[2:4], k.psb[4:6]
    pS1, pS2 = k.ps[6], k.ps[7]
    bS1, bS2 = k.psb[6], k.psb[7]
    cn = [0]

    def sigmoid_from(psrc, bsrc, dstap, bdst, n):
        P.act(lambda e: e.activation(out=dstap, in_=psrc, func=AF.Exp, scale=-1.0), R=[bsrc], W=[bdst])
        P.act(lambda e: e.activation(out=dstap, in_=dstap, func=AF.Ln, bias=1.0), R=[bdst], W=[bdst])
        P.act(lambda e: e.activation(out=dstap, in_=dstap, func=AF.Exp, scale=-1.0), R=[bdst], W=[bdst])

    for i in range(NTILE):
        bi = i % 2
        c = 1 if i == 0 else 0
        t0 = i * TN
        s0, s1 = (0, LCTX) if i == 0 else (LCTX, T)
        lo = max(0, HALO - (t0 - s0))
        hi = min(NW, HALO + (s1 - t0))
        g0, g1 = t0 - HALO + lo, t0 - HALO + hi
        P.dma("sp", lambda e, bi=bi, lo=lo, hi=hi, g0=g0, g1=g1: e.dma_start(
            out=xs[bi][:, :, lo:hi], in_=src_cols(g0, g1)), R=[src_bufs[j] for j in range(max(0, i - 1), min(NTILE, i + 2))],
            W=[b_xs[bi]], semkey="exs%d" % bi)
        if lo > 0 or hi < NW:
            P.dve(lambda e, bi=bi: e.memset(xm[bi], 0.0), W=[b_xm[bi]])
        for kt in range(KT):
            P.act(lambda e, kt=kt, bi=bi, c=c, lo=lo, hi=hi: e.activation(
                out=xm[bi][:, kt, lo:hi], in_=xs[bi][:, kt, lo:hi], func=AF.Identity,
                scale=k.mod1p[:, l, 4 * 8 + kt, c:c + 1], bias=k.mods[:, l, 3 * 8 + kt, c:c + 1]),
                R=[b_xs[bi], k.b_mods], W=[b_xm[bi]])
        def qkv_gen(ft, i=i, bi=bi, t0=t0):
            j = cn[0] % 2
            cn[0] += 1
            for kt in range(KT):
                P.pe(lambda e, kt=kt: e.matmul(
                    pp[j][:, :260], lhsT=win[:, kt, ft * 128:(ft + 1) * 128], rhs=xm[bi][:, kt, 13:273],
                    start=(kt == 0), stop=(kt == KT - 1)), R=[b_wl, b_xm[bi]], W=[bpp[j]])
            yield
            P.dve(lambda e: e.tensor_scalar_mul(out=acc[j], in0=pp[j][:, 0:TN], scalar1=cw5[:, ft, 0:1]),
                  R=[bpp[j], b_cst], W=[b_acc[j]])
            for tap in range(1, 5):
                P.dve(lambda e, tap=tap: e.scalar_tensor_tensor(
                    out=acc[j], in0=pp[j][:, tap:tap + TN], scalar=cw5[:, ft, tap:tap + 1], in1=acc[j],
                    op0=ALU.mult, op1=ALU.add), R=[bpp[j], b_cst, b_acc[j]], W=[b_acc[j]])
            yield
            sigmoid_from(acc[j], b_acc[j], sg[j][:, :TN], b_sg[j], TN)
            yield
            P.dve(lambda e: e.tensor_tensor(out=acc[j], in0=acc[j], in1=sg[j][:, :TN], op=ALU.mult),
                  R=[b_sg[j], b_acc[j]], W=[b_acc[j]])
            hh = ft % 4
            if ft >= 8:
                P.dve(lambda e: e.tensor_copy(out=VV[:, hh, t0:t0 + TN], in_=acc[j]),
                      R=[b_acc[j]], W=[b_VV[hh][2 * i], b_VV[hh][2 * i + 1]])
                return
            yield
            P.act(lambda e: e.activation(out=sqb[j], in_=acc[j], func=AF.Square), R=[b_acc[j]], W=[b_sqb[j]])
            yield
            P.pe(lambda e: e.matmul(pm[j][:, :TN], lhsT=k.ones_bf, rhs=sqb[j], start=True, stop=True),
                 R=[b_sqb[j], k.b_ones], W=[bpm[j]])
            yield
            P.act(lambda e: e.activation(out=rs[j], in_=pm[j][:, :TN], func=AF.Ln, bias=RMS_EPS),
                  R=[bpm[j]], W=[b_rs[j]])
            P.act(lambda e: e.activation(out=rs[j], in_=rs[j], func=AF.Exp, scale=-0.5), R=[b_rs[j]], W=[b_rs[j]])
            yield
            if ft < 4:
                P.dve(lambda e: e.scalar_tensor_tensor(
                    out=QN[:, hh, t0:t0 + TN], in0=acc[j], scalar=128 ** -0.5, in1=rs[j], op0=ALU.mult, op1=ALU.mult),
                    R=[b_acc[j], b_rs[j]], W=[b_QN[hh][2 * i], b_QN[hh][2 * i + 1]])
            else:
                P.dve(lambda e: e.tensor_tensor(
                    out=KN[:, hh, t0:t0 + TN], in0=acc[j], in1=rs[j], op=ALU.mult),
                    R=[b_acc[j], b_rs[j]], W=[b_KN[hh][2 * i], b_KN[hh][2 * i + 1]])
        run_window((qkv_gen(ft) for ft in range(12)), 2)
        for ct in range(4):
            j = cn[0] % 2
            cn[0] += 1
            for kt in range(KT):
                P.pe(lambda e, kt=kt, ct=ct, j=j, bi=bi: e.matmul(
                    pp[j][:, :NW], lhsT=win[:, kt, 2064 + ct * 128:2064 + (ct + 1) * 128], rhs=xm[bi][:, kt, :],
                    start=(kt == 0), stop=(kt == KT - 1)), R=[b_wl, b_xm[bi]], W=[bpp[j]])
            for kt in range(KT):
                P.pe(lambda e, kt=kt, ct=ct, j=j, bi=bi: e.matmul(
                    pq[j][:, :NW], lhsT=win[:, kt, 2576 + ct * 128:2576 + (ct + 1) * 128], rhs=xm[bi][:, kt, :],
                    start=(kt == 0), stop=(kt == KT - 1)), R=[b_wl, b_xm[bi]], W=[bpq[j]])
            sigmoid_from(pq[j][:, :NW], bpq[j], sg[j], b_sg[j], NW)
            P.dve(lambda e, j=j: e.tensor_tensor(out=uu[j], in0=pp[j][:, :NW], in1=sg[j], op=ALU.mult),
                  R=[bpp[j], b_sg[j]], W=[b_uu[j]])
            P.dve(lambda e, j=j, ct=ct: e.tensor_scalar(out=yy[:, ct, :], in0=uu[j][:, 0:TN], scalar1=cw31[:, ct, 0:1],
                                                       scalar2=cvec[:, ct, 0:1], op0=ALU.mult, op1=ALU.add),
                  R=[b_uu[j], b_cst], W=[b_yy[ct]])
            for tap in range(1, 31):
                P.dve(lambda e, j=j, ct=ct, tap=tap: e.scalar_tensor_tensor(
                    out=yy[:, ct, :], in0=uu[j][:, tap:tap + TN], scalar=cw31[:, ct, tap:tap + 1], in1=yy[:, ct, :],
                    op0=ALU.mult, op1=ALU.add), R=[b_uu[j], b_cst, b_yy[ct]], W=[b_yy[ct]])
            P.act(lambda e, j=j, ct=ct: e.activation(out=yq[j], in_=yy[:, ct, :], func=AF.Square), R=[b_yy[ct]], W=[b_yq[j]])
            P.dve(lambda e, j=j, ct=ct: e.tensor_copy(out=yb[j], in_=yy[:, ct, :]), R=[b_yy[ct]], W=[b_yb[j]])
            P.pe(lambda e, j=j, ct=ct: e.matmul(pS1[:, :TN], lhsT=k.ones_bf, rhs=yb[j], start=(ct == 0), stop=(ct == 3)),
                 R=[b_yb[j], k.b_ones], W=[bS1])
            P.pe(lambda e, j=j, ct=ct: e.matmul(pS2[:, :TN], lhsT=k.ones_bf, rhs=yq[j], start=(ct == 0), stop=(ct == 3)),
                 R=[b_yq[j], k.b_ones], W=[bS2])
        P.dve(lambda e: e.tensor_scalar_mul(out=st_m, in0=pS1[:, :TN], scalar1=1.0 / 512), R=[bS1], W=[b_st])
        P.dve(lambda e: e.tensor_tensor(out=st_v, in0=st_m, in1=st_m, op=ALU.mult), R=[b_st], W=[b_st])
        P.dve(lambda e: e.scalar_tensor_tensor(out=st_v, in0=pS2[:, :TN], scalar=1.0 / 512, in1=st_v,
                                               op0=ALU.mult, op1=ALU.subtract), R=[bS2, b_st], W=[b_st])
        P.act(lambda e: e.activation(out=st_r, in_=st_v, func=AF.Ln, bias=LN_EPS), R=[b_st], W=[b_st])
        P.act(lambda e: e.activation(out=st_r, in_=st_r, func=AF.Exp, scale=-0.5), R=[b_st], W=[b_st])
        for ct in range(4):
            j = ct % 2
            P.dve(lambda e, ct=ct: e.tensor_tensor(out=yy[:, ct, :], in0=yy[:, ct, :], in1=st_m, op=ALU.subtract),
                  R=[b_st, b_yy[ct]], W=[b_yy[ct]])
            P.dve(lambda e, ct=ct: e.tensor_tensor(out=yy[:, ct, :], in0=yy[:, ct, :], in1=st_r, op=ALU.mult),
                  R=[b_st, b_yy[ct]], W=[b_yy[ct]])
            P.act(lambda e, ct=ct: e.activation(out=yy[:, ct, :], in_=yy[:, ct, :], func=AF.Identity,
                                              scale=cvec[:, ct, 1:2], bias=cvec[:, ct, 2:3]),
                  R=[b_yy[ct], b_cst], W=[b_yy[ct]])
            sigmoid_from(yy[:, ct, :], b_yy[ct], sg[j][:, :TN], b_sg[j], TN)
            P.dve(lambda e, ct=ct, j=j, bi=bi: e.tensor_tensor(out=cft[bi][:, ct, :], in0=yy[:, ct, :], in1=sg[j][:, :TN],
                                                             op=ALU.mult), R=[b_yy[ct], b_sg[j]], W=[b_cft[bi]])
        P.dma("sp", lambda e, bi=bi, t0=t0: e.dma_start(
            out=k.CFd[:, t0:t0 + TN].rearrange("(ct p) t -> p ct t", p=128), in_=cft[bi]),
            R=[b_cft[bi]], W=[b_CFd[i]], semkey="ecf%d" % bi)
        for blk in range(2):
            j = cn[0] % 2
            cn[0] += 1
            ch = 2 * i + blk
            c0 = HALO + blk * 128
            for kt in range(KT):
                P.pe(lambda e, kt=kt, j=j, bi=bi, c0=c0: e.matmul(
                    pp[j][:, :512], lhsT=xm[bi][:, kt, c0:c0 + 128], rhs=win[:, kt, 1536:2048],
                    start=(kt == 0), stop=(kt == KT - 1)), R=[b_wl, b_xm[bi]], W=[bpp[j]])
            for kt in range(KT):
                P.pe(lambda e, kt=kt, j=j, bi=bi, c0=c0: e.matmul(
                    pq[j][:, :16], lhsT=xm[bi][:, kt, c0:c0 + 128], rhs=win[:, kt, 2048:2064],
                    start=(kt == 0), stop=(kt == KT - 1)), R=[b_wl, b_xm[bi]], W=[bpq[j]])
            sigmoid_from(pp[j][:, :512], bpp[j], zs[j], b_zs[j], 512)
            P.dve(lambda e, j=j: e.tensor_tensor(out=zb[j], in0=pp[j][:, :512], in1=zs[j], op=ALU.mult),
                  R=[bpp[j], b_zs[j]], W=[b_zb[j]])
            P.dma("sp", lambda e, j=j, ch=ch: e.dma_start(out=k.ZSd[ch * 128:(ch + 1) * 128, :], in_=zb[j]),
                  R=[b_zb[j]], W=[b_ZSd[ch]], semkey="ezs%d" % j)
            P.small_on = True
            P.dve(lambda e, j=j: e.tensor_tensor(out=gt1[j][:, 0:8], in0=pq[j][:, 0:8], in1=rowc[:, 0, :], op=ALU.add),
                  R=[bpq[j], b_cst], W=[b_gt1[j]])
            P.act(lambda e, j=j: e.activation(out=gt1[j][:, 0:8], in_=gt1[j][:, 0:8], func=AF.Exp), R=[b_gt1[j]], W=[b_gt1[j]])
            P.act(lambda e, j=j: e.activation(out=gt1[j][:, 0:8], in_=gt1[j][:, 0:8], func=AF.Ln, bias=1.0),
                  R=[b_gt1[j]], W=[b_gt1[j]])
            P.dve(lambda e, j=j, ch=ch: e.tensor_tensor(out=GT[:, ch, 0:8], in0=gt1[j][:, 0:8], in1=rowc[:, 2, :], op=ALU.mult),
                  R=[b_gt1[j], b_cst], W=[b_GT[ch]])
            P.act(lambda e, j=j: e.activation(out=gt1[j][:, 8:16], in_=pq[j][:, 8:16], func=AF.Exp, scale=-1.0),
                  R=[bpq[j]], W=[b_gt1[j]])
            P.act(lambda e, j=j: e.activation(out=gt1[j][:, 8:16], in_=gt1[j][:, 8:16], func=AF.Ln, bias=1.0),
                  R=[b_gt1[j]], W=[b_gt1[j]])
            P.act(lambda e, j=j, ch=ch: e.activation(out=GT[:, ch, 8:16], in_=gt1[j][:, 8:16], func=AF.Exp, scale=-1.0),
                  R=[b_gt1[j]], W=[b_GT[ch]])
            P.small_on = False
    A.off = A.set_mark2
    P.barrier(lambda e: e.memset(k.scr, 0.0))
    insts = [(d, h) for d in range(2) for h in range(4)]
    order = [list(range(NCH)), [1, 0] + list(range(NCH - 1, 1, -1))]

    def mk(n, shape, dt):
        return [A.alloc(shape, dt) for _ in range(n)]
    S_ = mk(8, [128], F32)
    Sb = mk(8, [128], BF16)
    cols = mk(8, [16], F32)
    ktl = mk(8, [128], BF16)
    vtk = mk(8, [128], F32)
    dg = mk(8, [128], F32)
    DT = mk(8, [128], F32)
    AT = mk(8, [128], BF16)
    WA = mk(8, [128], F32)
    WB = mk(8, [128], F32)
    PP = mk(8, [128], F32)
    r2 = mk(8, [128], F32)
    vn = mk(8, [128], BF16)
    o1 = mk(8, [128], F32)
    oo = mk(8, [128], F32)
    bI = [Buf() for _ in range(8)]
    bO = [Buf() for _ in range(8)]
    b_Od = [[Buf() for _ in range(NCH)] for _ in range(2)]
    for n in range(8):
        P.dve(lambda e, n=n: e.memset(S_[n], 0.0), W=[bI[n]])
        P.dve(lambda e, n=n: e.memset(Sb[n], 0.0), W=[bI[n]])
    bank = k.ps
    bB = k.psb

    def reg(n, r):
        return bank[n][:, r * 128:(r + 1) * 128]

    def regb(n):
        return bank[n][:, 384:448].bitcast(BF16)

    def regb2(n):
        return bank[n][:, 448:512].bitcast(BF16)

    def inst_gen(n, d, h, ch, tk):
        gcol = GT[:, ch, d * 4 + h:d * 4 + h + 1]
        bcol = GT[:, ch, 8 + d * 4 + h:8 + d * 4 + h + 1]
        cl = cols[n]
        kc = KN[:, h, tk:tk + 128]
        qc = QN[:, h, tk:tk + 128]
        P.pe(lambda e: e.matmul(reg(n, 0)[:, 0:1], lhsT=(UT if d == 0 else LT), rhs=gcol, start=True, stop=True),
             R=[b_GT[ch], b_cst], W=[bB[n]])
        P.pe(lambda e: e.matmul(reg(n, 0)[:, 1:2], lhsT=onesF, rhs=gcol, start=True, stop=True),
             R=[b_GT[ch], b_cst], W=[bB[n]])
        P.pe(lambda e: e.transpose(regb(n), kc, identb), R=[b_KN[h][ch], b_cst], W=[bB[n]])
        P.pe(lambda e: e.transpose(regb2(n), VV[:, h, tk:tk + 128], identb), R=[b_VV[h][ch], b_cst], W=[bB[n]])
        P.pe(lambda e: e.matmul(reg(n, 1), lhsT=kc, rhs=kc, start=True, stop=True), R=[b_KN[h][ch]], W=[bB[n]])
        P.pe(lambda e: e.matmul(reg(n, 2), lhsT=kc, rhs=qc, start=True, stop=True),
             R=[b_KN[h][ch], b_QN[h][ch]], W=[bB[n]])
        yield
        P.dve(lambda e: e.tensor_copy(out=cl[:, 0:2], in_=reg(n, 0)[:, 0:2]), R=[bB[n]], W=[bI[n]], small=True)
        P.dve(lambda e: e.tensor_scalar_mul(out=cl[:, 2:3], in0=cl[:, 0:1], scalar1=-1.0), R=[bI[n]], W=[bI[n]], small=True)
        P.act(lambda e: e.activation(out=vtk[n], in_=regb2(n), func=AF.Copy), R=[bB[n]], W=[bI[n]])
        yield
        P.act(lambda e: e.activation(out=cl[:, 3:4], in_=cl[:, 0:1], func=AF.Exp), R=[bI[n]], W=[bI[n]], small=True)
        P.act(lambda e: e.activation(out=cl[:, 4:5], in_=cl[:, 0:1], func=AF.Exp, scale=-1.0, bias=cl[:, 1:2]),
              R=[bI[n]], W=[bI[n]], small=True)
        P.act(lambda e: e.activation(out=cl[:, 5:6], in_=cl[:, 1:2], func=AF.Exp), R=[bI[n]], W=[bI[n]], small=True)
        yield
        P.dve(lambda e: e.scalar_tensor_tensor(out=cl[:, 6:7], in0=bcol, scalar=-1.0, in1=cl[:, 3:4],
                                               op0=ALU.mult, op1=ALU.mult), R=[bI[n], b_GT[ch]], W=[bI[n]], small=True)
        P.dve(lambda e: e.tensor_scalar_mul(out=cl[:, 7:8], in0=cl[:, 3:4], scalar1=-1.0), R=[bI[n]], W=[bI[n]], small=True)
        P.dve(lambda e: e.tensor_scalar_mul(out=ktl[n], in0=regb(n), scalar1=cl[:, 4:5]), R=[bB[n], bI[n]], W=[bI[n]],
              small=True)
        P.dve(lambda e: e.tensor_scalar_mul(out=dg[n], in0=ident, scalar1=cl[:, 0:1]), R=[bI[n], b_cst], W=[bI[n]],
              small=True)
        yield
        P.pe(lambda e: e.matmul(reg(n, 0), lhsT=onesF, rhs=dg[n], start=True, stop=True), R=[bI[n], b_cst], W=[bB[n]])
        yield
        P.dve(lambda e: e.tensor_tensor(out=DT[n], in0=reg(n, 0), in1=negm[d], op=ALU.add), R=[bB[n], b_cst], W=[bI[n]])
        yield
        P.act(lambda e: e.activation(out=DT[n], in_=DT[n], func=AF.Exp, bias=cl[:, 2:3]), R=[bI[n]], W=[bI[n]])
        yield
        P.dve(lambda e: e.tensor_tensor(out=AT[n], in0=reg(n, 2), in1=DT[n], op=ALU.mult), R=[bB[n], bI[n]], W=[bI[n]])
        P.dve(lambda e: e.tensor_tensor(out=WA[n], in0=reg(n, 1), in1=DT[n], op=ALU.mult), R=[bB[n], bI[n]], W=[bI[n]])
        P.dve(lambda e: e.scalar_tensor_tensor(out=WA[n], in0=WA[n], scalar=bcol, in1=smask[d],
                                               op0=ALU.mult, op1=ALU.mult), R=[bI[n], b_GT[ch], b_cst], W=[bI[n]])
        P.dve(lambda e: e.tensor_tensor(out=PP[n], in0=ident, in1=WA[n], op=ALU.subtract), R=[bI[n], b_cst], W=[bI[n]])
        yield
        P.pe(lambda e: e.transpose(reg(n, 3), WA[n], ident), R=[bI[n], b_cst], W=[bB[n]])
        yield
        P.act(lambda e: e.activation(out=WB[n], in_=reg(n, 3), func=AF.Copy), R=[bB[n]], W=[bI[n]])
        yield
        for lev in range(6):
            last = lev == 5
            P.pe(lambda e: e.matmul(reg(n, 1), lhsT=WA[n], rhs=WB[n], start=True, stop=True), R=[bI[n]], W=[bB[n]])
            if not last:
                P.pe(lambda e: e.matmul(reg(n, 2), lhsT=WB[n], rhs=WA[n], start=True, stop=True), R=[bI[n]], W=[bB[n]])
            yield
            P.act(lambda e: e.activation(out=WB[n], in_=reg(n, 1), func=AF.Copy), R=[bB[n]], W=[bI[n]])
            if not last:
                P.dve(lambda e: e.tensor_copy(out=WA[n], in_=reg(n, 2)), R=[bB[n]], W=[bI[n]])
            yield
            P.pe(lambda e: e.matmul(reg(n, 3), lhsT=WB[n], rhs=PP[n], start=True, stop=True), R=[bI[n]], W=[bB[n]])
            yield
            P.dve(lambda e: e.tensor_tensor(out=PP[n], in0=reg(n, 3), in1=PP[n], op=ALU.add), R=[bB[n], bI[n]], W=[bI[n]])
            yield
        P.pe(lambda e: e.matmul(reg(n, 0), lhsT=kc, rhs=Sb[n], start=True, stop=True), R=[b_KN[h][ch], bI[n]], W=[bB[n]])
        P.pe(lambda e: e.matmul(reg(n, 2), lhsT=qc, rhs=Sb[n], start=True, stop=True), R=[b_QN[h][ch], bI[n]], W=[bB[n]])
        yield
        P.dve(lambda e: e.scalar_tensor_tensor(out=r2[n], in0=reg(n, 0), scalar=cl[:, 7:8], in1=vtk[n],
                                               op0=ALU.mult, op1=ALU.add), R=[bB[n], bI[n]], W=[bI[n]])
        yield
        P.pe(lambda e: e.matmul(reg(n, 1), lhsT=PP[n], rhs=r2[n], start=True, stop=True), R=[bI[n]], W=[bB[n]])
        yield
        P.dve(lambda e: e.tensor_scalar_mul(out=vn[n], in0=reg(n, 1), scalar1=bcol), R=[bB[n], b_GT[ch]], W=[bI[n]])
        yield
        P.pe(lambda e: e.matmul(reg(n, 3), lhsT=AT[n], rhs=vn[n], start=True, stop=True), R=[bI[n]], W=[bB[n]])
        P.pe(lambda e: e.matmul(reg(n, 0), lhsT=ktl[n], rhs=vn[n], start=True, stop=True), R=[bI[n]], W=[bB[n]])
        yield
        P.act(lambda e: e.activation(out=o1[n], in_=reg(n, 3), func=AF.Copy), R=[bB[n]], W=[bI[n]])
        P.dve(lambda e: e.scalar_tensor_tensor(out=S_[n], in0=S_[n], scalar=cl[:, 5:6], in1=reg(n, 0),
                                               op0=ALU.mult, op1=ALU.add), R=[bB[n], bI[n]], W=[bI[n]])
        yield
        P.dve(lambda e: e.scalar_tensor_tensor(out=oo[n], in0=reg(n, 2), scalar=cl[:, 3:4], in1=o1[n],
                                               op0=ALU.mult, op1=ALU.add), R=[bB[n], bI[n]], W=[bO[n]])
        P.act(lambda e: e.activation(out=Sb[n], in_=S_[n], func=AF.Copy), R=[bI[n]], W=[bI[n]])
        yield
        P.dma("sp", lambda e: e.dma_start(out=k.Od[d][tk:tk + 128, h * 128:(h + 1) * 128], in_=oo[n]),
              R=[bO[n]], W=[b_Od[d][ch]], semkey="eo%d" % n)

    for step in range(NCH):
        gens = [inst_gen(n, d, h, order[d][step], order[d][step] * 128) for n, (d, h) in enumerate(insts)]
        alive = True
        while alive:
            alive = False
            for g_ in gens:
                try:
                    next(g_)
                    alive = True
                except StopIteration:
                    pass
    A.off = A.set_mark2
    P.barrier(lambda e: e.memset(k.scr, 0.0))
    wout = A.alloc([KT, D], BF16)
    for kt in range(KT):
        P.dma("pool", lambda e, kt=kt: e.dma_start(out=wout[:, kt, :], in_=k.even_w_out[kt * 128:(kt + 1) * 128, :]),
              W=[Buf()], semkey="ewout")
    b_wo = Buf()
    P.act(lambda e: e.activation(out=k.scr[:, 0:1], in_=k.scr[:, 1:2], func=AF.Copy), W=[b_wo])
    k.P.ops[-1].dw["ewout"] = k.P.dcnt["ewout"]
    of_ = [A.alloc([512], F32) for _ in range(2)]
    ob_ = [A.alloc([512], F32) for _ in range(2)]
    zl = [A.alloc([512], BF16) for _ in range(2)]
    sqj = A.alloc([512], F32)
    ss = [A.alloc([4], F32) for _ in range(2)]
    gmb = [A.alloc([512], BF16) for _ in range(2)]
    gTt = [A.alloc([4, TN], BF16) for _ in range(2)]
    cfl = [A.alloc([4, TN], BF16) for _ in range(2)]
    b_of, b_ob, b_zl, b_ss, b_gmb, b_gT, b_cfl = ([Buf(), Buf()] for _ in range(7))
    b_sqj = Buf()
    ptr = k.ps[0:2]
    bptr = k.psb[0:2]
    cnm = [0]

    def pre_tile(i, idx):
        bi = idx % 2
        for blk in range(2):
            j = cnm[0] % 2
            cnm[0] += 1
            ch = 2 * i + blk
            P.dma("sp", lambda e, j=j, ch=ch: e.dma_start(out=of_[j], in_=k.Od[0][ch * 128:(ch + 1) * 128, :]),
                  R=[b_Od[0][ch]], W=[b_of[j]], semkey="mof%d" % j)
            P.dma("sp", lambda e, j=j, ch=ch: e.dma_start(out=ob_[j], in_=k.Od[1][ch * 128:(ch + 1) * 128, :]),
                  R=[b_Od[1][ch]], W=[b_ob[j]], semkey="mob%d" % j)
            P.dma("sp", lambda e, j=j, ch=ch: e.dma_start(out=zl[j], in_=k.ZSd[ch * 128:(ch + 1) * 128, :]),
                  R=[b_ZSd[ch]], W=[b_zl[j]], semkey="mzl%d" % j)
            P.dve(lambda e, j=j: e.tensor_tensor(out=of_[j], in0=of_[j], in1=ob_[j], op=ALU.add),
                  R=[b_of[j], b_ob[j]], W=[b_of[j]])
            P.small_on = True
            P.dve(lambda e, j=j: e.memset(ss[j], 0.0), W=[b_ss[j]])
            for hh in range(4):
                P.act(lambda e, j=j, hh=hh: e.activation(out=sqj[:, hh * 128:(hh + 1) * 128],
                                                        in_=of_[j][:, hh * 128:(hh + 1) * 128], func=AF.Square,
                                                        accum_out=ss[j][:, hh:hh + 1]),
                      R=[b_of[j]], W=[b_sqj, b_ss[j]])
            P.act(lambda e, j=j: e.activation(out=ss[j], in_=ss[j], func=AF.Ln, scale=1.0 / 128, bias=RMS_EPS),
                  R=[b_ss[j]], W=[b_ss[j]])
            P.act(lambda e, j=j: e.activation(out=ss[j], in_=ss[j], func=AF.Exp, scale=-0.5), R=[b_ss[j]], W=[b_ss[j]])
            P.small_on = False
            for hh in range(4):
                P.dve(lambda e, j=j, hh=hh: e.scalar_tensor_tensor(
                    out=of_[j][:, hh * 128:(hh + 1) * 128], in0=of_[j][:, hh * 128:(hh + 1) * 128],
                    scalar=ss[j][:, hh:hh + 1], in1=gnorm[:, hh * 128:(hh + 1) * 128], op0=ALU.mult, op1=ALU.mult),
                    R=[b_of[j], b_ss[j], b_cst], W=[b_of[j]])
            P.dve(lambda e, j=j: e.tensor_tensor(out=gmb[j], in0=of_[j], in1=zl[j], op=ALU.mult),
                  R=[b_of[j], b_zl[j]], W=[b_gmb[j]])
            for hh in range(4):
                jj = (hh + blk) % 2
                P.pe(lambda e, j=j, hh=hh, jj=jj: e.transpose(ptr[jj][:, 0:64].bitcast(BF16),
                                                              gmb[j][:, hh * 128:(hh + 1) * 128], identb),
                     R=[b_gmb[j], b_cst], W=[bptr[jj]])
                P.act(lambda e, hh=hh, jj=jj, bi=bi, blk=blk: e.activation(
                    out=gTt[bi][:, hh, blk * 128:(blk + 1) * 128], in_=ptr[jj][:, 0:64].bitcast(BF16), func=AF.Copy),
                    R=[bptr[jj]], W=[b_gT[bi]])
        P.dma("sp", lambda e, bi=bi, i=i: e.dma_start(
            out=cfl[bi], in_=k.CFd[:, i * TN:(i + 1) * TN].rearrange("(ct p) t -> p ct t", p=128)),
            R=[b_CFd[i]], W=[b_cfl[bi]], semkey="mcf%d" % bi)

    def rhs_fn(i, f, idx):
        bi = idx % 2
        return gTt[bi][:, f, :] if f < 4 else cfl[bi][:, f - 4, :]

    def rhs_bufs(i, f, idx):
        bi = idx % 2
        return [b_gT[bi]] if f < 4 else [b_cfl[bi]]

    emit_proj_ln(k, l, l * 3 + 1, 5, 8, wout, b_wo, rhs_fn, rhs_bufs, src, src_bufs, dst, dst_bufs,
                 list(range(NTILE)), pre_tile=pre_tile)


def kernel(**inputs):
    if "nc" not in _CACHE:
        _CACHE["nc"] = build_program()[0]
    nc = _CACHE["nc"]
    maps = prep_inputs(inputs)
    res = run_bass_kernel_spmd(nc, maps, core_ids=list(range(8)))
    out = np.stack([np.ascontiguousarray(r["yT"].T) for r in res.results], axis=0)
    return out.astype(np.float32)
```

```python
import contextlib
import os
import numpy as np
import concourse.bass as bass
import concourse.mybir as mybir
from concourse.bass_utils import run_bass_kernel_spmd

F32 = mybir.dt.float32
BF16 = mybir.dt.bfloat16
AF = mybir.ActivationFunctionType
ALU = mybir.AluOpType
AX = mybir.AxisListType

ENGS = ("pe", "act", "dve", "pool", "sp")
SAME_ENGINE_SYNC = False

D = 1024
KT = 8
DFF = 2816
FT = 22
LCTX = 256
LLAT = 4096
T = LCTX + LLAT
TN = 256
NTILE = T // TN
DEPTH = 2
ALPHA = (2.0 * DEPTH) ** 0.25
LN_EPS = 1e-5
RMS_EPS = 1e-6


class Buf:
    __slots__ = ("name", "lw", "rd")

    def __init__(self, name=""):
        self.name = name
        self.lw = None
        self.rd = {}


class Op:
    __slots__ = ("eng", "fn", "cw", "dw", "is_dma", "need_inc", "ticket", "semkey", "semval", "small")

    def __init__(self, eng, fn, is_dma, semkey):
        self.eng = eng
        self.fn = fn
        self.cw = {}
        self.dw = {}
        self.is_dma = is_dma
        self.need_inc = False
        self.ticket = 0
        self.semkey = semkey
        self.semval = 0
        self.small = False


class Prog:
    def __init__(self):
        self.small_on = False
        self.ops = []
        self.dcnt = {}
        self.last = {}
        self.barrier_idx = None

    def _dep(self, op, d):
        dop = self.ops[d]
        if dop.is_dma:
            k = dop.semkey
            v = self.dcnt[k]
            if v > op.dw.get(k, 0):
                op.dw[k] = v
        else:
            if dop.eng == op.eng and not op.is_dma:
                if dop.eng == "pe":
                    return
                if not SAME_ENGINE_SYNC and not (dop.small or op.small):
                    return
            if d > op.cw.get(dop.eng, -1):
                op.cw[dop.eng] = d

    def add(self, eng, fn, R=(), W=(), dma=False, semkey=None, small=None):
        i = len(self.ops)
        op = Op(eng, fn, dma, semkey)
        op.small = self.small_on if small is None else small
        self.ops.append(op)
        for b in R:
            if b.lw is not None:
                self._dep(op, b.lw)
        for b in W:
            if b.lw is not None:
                self._dep(op, b.lw)
            for r in b.rd.values():
                self._dep(op, r)
        if self.barrier_idx is not None:
            self._dep(op, self.barrier_idx)
        rk = ("d", semkey) if dma else eng
        for b in R:
            b.rd[rk] = i
        for b in W:
            b.lw = i
            b.rd = {}
        if dma:
            self.dcnt[semkey] = self.dcnt.get(semkey, 0) + 16
            op.semval = self.dcnt[semkey]
            self.last[("d", semkey)] = i
        else:
            self.last[eng] = i
        return i

    def pe(self, fn, R=(), W=(), small=None):
        return self.add("pe", fn, R, W, small=small)

    def act(self, fn, R=(), W=(), small=None):
        return self.add("act", fn, R, W, small=small)

    def dve(self, fn, R=(), W=(), small=None):
        return self.add("dve", fn, R, W, small=small)

    def pool(self, fn, R=(), W=()):
        return self.add("pool", fn, R, W)

    def dma(self, q, fn, R=(), W=(), semkey=None):
        return self.add(q, fn, R, W, dma=True, semkey=semkey)

    def barrier(self, fn):
        i = len(self.ops)
        op = Op("dve", fn, False, None)
        op.small = True
        self.ops.append(op)
        for k, d in self.last.items():
            dop = self.ops[d]
            if dop.is_dma:
                op.dw[dop.semkey] = self.dcnt[dop.semkey]
            elif d > op.cw.get(dop.eng, -1):
                op.cw[dop.eng] = d
        self.last["dve"] = i
        self.barrier_idx = i
        return i

    def emit(self, nc, final_wait_eng="sp"):
        ops = self.ops
        for op in ops:
            for e, d in op.cw.items():
                ops[d].need_inc = True
        cnt = {e: 0 for e in ENGS}
        for op in ops:
            if not op.is_dma and op.need_inc:
                cnt[op.eng] += 1
                op.ticket = cnt[op.eng]
        dcnt = self.dcnt
        self.stats = dict(cnt=dict(cnt), n_ops=len(ops), n_dma_sems=len(dcnt))
        per_eng = {e: [] for e in ENGS}
        for op in ops:
            per_eng[op.eng].append(op)
        with contextlib.ExitStack() as es:
            esem = {e: es.enter_context(nc.semaphore("s_" + e)) for e in ENGS}
            dsem = {k: es.enter_context(nc.semaphore("d_%s" % (k,))) for k in dcnt}
            block = es.enter_context(nc.Block())

            def make(e):
                def body(eng):
                    known = {f: 0 for f in ENGS}
                    kd = {}
                    for op in per_eng[e]:
                        for f, d in op.cw.items():
                            t = ops[d].ticket
                            if t > known[f]:
                                eng.wait_ge(esem[f], t)
                                known[f] = t
                        for k, v in op.dw.items():
                            if v > kd.get(k, 0):
                                eng.wait_ge(dsem[k], v)
                                kd[k] = v
                        ins = op.fn(eng)
                        if op.is_dma:
                            ins.then_inc(dsem[op.semkey], 16)
                        elif op.need_inc:
                            ins.then_inc(esem[e], 1)
                    if e == final_wait_eng:
                        for f in ENGS:
                            if f != e and cnt[f] > 0:
                                eng.wait_ge(esem[f], cnt[f])
                        for k, v in dcnt.items():
                            eng.wait_ge(dsem[k], v)
                return body

            block.tensor(make("pe"))
            block.scalar(make("act"))
            block.vector(make("dve"))
            block.gpsimd(make("pool"))
            block.sync(make("sp"))


def run_window(gens, width=2):
    it = iter(gens)
    active = []
    done = False
    while True:
        while len(active) < width and not done:
            try:
                active.append(next(it))
            except StopIteration:
                done = True
        if not active:
            break
        for g_ in list(active):
            try:
                next(g_)
            except StopIteration:
                active.remove(g_)


class Arena:
    def __init__(self, t, nwords):
        self.t = t
        self.n = nwords
        self.off = 0
        self.mark = 0

    def alloc(self, shape, dtype):
        n = 1
        for s in shape:
            n *= s
        if dtype == BF16:
            w = (n + 1) // 2
        else:
            w = n
        assert self.off + w <= self.n, ("SBUF arena overflow", self.off, w, self.n)
        v = self.t[:, self.off:self.off + w]
        self.off += w
        if dtype == BF16:
            v = v.bitcast(BF16)[:, :n]
        if len(shape) == 1:
            return v
        if len(shape) == 2:
            return v.rearrange("p (a b) -> p a b", a=shape[0])
        if len(shape) == 3:
            return v.rearrange("p (a b c) -> p a b c", a=shape[0], b=shape[1])
        raise ValueError(shape)

    def set_mark(self):
        self.mark = self.off

    def reset(self):
        self.off = self.mark


class K:
    pass


def build_program(stages=("mods", "l0f1", "l0mix", "l0f2", "l1f1", "l1mix", "l1f2"), dbg=None):
    nc = bass.Bass("TRN2", target_bir_lowering=False)
    k = K()
    k.nc = nc
    k.P = Prog()
    P = k.P

    def din(name, shape):
        return nc.dram_tensor(name, list(shape), F32, kind="ExternalInput").ap()

    k.xT = din("xT", [D, LLAT])
    k.ctxT = din("ctxT", [D, LCTX])
    k.cond = din("cond", [128, KT, 2])
    k.ada_w = [din("ada_w%d" % l, [D, 9 * D]) if "mods" in stages else None for l in range(DEPTH)]
    k.ada_b = din("ada_b", [128, DEPTH, 72])
    k.ln_g = din("ln_g", [128, DEPTH * 3, KT])
    k.ln_b = din("ln_b", [128, DEPTH * 3, KT])
    k.w_gu = [[din("w_gu%d%d" % (l, s), [D, 2 * DFF]) if ("l%df%d" % (l, s + 1)) in stages else None
               for s in range(2)] for l in range(DEPTH)]
    k.w_dn = [[din("w_dn%d%d" % (l, s), [DFF, D]) if ("l%df%d" % (l, s + 1)) in stages else None
               for s in range(2)] for l in range(DEPTH)]
    if "l1mix" in stages:
        k.attn_w_in = din("attn_w_in", [D, 1536])
        k.attn_w_out = din("attn_w_out", [D, D])
        k.rotm = din("rotm", [128, 128])
        k.qk_gain = din("qk_gain", [128, 2])
        k.rope = din("rope", [128, 2, LLAT])
    if "l0mix" in stages:
        k.even_w_in = din("even_w_in", [D, 3088])
        k.even_w_out = din("even_w_out", [D, D])
        k.gdnc = din("gdnc", [128, 8, 128])
        k.cw5 = din("cw5", [128, 12, 5])
        k.cw31 = din("cw31", [128, 4, 31])
        k.cvec = din("cvec", [128, 4, 3])
        k.rowc = din("rowc", [128, 2, 8])
        k.gnorm = din("gnorm", [128, 512])
        k.CFd = nc.dram_tensor("CFd", [512, T], BF16, kind="Internal").ap()
        k.ZSd = nc.dram_tensor("ZSd", [T, 512], BF16, kind="Internal").ap()
        k.Od = [nc.dram_tensor("Od%d" % i, [T, 512], F32, kind="Internal").ap() for i in range(2)]
    k.out = nc.dram_tensor("yT", [D, LLAT], F32, kind="ExternalOutput").ap()
    k.S = [nc.dram_tensor("stream%d" % i, [D, T], F32, kind=("ExternalOutput" if dbg else "Internal")).ap() for i in range(2)]

    with contextlib.ExitStack() as es:
        arena_t = es.enter_context(nc.sbuf_tensor("arena", [128, 53200], F32))
        k.A = Arena(arena_t, 53200)
        k.ps = [es.enter_context(nc.psum_tensor("ps%d" % i, [128, 512], F32)) for i in range(8)]
        k.psb = [Buf("ps%d" % i) for i in range(8)]
        setup_consts(k)
        if "mods" in stages:
            emit_mods(k)
        if dbg == "mods":
            dm = nc.dram_tensor("dbg_mods", [128, DEPTH * 72 * 2], F32, kind="ExternalOutput").ap()
            P.dma("sp", lambda e: e.dma_start(out=dm, in_=k.mods.rearrange("p l n c -> p (l n c)")), R=[k.b_mods], semkey="dbg")
        def in_tile(i):
            if i == 0:
                return k.ctxT.rearrange("(kt p) t -> p kt t", p=128)
            return k.xT[:, (i - 1) * TN:i * TN].rearrange("(kt p) t -> p kt t", p=128)

        def s_tile(s):
            return lambda i: k.S[s][:, i * TN:(i + 1) * TN].rearrange("(kt p) t -> p kt t", p=128)

        def out_tile(i):
            return k.out[:, (i - 1) * TN:i * TN].rearrange("(kt p) t -> p kt t", p=128)

        k.sbuf_S = [[Buf("S%d_%d" % (s_, i)) for i in range(NTILE)] for s_ in range(2)]
        k.b_in = [Buf("in%d" % i) for i in range(NTILE)]
        k.b_out = [Buf("out%d" % i) for i in range(NTILE)]
        def in_cols(g0, g1):
            if g1 <= LCTX:
                return k.ctxT[:, g0:g1].rearrange("(kt p) t -> p kt t", p=128)
            assert g0 >= LCTX
            return k.xT[:, g0 - LCTX:g1 - LCTX].rearrange("(kt p) t -> p kt t", p=128)
        cur = (in_tile, k.b_in, in_cols)
        nxt = [0]

        def scratch():
            j = nxt[0]
            nxt[0] = 1 - j
            return (s_tile(j), k.sbuf_S[j],
                    lambda g0, g1, j=j: k.S[j][:, g0:g1].rearrange("(kt p) t -> p kt t", p=128))
        for st in ("l0f1", "l0mix", "l0f2", "l1f1", "l1mix", "l1f2"):
            if st not in stages:
                continue
            l = int(st[1])
            if st == "l1f2":
                dstp = (out_tile, k.b_out, None)
                emit_ffn(k, 1, 1, cur[0], cur[1], dstp[0], dstp[1], range(1, NTILE))
            elif st.endswith("f1") or st.endswith("f2"):
                dstp = scratch()
                tl = range(NTILE) if st != "l1f2" else range(1, NTILE)
                emit_ffn(k, l, 0 if st.endswith("f1") else 1, cur[0], cur[1], dstp[0], dstp[1], tl)
            elif st == "l1mix":
                dstp = scratch()
                emit_attn(k, 1, cur[0], cur[1], dstp[0], dstp[1])
            elif st == "l0mix":
                dstp = scratch()
                emit_even(k, 0, cur[2], cur[1], cur[0], dstp[0], dstp[1])
            cur = dstp
        P.emit(nc)
    k.stats = P.stats
    return nc, k


def setup_consts(k):
    nc, P, A = k.nc, k.P, k.A
    P.small_on = True
    k.ones_bf = A.alloc([128], BF16)
    k.b_ones = Buf("ones")
    P.dve(lambda e: e.memset(k.ones_bf, 1.0), W=[k.b_ones])
    k.scr = A.alloc([8], F32)
    k.mods = A.alloc([DEPTH, 72, 2], F32)
    k.b_mods = Buf("mods")
    k.mod1p = A.alloc([DEPTH, 72, 2], F32)
    k.lng = A.alloc([DEPTH * 3, KT], F32)
    k.lnb = A.alloc([DEPTH * 3, KT], F32)
    k.b_ln = Buf("ln")
    P.dma("sp", lambda e: e.dma_start(out=k.lng, in_=k.ln_g),
          W=[k.b_ln], semkey="c0")
    P.dma("sp", lambda e: e.dma_start(out=k.lnb, in_=k.ln_b),
          W=[k.b_ln], semkey="c0")
    A.set_mark()


def emit_mods(k):
    nc, P, A = k.nc, k.P, k.A
    A.reset()
    P.small_on = True
    cs = A.alloc([KT, 2], F32)
    sc = A.alloc([KT, 2], F32)
    bsb = A.alloc([DEPTH, 72], F32)
    NCH = 1024
    wbuf = [A.alloc([KT, NCH], F32) for _ in range(2)]
    b_c, b_sc, b_b = Buf(), Buf(), Buf()
    b_w = [Buf(), Buf()]
    P.dma("sp", lambda e: e.dma_start(out=cs, in_=k.cond), W=[b_c], semkey="c1")
    P.dma("sp", lambda e: e.dma_start(out=bsb, in_=k.ada_b), W=[b_b], semkey="c2")
    P.act(lambda e: e.activation(out=sc, in_=cs, func=AF.Exp, scale=-1.0), R=[b_c], W=[b_sc])
    P.act(lambda e: e.activation(out=sc, in_=sc, func=AF.Ln, bias=1.0), R=[b_sc], W=[b_sc])
    P.act(lambda e: e.activation(out=sc, in_=sc, func=AF.Exp, scale=-1.0), R=[b_sc], W=[b_sc])
    P.dve(lambda e: e.tensor_tensor(out=sc, in0=sc, in1=cs, op=ALU.mult), R=[b_sc, b_c], W=[b_sc])
    pst = k.ps[0]
    for l in range(DEPTH):
        for ch in range(9):
            wb = wbuf[(l * 9 + ch) % 2]
            bw = b_w[(l * 9 + ch) % 2]
            for kt in range(KT):
                P.dma("sp" if (kt % 2 == 0 or os.environ.get("NOACTQ")) else "act",
                      lambda e, l=l, ch=ch, kt=kt, wb=wb: e.dma_start(
                          out=wb[:, kt, :], in_=k.ada_w[l][kt * 128:(kt + 1) * 128, ch * NCH:(ch + 1) * NCH]),
                      W=[bw], semkey="mw%d" % ((l * 9 + ch) % 2))
            for nt in range(8):
                n = ch * 8 + nt
                for kt in range(KT):
                    P.pe(lambda e, wb=wb, kt=kt, nt=nt, n=n: e.matmul(
                        pst[:, 2 * n:2 * n + 2], lhsT=wb[:, kt, nt * 128:(nt + 1) * 128], rhs=sc[:, kt, :],
                        start=(kt == 0), stop=(kt == KT - 1)), R=[bw, b_sc], W=[k.psb[0]])
        P.dve(lambda e, l=l: e.tensor_tensor(
            out=k.mods[:, l, :, :], in0=pst[:, 0:144].rearrange("p (n c) -> p n c", c=2),
            in1=bsb[:, l, :].unsqueeze(2).to_broadcast([128, 72, 2]), op=ALU.add),
            R=[k.psb[0], b_b], W=[k.b_mods])
    P.dve(lambda e: e.tensor_scalar_add(out=k.mod1p, in0=k.mods, scalar1=1.0), R=[k.b_mods], W=[k.b_mods])
    for j in (2, 8):
        P.dve(lambda e, j=j: e.tensor_scalar_mul(out=k.mod1p[:, :, j * 8:(j + 1) * 8, :],
                                                 in0=k.mods[:, :, j * 8:(j + 1) * 8, :], scalar1=0.5),
              R=[k.b_mods], W=[k.b_mods])
    P.small_on = False


def emit_ffn(k, l, s, src, src_bufs, dst, dst_bufs, tiles, final_out=None):
    nc, P, A = k.nc, k.P, k.A
    A.reset()
    P.small_on = False
    P.barrier(lambda e: e.memset(k.scr, 0.0))
    j0 = 0 if s == 0 else 6
    lnrow = l * 3 + (0 if s == 0 else 2)
    NG = 2
    wgu = A.alloc([KT, 2 * DFF], BF16)
    wdn = A.alloc([FT, D], BF16)
    b_wgu = [[[Buf() for _ in range(KT)] for _ in range(NG)] for _ in range(2)]
    b_wdn = [Buf() for _ in range(FT)]
    for g in range(NG):
        for gu in range(2):
            for kt in range(KT):
                c0 = gu * DFF + g * 11 * 128
                P.dma("pool", lambda e, kt=kt, c0=c0: e.dma_start(
                    out=wgu[:, kt, c0:c0 + 11 * 128], in_=k.w_gu[l][s][kt * 128:(kt + 1) * 128, c0:c0 + 11 * 128]),
                    W=[b_wgu[gu][g][kt]], semkey="wgu%d" % (gu * NG + g))
    for f in range(FT):
        P.dma("pool", lambda e, f=f: e.dma_start(out=wdn[:, f, :], in_=k.w_dn[l][s][f * 128:(f + 1) * 128, :]),
              W=[b_wdn[f]], semkey="wdn%d" % (f // 11))
    xs = [A.alloc([KT, TN], F32) for _ in range(2)]
    xm = [A.alloc([KT, TN], BF16) for _ in range(2)]
    h = A.alloc([FT, TN], BF16)
    rr = [A.alloc([KT, TN], F32) for _ in range(2)]
    tmp = [A.alloc([TN], F32) for _ in range(2)]
    sg = [A.alloc([TN], F32) for _ in range(4)]
    rb = [A.alloc([TN], BF16) for _ in range(2)]
    rq = [A.alloc([TN], BF16) for _ in range(2)]
    st_m = A.alloc([TN], F32)
    st_r = A.alloc([TN], F32)
    st_v = A.alloc([TN], F32)
    b_xs = [Buf(), Buf()]
    b_xm = [Buf(), Buf()]
    b_h = [Buf() for _ in range(FT)]
    b_rr = [[Buf() for _ in range(KT)] for _ in range(2)]
    b_rl = [Buf(), Buf()]
    b_tmp = [Buf(), Buf()]
    b_sg = [Buf() for _ in range(4)]
    b_rb = [Buf(), Buf()]
    b_rq = [Buf(), Buf()]
    b_st = Buf()
    pgu, pd, pS1, pS2 = k.ps[0:4], k.ps[4:6], k.ps[6], k.ps[7]
    bpgu, bpd, bS1, bS2 = k.psb[0:4], k.psb[4:6], k.psb[6], k.psb[7]
    pg = [p[:, 0:TN] for p in pgu]
    pu = [p[:, TN:2 * TN] for p in pgu]
    bpg = bpgu
    bpu = bpgu
    tiles = list(tiles)
    cnt_f = [0]
    cnt_d = [0]

    def load_x(idx):
        i = tiles[idx]
        bi = idx % 2
        P.dma("sp", lambda e: e.dma_start(out=xs[bi], in_=src(i)), R=[src_bufs[i]], W=[b_xs[bi]], semkey="xs%d" % bi)

    def load_r(idx):
        i = tiles[idx]
        bi = idx % 2
        P.dma("sp", lambda e: e.dma_start(out=rr[bi], in_=src(i)), R=[src_bufs[i]], W=b_rr[bi] + [b_rl[bi]],
              semkey="rr%d" % bi)

    def modulate(idx):
        i = tiles[idx]
        bi = idx % 2
        c = 1 if i == 0 else 0
        for kt in range(KT):
            P.act(lambda e, kt=kt: e.activation(
                out=xm[bi][:, kt, :], in_=xs[bi][:, kt, :], func=AF.Identity,
                scale=k.mod1p[:, l, (j0 + 1) * 8 + kt, c:c + 1], bias=k.mods[:, l, j0 * 8 + kt, c:c + 1]),
                R=[b_xs[bi], k.b_mods], W=[b_xm[bi]])

    def tail_steps(idx):
        i = tiles[idx]
        bi = idx % 2
        steps = []

        def stat_math():
            P.dve(lambda e: e.tensor_scalar_mul(out=st_m, in0=pS1[:, :TN], scalar1=1.0 / D), R=[bS1], W=[b_st])
            P.dve(lambda e: e.tensor_tensor(out=st_v, in0=st_m, in1=st_m, op=ALU.mult), R=[b_st], W=[b_st])
            P.dve(lambda e: e.scalar_tensor_tensor(out=st_v, in0=pS2[:, :TN], scalar=1.0 / D, in1=st_v,
                                                   op0=ALU.mult, op1=ALU.subtract), R=[bS2, b_st], W=[b_st])
            P.act(lambda e: e.activation(out=st_r, in_=st_v, func=AF.Ln, bias=LN_EPS), R=[b_st], W=[b_st])
            P.act(lambda e: e.activation(out=st_r, in_=st_r, func=AF.Exp, scale=-0.5), R=[b_st], W=[b_st])
        steps.append(stat_math)
        for d in range(KT):
            def norm(d=d):
                P.dve(lambda e: e.tensor_tensor(out=rr[bi][:, d, :], in0=rr[bi][:, d, :], in1=st_m, op=ALU.subtract),
                      R=[b_st, b_rr[bi][d]], W=[b_rr[bi][d]])
                P.dve(lambda e: e.tensor_tensor(out=rr[bi][:, d, :], in0=rr[bi][:, d, :], in1=st_r, op=ALU.mult),
                      R=[b_st, b_rr[bi][d]], W=[b_rr[bi][d]])
            steps.append(norm)

            def affine(d=d):
                P.act(lambda e: e.activation(out=rr[bi][:, d, :], in_=rr[bi][:, d, :], func=AF.Identity,
                                             scale=k.lng[:, lnrow, d:d + 1], bias=k.lnb[:, lnrow, d:d + 1]),
                      R=[b_rr[bi][d], k.b_ln], W=[b_rr[bi][d]])
            steps.append(affine)

        def store():
            if final_out is not None and i >= 1:
                P.dma("sp", lambda e: e.dma_start(out=final_out[0](i), in_=rr[bi]), R=b_rr[bi], W=[final_out[1][i]],
                      semkey="st%d" % bi)
            else:
                P.dma("sp", lambda e: e.dma_start(out=dst(i), in_=rr[bi]), R=b_rr[bi], W=[dst_bufs[i]],
                      semkey="st%d" % bi)
        steps.append(store)
        return steps

    def tile_body(idx, deferred):
        i = tiles[idx]
        bi = idx % 2
        c = 1 if i == 0 else 0
        if idx + 1 < len(tiles):
            load_x(idx + 1)
        load_r(idx)
        if idx == 0:
            modulate(0)
        for f in range(FT):
            fb = cnt_f[0] % 4
            cnt_f[0] += 1
            g = f // 11
            for kt in range(KT):
                P.pe(lambda e, kt=kt, f=f, fb=fb: e.matmul(
                    pg[fb], lhsT=wgu[:, kt, f * 128:(f + 1) * 128], rhs=xm[bi][:, kt, :],
                    start=(kt == 0), stop=(kt == KT - 1)), R=[b_wgu[0][g][kt], b_xm[bi]], W=[bpg[fb]])
            for kt in range(KT):
                P.pe(lambda e, kt=kt, f=f, fb=fb: e.matmul(
                    pu[fb], lhsT=wgu[:, kt, DFF + f * 128:DFF + (f + 1) * 128], rhs=xm[bi][:, kt, :],
                    start=(kt == 0), stop=(kt == KT - 1)), R=[b_wgu[1][g][kt], b_xm[bi]], W=[bpu[fb]])
            if deferred and f >= 1:
                deferred.pop(0)()
            P.act(lambda e, fb=fb: e.activation(out=sg[fb], in_=pg[fb], func=AF.Exp, scale=-1.0),
                  R=[bpg[fb]], W=[b_sg[fb]])
            P.act(lambda e, fb=fb: e.activation(out=sg[fb], in_=sg[fb], func=AF.Ln, bias=1.0),
                  R=[b_sg[fb]], W=[b_sg[fb]])
            P.act(lambda e, fb=fb: e.activation(out=sg[fb], in_=sg[fb], func=AF.Exp, scale=-1.0),
                  R=[b_sg[fb]], W=[b_sg[fb]])
            P.dve(lambda e, fb=fb: e.tensor_tensor(out=sg[fb], in0=pg[fb], in1=sg[fb], op=ALU.mult),
                  R=[bpg[fb], b_sg[fb]], W=[b_sg[fb]])
            P.dve(lambda e, fb=fb, f=f: e.tensor_tensor(out=h[:, f, :], in0=pu[fb], in1=sg[fb], op=ALU.mult),
                  R=[bpu[fb], b_sg[fb]], W=[b_h[f]])
        while deferred:
            deferred.pop(0)()
        if idx + 1 < len(tiles):
            modulate(idx + 1)
        pend = None
        for d in range(KT):
            db = cnt_d[0] % 2
            cnt_d[0] += 1
            for f in range(FT):
                P.pe(lambda e, f=f, d=d, db=db: e.matmul(
                    pd[db][:, :TN], lhsT=wdn[:, f, d * 128:(d + 1) * 128], rhs=h[:, f, :],
                    start=(f == 0), stop=(f == FT - 1)), R=[b_wdn[f], b_h[f]], W=[bpd[db]])
            if pend is not None:
                pend()

            def epi(d=d, db=db):
                tb = d % 2
                P.act(lambda e: e.activation(out=tmp[tb], in_=pd[db][:, :TN], func=AF.Copy,
                                             scale=k.mod1p[:, l, (j0 + 2) * 8 + d, c:c + 1]),
                      R=[bpd[db], k.b_mods], W=[b_tmp[tb]])
                P.dve(lambda e: e.scalar_tensor_tensor(out=rr[bi][:, d, :], in0=rr[bi][:, d, :], scalar=ALPHA,
                                                       in1=tmp[tb], op0=ALU.mult, op1=ALU.add),
                      R=[b_tmp[tb], b_rr[bi][d]], W=[b_rr[bi][d]])
                P.act(lambda e: e.activation(out=rq[tb], in_=rr[bi][:, d, :], func=AF.Square),
                      R=[b_rr[bi][d]], W=[b_rq[tb]])
                P.dve(lambda e: e.tensor_copy(out=rb[tb], in_=rr[bi][:, d, :]), R=[b_rr[bi][d]], W=[b_rb[tb]])
            epi()

            def stats(d=d):
                tb = d % 2
                P.pe(lambda e: e.matmul(pS1[:, :TN], lhsT=k.ones_bf, rhs=rb[tb],
                                        start=(d == 0), stop=(d == KT - 1)), R=[b_rb[tb], k.b_ones], W=[bS1])
                P.pe(lambda e: e.matmul(pS2[:, :TN], lhsT=k.ones_bf, rhs=rq[tb],
                                        start=(d == 0), stop=(d == KT - 1)), R=[b_rq[tb], k.b_ones], W=[bS2])
            pend = stats
        return [pend] + tail_steps(idx)

    load_x(0)
    deferred = []
    for idx in range(len(tiles)):
        deferred = tile_body(idx, deferred)
    while deferred:
        deferred.pop(0)()


def emit_proj_ln(k, l, lnrow, gate_j, NK, wout, b_wout, rhs_fn, rhs_bufs_fn, src, src_bufs, dst, dst_bufs, tiles,
                 pre_tile=None):
    nc, P, A = k.nc, k.P, k.A
    rr = [A.alloc([KT, TN], F32) for _ in range(2)]
    tmp = [A.alloc([TN], F32) for _ in range(2)]
    rb = [A.alloc([TN], BF16) for _ in range(2)]
    rq = [A.alloc([TN], BF16) for _ in range(2)]
    st_m, st_r, st_v = A.alloc([TN], F32), A.alloc([TN], F32), A.alloc([TN], F32)
    b_rr = [[Buf() for _ in range(KT)] for _ in range(2)]
    b_tmp, b_rb, b_rq = [Buf(), Buf()], [Buf(), Buf()], [Buf(), Buf()]
    b_st = Buf()
    pd, pS1, pS2 = k.ps[4:6], k.ps[6], k.ps[7]
    bpd, bS1, bS2 = k.psb[4:6], k.psb[6], k.psb[7]
    cnt_d = [0]

    def tile(idx):
        i = tiles[idx]
        bi = idx % 2
        c = 1 if i == 0 else 0
        if pre_tile is not None:
            pre_tile(i, idx)
        P.dma("sp", lambda e: e.dma_start(out=rr[bi], in_=src(i)), R=[src_bufs[i]], W=b_rr[bi], semkey="prr%d" % bi)
        pend = []
        for d in range(KT):
            db = cnt_d[0] % 2
            cnt_d[0] += 1
            tb = d % 2
            for f in range(NK):
                P.pe(lambda e, f=f, d=d, db=db: e.matmul(
                    pd[db][:, :TN], lhsT=wout[:, f, d * 128:(d + 1) * 128], rhs=rhs_fn(i, f, idx),
                    start=(f == 0), stop=(f == NK - 1)), R=[b_wout] + rhs_bufs_fn(i, f, idx), W=[bpd[db]])
            while pend:
                pend.pop(0)()
            P.act(lambda e, d=d, db=db, tb=tb: e.activation(out=tmp[tb], in_=pd[db][:, :TN], func=AF.Copy,
                                                          scale=k.mods[:, l, gate_j * 8 + d, c:c + 1]),
                  R=[bpd[db], k.b_mods], W=[b_tmp[tb]])
            P.dve(lambda e, d=d, tb=tb: e.scalar_tensor_tensor(out=rr[bi][:, d, :], in0=rr[bi][:, d, :], scalar=ALPHA,
                                                             in1=tmp[tb], op0=ALU.mult, op1=ALU.add),
                  R=[b_tmp[tb], b_rr[bi][d]], W=[b_rr[bi][d]])
            P.act(lambda e, d=d, tb=tb: e.activation(out=rq[tb], in_=rr[bi][:, d, :], func=AF.Square),
                  R=[b_rr[bi][d]], W=[b_rq[tb]])
            P.dve(lambda e, d=d, tb=tb: e.tensor_copy(out=rb[tb], in_=rr[bi][:, d, :]), R=[b_rr[bi][d]], W=[b_rb[tb]])

            def stats(d=d, tb=tb):
                P.pe(lambda e: e.matmul(pS1[:, :TN], lhsT=k.ones_bf, rhs=rb[tb],
                                        start=(d == 0), stop=(d == KT - 1)), R=[b_rb[tb], k.b_ones], W=[bS1])
                P.pe(lambda e: e.matmul(pS2[:, :TN], lhsT=k.ones_bf, rhs=rq[tb],
                                        start=(d == 0), stop=(d == KT - 1)), R=[b_rq[tb], k.b_ones], W=[bS2])
            pend.append(stats)
        while pend:
            pend.pop(0)()
        P.dve(lambda e: e.tensor_scalar_mul(out=st_m, in0=pS1[:, :TN], scalar1=1.0 / D), R=[bS1], W=[b_st])
        P.dve(lambda e: e.tensor_tensor(out=st_v, in0=st_m, in1=st_m, op=ALU.mult), R=[b_st], W=[b_st])
        P.dve(lambda e: e.scalar_tensor_tensor(out=st_v, in0=pS2[:, :TN], scalar=1.0 / D, in1=st_v,
                                               op0=ALU.mult, op1=ALU.subtract), R=[bS2, b_st], W=[b_st])
        P.act(lambda e: e.activation(out=st_r, in_=st_v, func=AF.Ln, bias=LN_EPS), R=[b_st], W=[b_st])
        P.act(lambda e: e.activation(out=st_r, in_=st_r, func=AF.Exp, scale=-0.5), R=[b_st], W=[b_st])
        for d in range(KT):
            P.dve(lambda e, d=d: e.tensor_tensor(out=rr[bi][:, d, :], in0=rr[bi][:, d, :], in1=st_m, op=ALU.subtract),
                  R=[b_st, b_rr[bi][d]], W=[b_rr[bi][d]])
            P.dve(lambda e, d=d: e.tensor_tensor(out=rr[bi][:, d, :], in0=rr[bi][:, d, :], in1=st_r, op=ALU.mult),
                  R=[b_st, b_rr[bi][d]], W=[b_rr[bi][d]])
            P.act(lambda e, d=d: e.activation(out=rr[bi][:, d, :], in_=rr[bi][:, d, :], func=AF.Identity,
                                             scale=k.lng[:, lnrow, d:d + 1], bias=k.lnb[:, lnrow, d:d + 1]),
                  R=[b_rr[bi][d], k.b_ln], W=[b_rr[bi][d]])
        P.dma("sp", lambda e: e.dma_start(out=dst(i), in_=rr[bi]), R=b_rr[bi], W=[dst_bufs[i]], semkey="pst%d" % bi)

    for idx in range(len(tiles)):
        tile(idx)


def emit_attn(k, l, src, src_bufs, dst, dst_bufs):
    nc, P, A = k.nc, k.P, k.A
    A.reset()
    P.small_on = False
    P.barrier(lambda e: e.memset(k.scr, 0.0))
    NQ = LLAT // 512
    win = A.alloc([KT, 1536], BF16)
    wout = A.alloc([KT, D], BF16)
    qT = A.alloc([8, LLAT], BF16)
    kT = A.alloc([2, T], BF16)
    V = A.alloc([T // 128, 256], BF16)
    Rm = A.alloc([128], BF16)
    gq = A.alloc([2], F32)
    b_win, b_wout, b_c = Buf(), Buf(), Buf()
    for kt in range(KT):
        P.dma("pool", lambda e, kt=kt: e.dma_start(out=win[:, kt, :], in_=k.attn_w_in[kt * 128:(kt + 1) * 128, :]),
              W=[b_win], semkey="awin") if kt == 0 else P.dma(
            "pool", lambda e, kt=kt: e.dma_start(out=win[:, kt, :], in_=k.attn_w_in[kt * 128:(kt + 1) * 128, :]),
            R=[], W=[Buf()], semkey="awin")
    for kt in range(KT):
        P.dma("pool", lambda e, kt=kt: e.dma_start(out=wout[:, kt, :], in_=k.attn_w_out[kt * 128:(kt + 1) * 128, :]),
              W=[b_wout] if kt == 0 else [Buf()], semkey="awout")
    P.dma("pool", lambda e: e.dma_start(out=Rm, in_=k.rotm), W=[b_c], semkey="ac")
    P.dma("sp", lambda e: e.dma_start(out=gq, in_=k.qk_gain), W=[b_c], semkey="ac")
    b_wl = Buf()
    P.act(lambda e: e.activation(out=k.scr[:, 0:1], in_=k.scr[:, 1:2], func=AF.Copy), R=[b_win, b_wout, b_c], W=[b_wl])
    mark_a = A.off
    xs = [A.alloc([KT, TN], F32) for _ in range(2)]
    xm = [A.alloc([KT, TN], BF16) for _ in range(2)]
    cs = [A.alloc([2, TN], F32) for _ in range(2)]
    qf = [A.alloc([TN], F32) for _ in range(2)]
    sq = [A.alloc([TN], BF16) for _ in range(2)]
    rs = [A.alloc([TN], F32) for _ in range(2)]
    qn = [A.alloc([TN], F32) for _ in range(2)]
    qnb = [A.alloc([TN], BF16) for _ in range(2)]
    t1 = [A.alloc([TN], F32) for _ in range(2)]
    b_xs, b_xm, b_cs = [Buf(), Buf()], [Buf(), Buf()], [Buf(), Buf()]
    b_qf, b_sq, b_rs, b_qn, b_qnb, b_t1 = ([Buf(), Buf()] for _ in range(6))
    b_qT = [[Buf() for _ in range(NTILE)] for _ in range(8)]
    b_kT = [[Buf() for _ in range(NTILE)] for _ in range(2)]
    b_V = [Buf() for _ in range(NTILE)]
    pp, pm, pr, pv = k.ps[0:2], k.ps[2:4], k.ps[4:6], k.ps[6:8]
    bpp, bpm, bpr, bpv = k.psb[0:2], k.psb[2:4], k.psb[4:6], k.psb[6:8]
    cn = [0]
    for i in range(NTILE):
        bi = i % 2
        c = 1 if i == 0 else 0
        P.dma("sp", lambda e, i=i, bi=bi: e.dma_start(out=xs[bi], in_=src(i)), R=[src_bufs[i]], W=[b_xs[bi]],
              semkey="axs%d" % bi)
        if i >= 1:
            P.dma("sp", lambda e, i=i, bi=bi: e.dma_start(out=cs[bi], in_=k.rope[:, :, (i - 1) * TN:i * TN]),
                  W=[b_cs[bi]], semkey="acs%d" % bi)
        for kt in range(KT):
            P.act(lambda e, kt=kt, bi=bi, c=c: e.activation(
                out=xm[bi][:, kt, :], in_=xs[bi][:, kt, :], func=AF.Identity,
                scale=k.mod1p[:, l, 4 * 8 + kt, c:c + 1], bias=k.mods[:, l, 3 * 8 + kt, c:c + 1]),
                R=[b_xs[bi], k.b_mods], W=[b_xm[bi]])
        fts = ([] if i == 0 else list(range(8))) + [8, 9]

        def ft_gen(ft, i=i, bi=bi):
            j = cn[0] % 2
            cn[0] += 1
            isq = ft < 8
            for kt in range(KT):
                P.pe(lambda e, kt=kt: e.matmul(
                    pp[j][:, :TN], lhsT=win[:, kt, ft * 128:(ft + 1) * 128], rhs=xm[bi][:, kt, :],
                    start=(kt == 0), stop=(kt == KT - 1)), R=[b_wl, b_xm[bi]], W=[bpp[j]])
            yield
            P.act(lambda e: e.activation(out=qf[j], in_=pp[j][:, :TN], func=AF.Copy), R=[bpp[j]], W=[b_qf[j]])
            P.act(lambda e: e.activation(out=sq[j], in_=pp[j][:, :TN], func=AF.Square), R=[bpp[j]], W=[b_sq[j]])
            yield
            P.pe(lambda e: e.matmul(pm[j][:, :TN], lhsT=k.ones_bf, rhs=sq[j], start=True, stop=True),
                 R=[b_sq[j], k.b_ones], W=[bpm[j]])
            yield
            P.act(lambda e: e.activation(out=rs[j], in_=pm[j][:, :TN], func=AF.Ln, scale=1.0 / 128, bias=RMS_EPS),
                  R=[bpm[j]], W=[b_rs[j]])
            P.act(lambda e: e.activation(out=rs[j], in_=rs[j], func=AF.Exp, scale=-0.5), R=[b_rs[j]], W=[b_rs[j]])
            yield
            gcol = gq[:, 0:1] if isq else gq[:, 1:2]
            if i == 0:
                dstv = kT[:, ft - 8, 0:TN]
                P.dve(lambda e: e.scalar_tensor_tensor(
                    out=dstv, in0=qf[j], scalar=gcol, in1=rs[j], op0=ALU.mult, op1=ALU.mult),
                    R=[b_qf[j], b_rs[j], b_wl], W=[b_kT[ft - 8][i]])
                return
            P.dve(lambda e: e.scalar_tensor_tensor(
                out=qn[j], in0=qf[j], scalar=gcol, in1=rs[j], op0=ALU.mult, op1=ALU.mult),
                R=[b_qf[j], b_rs[j], b_wl], W=[b_qn[j]])
            yield
            P.act(lambda e: e.activation(out=qnb[j], in_=qn[j], func=AF.Copy), R=[b_qn[j]], W=[b_qnb[j]])
            yield
            P.pe(lambda e: e.matmul(pr[j][:, :TN], lhsT=Rm, rhs=qnb[j], start=True, stop=True),
                 R=[b_qnb[j], b_wl], W=[bpr[j]])
            yield
            P.dve(lambda e: e.tensor_tensor(out=t1[j], in0=qn[j], in1=cs[bi][:, 0, :], op=ALU.mult),
                  R=[b_qn[j], b_cs[bi]], W=[b_t1[j]])
            P.dve(lambda e: e.tensor_tensor(out=qn[j], in0=pr[j][:, :TN], in1=cs[bi][:, 1, :], op=ALU.mult),
                  R=[bpr[j], b_cs[bi]], W=[b_qn[j]])
            if isq:
                dstv, db = qT[:, ft, (i - 1) * TN:i * TN], b_qT[ft][i]
            else:
                dstv, db = kT[:, ft - 8, i * TN:(i + 1) * TN], b_kT[ft - 8][i]
            P.dve(lambda e: e.tensor_tensor(out=dstv, in0=t1[j], in1=qn[j], op=ALU.add),
                  R=[b_t1[j], b_qn[j]], W=[db])
        run_window((ft_gen(ft) for ft in fts), 2)
        for blk in range(2):
            j = cn[0] % 2
            cn[0] += 1
            for kt in range(KT):
                P.pe(lambda e, kt=kt, blk=blk, j=j, bi=bi: e.matmul(
                    pv[j][:, :256], lhsT=xm[bi][:, kt, blk * 128:(blk + 1) * 128], rhs=win[:, kt, 1280:1536],
                    start=(kt == 0), stop=(kt == KT - 1)), R=[b_wl, b_xm[bi]], W=[bpv[j]])
            P.act(lambda e, j=j, i=i, blk=blk: e.activation(out=V[:, i * 2 + blk, :], in_=pv[j][:, :256], func=AF.Copy),
                  R=[bpv[j]], W=[b_V[i]])
    A.off = mark_a
    P.barrier(lambda e: e.memset(k.scr, 0.0))
    pt = [A.alloc([512], BF16) for _ in range(3)]
    rec = [A.alloc([512], F32) for _ in range(2)]
    b_pt, b_rec = [Buf() for _ in range(3)], [Buf(), Buf()]
    pS, pO, pZ = k.ps[0:3], k.ps[3:5], k.ps[5:7]
    bpS, bpO, bpZ = k.psb[0:3], k.psb[3:5], k.psb[5:7]
    b_o = [[Buf() for _ in range(NQ)] for _ in range(8)]
    SCALE = 128 ** -0.5
    SHIFT = 8.0
    NKB = T // 128
    items = []
    it = 0
    for h in range(8):
        for qt in range(NQ):
            for kb in range(NKB):
                items.append((h, h // 4, qt, kb, it % 2))
            it += 1
    LOOK = 2

    def emit_S(idx):
        h, g, qt, kb, ob = items[idx]
        sb = idx % 3
        qbufs = [b_qT[h][1 + 2 * qt], b_qT[h][2 + 2 * qt]]
        P.pe(lambda e: e.matmul(
            pS[sb][:, :], lhsT=kT[:, g, kb * 128:(kb + 1) * 128], rhs=qT[:, h, qt * 512:(qt + 1) * 512],
            start=True, stop=True), R=[b_kT[g][kb // 2]] + qbufs, W=[bpS[sb]])
        P.act(lambda e: e.activation(out=pt[sb], in_=pS[sb][:, :], func=AF.Exp, scale=SCALE, bias=-SHIFT),
              R=[bpS[sb]], W=[b_pt[sb]])

    def emit_OZ(idx):
        h, g, qt, kb, ob = items[idx]
        sb = idx % 3
        qbufs = [b_qT[h][1 + 2 * qt], b_qT[h][2 + 2 * qt]]
        P.pe(lambda e: e.matmul(
            pO[ob][:, :], lhsT=V[:, kb, g * 128:(g + 1) * 128], rhs=pt[sb],
            start=(kb == 0), stop=(kb == NKB - 1)), R=[b_V[kb // 2], b_pt[sb]], W=[bpO[ob]])
        P.pe(lambda e: e.matmul(
            pZ[ob][:, :], lhsT=k.ones_bf, rhs=pt[sb],
            start=(kb == 0), stop=(kb == NKB - 1)), R=[b_pt[sb], k.b_ones], W=[bpZ[ob]])
        if kb == NKB - 1:
            P.dve(lambda e: e.reciprocal(out=rec[ob], in_=pZ[ob][:, :]), R=[bpZ[ob]], W=[b_rec[ob]])
            P.dve(lambda e: e.tensor_tensor(
                out=qT[:, h, qt * 512:(qt + 1) * 512], in0=pO[ob][:, :], in1=rec[ob], op=ALU.mult),
                R=[bpO[ob], b_rec[ob]], W=qbufs + [b_o[h][qt]])

    for idx in range(len(items) + LOOK):
        if idx < len(items):
            emit_S(idx)
        if idx - LOOK >= 0:
            emit_OZ(idx - LOOK)
    emit_proj_ln(k, l, l * 3 + 1, 5, 8, wout, b_wl,
                 lambda i, f, idx: qT[:, f, (i - 1) * TN:i * TN],
                 lambda i, f, idx: [b_o[f][(i - 1) // 2]],
                 src, src_bufs, dst, dst_bufs, list(range(1, NTILE)))


_CACHE = {}

ALL_STAGES = ("mods", "l0f1", "l0mix", "l0f2", "l1f1", "l1mix", "l1f2")


def prep_inputs(inputs, stages=ALL_STAGES, ncores=8):
    f32 = lambda n: np.asarray(inputs[n], dtype=np.float32)
    pl = np.ascontiguousarray
    x, ctx, c, c_ctx = f32("x"), f32("ctx"), f32("c"), f32("c_ctx")
    shared = {}
    shared["ada_b"] = pl(f32("ada_b").reshape(DEPTH, 72, 128).transpose(2, 0, 1))
    shared["ln_g"] = pl(f32("ln_g").reshape(DEPTH * 3, KT, 128).transpose(2, 0, 1))
    shared["ln_b"] = pl(f32("ln_b").reshape(DEPTH * 3, KT, 128).transpose(2, 0, 1))
    for l in range(DEPTH):
        if "mods" in stages:
            shared["ada_w%d" % l] = pl(f32("ada_w")[l])
        for s in range(2):
            if ("l%df%d" % (l, s + 1)) in stages:
                shared["w_gu%d%d" % (l, s)] = pl(f32("ffn_w_gu")[l, s])
                shared["w_dn%d%d" % (l, s)] = pl(f32("ffn_w_down")[l, s])
    if "l1mix" in stages:
        shared["attn_w_in"] = pl(f32("attn_w_in")[0])
        shared["attn_w_out"] = pl(f32("attn_w_out")[0])
        shared["qk_gain"] = pl(np.stack([f32("attn_q_norm")[0], f32("attn_k_norm")[0]], axis=1))
        rot = np.zeros((128, 128), np.float32)
        for i in range(64):
            rot[2 * i + 1, 2 * i] = -1.0
            rot[2 * i, 2 * i + 1] = 1.0
        shared["rotm"] = rot
        rows = LLAT // 64
        rowp = np.repeat(np.arange(rows, dtype=np.float32), 64)
        colp = np.tile(np.arange(64, dtype=np.float32), rows)
        inv = (np.float32(10000.0) ** (-np.arange(32, dtype=np.float32) / np.float32(32))).astype(np.float32)
        ang = np.concatenate([rowp[:, None] * inv, colp[:, None] * inv], axis=-1).astype(np.float32)
        ang2 = np.repeat(ang, 2, axis=1).T
        shared["rope"] = pl(np.stack([np.cos(ang2), np.sin(ang2)], axis=1).astype(np.float32))
    if "l0mix" in stages:
        shared["even_w_in"] = pl(f32("even_w_in")[0])
        shared["even_w_out"] = pl(f32("even_w_out")[0])
        jj, ii = np.meshgrid(np.arange(128), np.arange(128), indexing="ij")
        gd = np.zeros((8, 128, 128), np.float32)
        gd[0] = np.eye(128)
        gd[1] = 1.0
        gd[2] = (jj <= ii)
        gd[3] = (jj >= ii)
        gd[4] = np.where(ii >= jj, 0.0, -30000.0)
        gd[5] = np.where(ii <= jj, 0.0, -30000.0)
        gd[6] = (ii > jj)
        gd[7] = (ii < jj)
        shared["gdnc"] = pl(gd.transpose(1, 0, 2))
        shared["cw5"] = pl(f32("even_qkv_conv")[0].reshape(5, 12, 128).transpose(2, 1, 0))
        shared["cw31"] = pl(f32("cf_dw_conv")[0].reshape(31, 4, 128).transpose(2, 1, 0))
        shared["cvec"] = pl(np.stack([f32("cf_dw_bias")[0], f32("cf_ln_g")[0], f32("cf_ln_b")[0]], axis=0)
                            .reshape(3, 4, 128).transpose(2, 1, 0))
        rc = np.stack([f32("gdn_dt_bias")[0].reshape(8), f32("gdn_a_log")[0].reshape(8)], axis=0)
        shared["rowc"] = pl(np.broadcast_to(rc[None], (128, 2, 8)))
        shared["gnorm"] = pl(np.broadcast_to(np.tile(f32("gdn_out_norm")[0], 4)[None], (128, 512)))
    maps = []
    for b in range(ncores):
        m = dict(shared)
        m["xT"] = pl(x[b].T)
        m["ctxT"] = pl(ctx[b].T)
        m["cond"] = pl(np.stack([c[b], c_ctx], axis=1).reshape(KT, 128, 2).transpose(1, 0, 2))
        maps.append(m)
    return maps


def _silu_inplace(P, ap_f32, b, psum_src=None, bsrc=None):
    pass


def emit_even(k, l, src_cols, src_bufs, src, dst, dst_bufs):
    nc, P, A = k.nc, k.P, k.A
    A.reset()
    P.small_on = False
    P.barrier(lambda e: e.memset(k.scr, 0.0))
    NCH = T // 128
    HALO = 15
    NW = TN + 2 * HALO
    QN = A.alloc([4, T], BF16)
    KN = A.alloc([4, T], BF16)
    VV = A.alloc([4, T], BF16)
    GT = A.alloc([NCH, 16], F32)
    cst = A.alloc([9, 128], F32)
    identb = A.alloc([128], BF16)
    cw5 = A.alloc([12, 5], F32)
    cw31 = A.alloc([4, 31], F32)
    cvec = A.alloc([4, 3], F32)
    rowc = A.alloc([3, 8], F32)
    gnorm = A.alloc([512], F32)
    b_cst = Buf()
    P.dma("sp", lambda e: e.dma_start(out=cst[:, 0:8, :], in_=k.gdnc), W=[b_cst], semkey="ec")
    P.dma("sp", lambda e: e.dma_start(out=cw5, in_=k.cw5), W=[b_cst], semkey="ec")
    P.dma("sp", lambda e: e.dma_start(out=cw31, in_=k.cw31), W=[b_cst], semkey="ec")
    P.dma("sp", lambda e: e.dma_start(out=cvec, in_=k.cvec), W=[b_cst], semkey="ec")
    P.dma("sp", lambda e: e.dma_start(out=rowc[:, 0:2, :], in_=k.rowc), W=[b_cst], semkey="ec")
    P.dma("sp", lambda e: e.dma_start(out=gnorm, in_=k.gnorm), W=[b_cst], semkey="ec")
    P.small_on = True
    P.act(lambda e: e.activation(out=rowc[:, 2, :], in_=rowc[:, 1, :], func=AF.Exp), R=[b_cst], W=[b_cst])
    P.dve(lambda e: e.tensor_scalar_mul(out=rowc[:, 2, :], in0=rowc[:, 2, :], scalar1=-1.0), R=[b_cst], W=[b_cst])
    P.dve(lambda e: e.tensor_copy(out=identb, in_=cst[:, 0, :]), R=[b_cst], W=[b_cst])
    P.small_on = False
    ident, onesF, UT, LT = cst[:, 0, :], cst[:, 1, :], cst[:, 2, :], cst[:, 3, :]
    negm = [cst[:, 4, :], cst[:, 5, :]]
    smask = [cst[:, 6, :], cst[:, 7, :]]
    A.set_mark2 = A.off
    win = A.alloc([KT, 3088], BF16)
    b_win = Buf()
    for kt in range(KT):
        P.dma("pool", lambda e, kt=kt: e.dma_start(out=win[:, kt, :], in_=k.even_w_in[kt * 128:(kt + 1) * 128, :]),
              W=[Buf()], semkey="ewin")
    b_wl = Buf()
    P.act(lambda e: e.activation(out=k.scr[:, 0:1], in_=k.scr[:, 1:2], func=AF.Copy), R=[b_cst], W=[b_wl])
    k.P.ops[-1].dw["ewin"] = k.P.dcnt["ewin"]
    xs = [A.alloc([KT, NW], F32)] * 2
    xm = [A.alloc([KT, NW], BF16)] * 2
    acc = [A.alloc([TN], F32) for _ in range(2)]
    sg = [A.alloc([NW], F32) for _ in range(2)]
    sqb = [A.alloc([TN], BF16) for _ in range(2)]
    rs = [A.alloc([TN], F32) for _ in range(2)]
    uu = [A.alloc([NW], F32) for _ in range(2)]
    yy = A.alloc([4, TN], F32)
    yb = [A.alloc([TN], BF16) for _ in range(2)]
    yq = [A.alloc([TN], BF16) for _ in range(2)]
    st_m, st_r, st_v = A.alloc([TN], F32), A.alloc([TN], F32), A.alloc([TN], F32)
    cft = [A.alloc([4, TN], BF16)] * 2
    zs = [A.alloc([512], F32)] * 2
    zb = [A.alloc([512], BF16)] * 2
    gt1 = [A.alloc([16], F32) for _ in range(2)]
    b_xs, b_xm = [Buf()] * 2, [Buf()] * 2
    b_acc, b_sg, b_sqb, b_rs, b_uu = ([Buf(), Buf()] for _ in range(5))
    b_yy = [Buf() for _ in range(4)]
    b_yb, b_yq, b_gt1 = ([Buf(), Buf()] for _ in range(3))
    b_cft, b_zs, b_zb = [Buf()] * 2, [Buf()] * 2, [Buf()] * 2
    b_st = Buf()
    b_QN = [[Buf() for _ in range(NCH)] for _ in range(4)]
    b_KN = [[Buf() for _ in range(NCH)] for _ in range(4)]
    b_VV = [[Buf() for _ in range(NCH)] for _ in range(4)]
    b_GT = [Buf() for _ in range(NCH)]
    b_CFd = [Buf() for _ in range(NTILE)]
    b_ZSd = [Buf() for _ in range(NCH)]
    pp, pq, pm = k.ps[0:2], k.ps[2:4], k.ps[4:6]
    bpp, bpq, bpm = k.psb[0:2], k.psb[2:4], k.psb[4:6]
    pS1, pS2 = k.ps[6], k.ps[7]
    bS1, bS2 = k.psb[6], k.psb[7]
    cn = [0]

    def sigmoid_from(psrc, bsrc, dstap, bdst, n):
        P.act(lambda e: e.activation(out=dstap, in_=psrc, func=AF.Exp, scale=-1.0), R=[bsrc], W=[bdst])
        P.act(lambda e: e.activation(out=dstap, in_=dstap, func=AF.Ln, bias=1.0), R=[bdst], W=[bdst])
        P.act(lambda e: e.activation(out=dstap, in_=dstap, func=AF.Exp, scale=-1.0), R=[bdst], W=[bdst])

    for i in range(NTILE):
        bi = i % 2
        c = 1 if i == 0 else 0
        t0 = i * TN
        s0, s1 = (0, LCTX) if i == 0 else (LCTX, T)
        lo = max(0, HALO - (t0 - s0))
        hi = min(NW, HALO + (s1 - t0))
        g0, g1 = t0 - HALO + lo, t0 - HALO + hi
        P.dma("sp", lambda e, bi=bi, lo=lo, hi=hi, g0=g0, g1=g1: e.dma_start(
            out=xs[bi][:, :, lo:hi], in_=src_cols(g0, g1)), R=[src_bufs[j] for j in range(max(0, i - 1), min(NTILE, i + 2))],
            W=[b_xs[bi]], semkey="exs%d" % bi)
        if lo > 0 or hi < NW:
            P.dve(lambda e, bi=bi: e.memset(xm[bi], 0.0), W=[b_xm[bi]])
        for kt in range(KT):
            P.act(lambda e, kt=kt, bi=bi, c=c, lo=lo, hi=hi: e.activation(
                out=xm[bi][:, kt, lo:hi], in_=xs[bi][:, kt, lo:hi], func=AF.Identity,
                scale=k.mod1p[:, l, 4 * 8 + kt, c:c + 1], bias=k.mods[:, l, 3 * 8 + kt, c:c + 1]),
                R=[b_xs[bi], k.b_mods], W=[b_xm[bi]])
        def qkv_gen(ft, i=i, bi=bi, t0=t0):
            j = cn[0] % 2
            cn[0] += 1
            for kt in range(KT):
                P.pe(lambda e, kt=kt: e.matmul(
                    pp[j][:, :260], lhsT=win[:, kt, ft * 128:(ft + 1) * 128], rhs=xm[bi][:, kt, 13:273],
                    start=(kt == 0), stop=(kt == KT - 1)), R=[b_wl, b_xm[bi]], W=[bpp[j]])
            yield
            P.dve(lambda e: e.tensor_scalar_mul(out=acc[j], in0=pp[j][:, 0:TN], scalar1=cw5[:, ft, 0:1]),
                  R=[bpp[j], b_cst], W=[b_acc[j]])
            for tap in range(1, 5):
                P.dve(lambda e, tap=tap: e.scalar_tensor_tensor(
                    out=acc[j], in0=pp[j][:, tap:tap + TN], scalar=cw5[:, ft, tap:tap + 1], in1=acc[j],
                    op0=ALU.mult, op1=ALU.add), R=[bpp[j], b_cst, b_acc[j]], W=[b_acc[j]])
            yield
            sigmoid_from(acc[j], b_acc[j], sg[j][:, :TN], b_sg[j], TN)
            yield
            P.dve(lambda e: e.tensor_tensor(out=acc[j], in0=acc[j], in1=sg[j][:, :TN], op=ALU.mult),
                  R=[b_sg[j], b_acc[j]], W=[b_acc[j]])
            hh = ft % 4
            if ft >= 8:
                P.dve(lambda e: e.tensor_copy(out=VV[:, hh, t0:t0 + TN], in_=acc[j]),
                      R=[b_acc[j]], W=[b_VV[hh][2 * i], b_VV[hh][2 * i + 1]])
                return
            yield
            P.act(lambda e: e.activation(out=sqb[j], in_=acc[j], func=AF.Square), R=[b_acc[j]], W=[b_sqb[j]])
            yield
            P.pe(lambda e: e.matmul(pm[j][:, :TN], lhsT=k.ones_bf, rhs=sqb[j], start=True, stop=True),
                 R=[b_sqb[j], k.b_ones], W=[bpm[j]])
            yield
            P.act(lambda e: e.activation(out=rs[j], in_=pm[j][:, :TN], func=AF.Ln, bias=RMS_EPS),
                  R=[bpm[j]], W=[b_rs[j]])
            P.act(lambda e: e.activation(out=rs[j], in_=rs[j], func=AF.Exp, scale=-0.5), R=[b_rs[j]], W=[b_rs[j]])
            yield
            if ft < 4:
                P.dve(lambda e: e.scalar_tensor_tensor(
                    out=QN[:, hh, t0:t0 + TN], in0=acc[j], scalar=128 ** -0.5, in1=rs[j], op0=ALU.mult, op1=ALU.mult),
                    R=[b_acc[j], b_rs[j]], W=[b_QN[hh][2 * i], b_QN[hh][2 * i + 1]])
            else:
                P.dve(lambda e: e.tensor_tensor(
                    out=KN[:, hh, t0:t0 + TN], in0=acc[j], in1=rs[j], op=ALU.mult),
                    R=[b_acc[j], b_rs[j]], W=[b_KN[hh][2 * i], b_KN[hh][2 * i + 1]])
        run_window((qkv_gen(ft) for ft in range(12)), 2)
        def cf_gen(ct, bi=bi):
            j = cn[0] % 2
            cn[0] += 1
            for kt in range(KT):
                P.pe(lambda e, kt=kt: e.matmul(
                    pp[j][:, :NW], lhsT=win[:, kt, 2064 + ct * 128:2064 + (ct + 1) * 128], rhs=xm[bi][:, kt, :],
                    start=(kt == 0), stop=(kt == KT - 1)), R=[b_wl, b_xm[bi]], W=[bpp[j]])
            for kt in range(KT):
                P.pe(lambda e, kt=kt: e.matmul(
                    pq[j][:, :NW], lhsT=win[:, kt, 2576 + ct * 128:2576 + (ct + 1) * 128], rhs=xm[bi][:, kt, :],
                    start=(kt == 0), stop=(kt == KT - 1)), R=[b_wl, b_xm[bi]], W=[bpq[j]])
            yield
            sigmoid_from(pq[j][:, :NW], bpq[j], sg[j], b_sg[j], NW)
            yield
            P.dve(lambda e: e.tensor_tensor(out=uu[j], in0=pp[j][:, :NW], in1=sg[j], op=ALU.mult),
                  R=[bpp[j], b_sg[j]], W=[b_uu[j]])
            P.dve(lambda e: e.tensor_scalar(out=yy[:, ct, :], in0=uu[j][:, 0:TN], scalar1=cw31[:, ct, 0:1],
                                            scalar2=cvec[:, ct, 0:1], op0=ALU.mult, op1=ALU.add),
                  R=[b_uu[j], b_cst], W=[b_yy[ct]])
            for tap in range(1, 31):
                P.dve(lambda e, tap=tap: e.scalar_tensor_tensor(
                    out=yy[:, ct, :], in0=uu[j][:, tap:tap + TN], scalar=cw31[:, ct, tap:tap + 1], in1=yy[:, ct, :],
                    op0=ALU.mult, op1=ALU.add), R=[b_uu[j], b_cst, b_yy[ct]], W=[b_yy[ct]])
            yield
            P.act(lambda e: e.activation(out=yq[j], in_=yy[:, ct, :], func=AF.Square), R=[b_yy[ct]], W=[b_yq[j]])
            P.dve(lambda e: e.tensor_copy(out=yb[j], in_=yy[:, ct, :]), R=[b_yy[ct]], W=[b_yb[j]])
            yield
            P.pe(lambda e: e.matmul(pS1[:, :TN], lhsT=k.ones_bf, rhs=yb[j], start=(ct == 0), stop=(ct == 3)),
                 R=[b_yb[j], k.b_ones], W=[bS1])
            P.pe(lambda e: e.matmul(pS2[:, :TN], lhsT=k.ones_bf, rhs=yq[j], start=(ct == 0), stop=(ct == 3)),
                 R=[b_yq[j], k.b_ones], W=[bS2])
        run_window((cf_gen(ct) for ct in range(4)), 2)
        P.dve(lambda e: e.tensor_scalar_mul(out=st_m, in0=pS1[:, :TN], scalar1=1.0 / 512), R=[bS1], W=[b_st])
        P.dve(lambda e: e.tensor_tensor(out=st_v, in0=st_m, in1=st_m, op=ALU.mult), R=[b_st], W=[b_st])
        P.dve(lambda e: e.scalar_tensor_tensor(out=st_v, in0=pS2[:, :TN], scalar=1.0 / 512, in1=st_v,
                                               op0=ALU.mult, op1=ALU.subtract), R=[bS2, b_st], W=[b_st])
        P.act(lambda e: e.activation(out=st_r, in_=st_v, func=AF.Ln, bias=LN_EPS), R=[b_st], W=[b_st])
        P.act(lambda e: e.activation(out=st_r, in_=st_r, func=AF.Exp, scale=-0.5), R=[b_st], W=[b_st])
        for ct in range(4):
            j = ct % 2
            P.dve(lambda e, ct=ct: e.tensor_tensor(out=yy[:, ct, :], in0=yy[:, ct, :], in1=st_m, op=ALU.subtract),
                  R=[b_st, b_yy[ct]], W=[b_yy[ct]])
            P.dve(lambda e, ct=ct: e.tensor_tensor(out=yy[:, ct, :], in0=yy[:, ct, :], in1=st_r, op=ALU.mult),
                  R=[b_st, b_yy[ct]], W=[b_yy[ct]])
            P.act(lambda e, ct=ct: e.activation(out=yy[:, ct, :], in_=yy[:, ct, :], func=AF.Identity,
                                              scale=cvec[:, ct, 1:2], bias=cvec[:, ct, 2:3]),
                  R=[b_yy[ct], b_cst], W=[b_yy[ct]])
            sigmoid_from(yy[:, ct, :], b_yy[ct], sg[j][:, :TN], b_sg[j], TN)
            P.dve(lambda e, ct=ct, j=j, bi=bi: e.tensor_tensor(out=cft[bi][:, ct, :], in0=yy[:, ct, :], in1=sg[j][:, :TN],
                                                             op=ALU.mult), R=[b_yy[ct], b_sg[j]], W=[b_cft[bi]])
        P.dma("sp", lambda e, bi=bi, t0=t0: e.dma_start(
            out=k.CFd[:, t0:t0 + TN].rearrange("(ct p) t -> p ct t", p=128), in_=cft[bi]),
            R=[b_cft[bi]], W=[b_CFd[i]], semkey="ecf%d" % bi)
        for blk in range(2):
            j = cn[0] % 2
            cn[0] += 1
            ch = 2 * i + blk
            c0 = HALO + blk * 128
            for kt in range(KT):
                P.pe(lambda e, kt=kt, j=j, bi=bi, c0=c0: e.matmul(
                    pp[j][:, :512], lhsT=xm[bi][:, kt, c0:c0 + 128], rhs=win[:, kt, 1536:2048],
                    start=(kt == 0), stop=(kt == KT - 1)), R=[b_wl, b_xm[bi]], W=[bpp[j]])
            for kt in range(KT):
                P.pe(lambda e, kt=kt, j=j, bi=bi, c0=c0: e.matmul(
                    pq[j][:, :16], lhsT=xm[bi][:, kt, c0:c0 + 128], rhs=win[:, kt, 2048:2064],
                    start=(kt == 0), stop=(kt == KT - 1)), R=[b_wl, b_xm[bi]], W=[bpq[j]])
            sigmoid_from(pp[j][:, :512], bpp[j], zs[j], b_zs[j], 512)
            P.dve(lambda e, j=j: e.tensor_tensor(out=zb[j], in0=pp[j][:, :512], in1=zs[j], op=ALU.mult),
                  R=[bpp[j], b_zs[j]], W=[b_zb[j]])
            P.dma("sp", lambda e, j=j, ch=ch: e.dma_start(out=k.ZSd[ch * 128:(ch + 1) * 128, :], in_=zb[j]),
                  R=[b_zb[j]], W=[b_ZSd[ch]], semkey="ezs%d" % j)
            P.small_on = True
            P.dve(lambda e, j=j: e.tensor_tensor(out=gt1[j][:, 0:8], in0=pq[j][:, 0:8], in1=rowc[:, 0, :], op=ALU.add),
                  R=[bpq[j], b_cst], W=[b_gt1[j]])
            P.act(lambda e, j=j: e.activation(out=gt1[j][:, 0:8], in_=gt1[j][:, 0:8], func=AF.Exp), R=[b_gt1[j]], W=[b_gt1[j]])
            P.act(lambda e, j=j: e.activation(out=gt1[j][:, 0:8], in_=gt1[j][:, 0:8], func=AF.Ln, bias=1.0),
                  R=[b_gt1[j]], W=[b_gt1[j]])
            P.dve(lambda e, j=j, ch=ch: e.tensor_tensor(out=GT[:, ch, 0:8], in0=gt1[j][:, 0:8], in1=rowc[:, 2, :], op=ALU.mult),
                  R=[b_gt1[j], b_cst], W=[b_GT[ch]])
            P.act(lambda e, j=j: e.activation(out=gt1[j][:, 8:16], in_=pq[j][:, 8:16], func=AF.Exp, scale=-1.0),
                  R=[bpq[j]], W=[b_gt1[j]])
            P.act(lambda e, j=j: e.activation(out=gt1[j][:, 8:16], in_=gt1[j][:, 8:16], func=AF.Ln, bias=1.0),
                  R=[b_gt1[j]], W=[b_gt1[j]])
            P.act(lambda e, j=j, ch=ch: e.activation(out=GT[:, ch, 8:16], in_=gt1[j][:, 8:16], func=AF.Exp, scale=-1.0),
                  R=[b_gt1[j]], W=[b_GT[ch]])
            P.small_on = False
    A.off = A.set_mark2
    P.barrier(lambda e: e.memset(k.scr, 0.0))
    insts = [(d, h) for d in range(2) for h in range(4)]
    order = [list(range(NCH)), [1, 0] + list(range(NCH - 1, 1, -1))]

    def mk(n, shape, dt):
        return [A.alloc(shape, dt) for _ in range(n)]
    S_ = mk(8, [128], F32)
    Sb = mk(8, [128], BF16)
    cols = mk(8, [16], F32)
    ktl = mk(8, [128], BF16)
    vtk = mk(8, [128], F32)
    dg = mk(8, [128], F32)
    DT = mk(8, [128], F32)
    AT = mk(8, [128], BF16)
    WA = mk(8, [128], F32)
    WB = mk(8, [128], F32)
    PP = mk(8, [128], F32)
    r2 = mk(8, [128], F32)
    vn = mk(8, [128], BF16)
    o1 = mk(8, [128], F32)
    oo = mk(8, [128], F32)
    bI = [Buf() for _ in range(8)]
    bO = [Buf() for _ in range(8)]
    b_Od = [[Buf() for _ in range(NCH)] for _ in range(2)]
    for n in range(8):
        P.dve(lambda e, n=n: e.memset(S_[n], 0.0), W=[bI[n]])
        P.dve(lambda e, n=n: e.memset(Sb[n], 0.0), W=[bI[n]])
    bank = k.ps
    bB = k.psb

    def reg(n, r):
        return bank[n][:, r * 128:(r + 1) * 128]

    def regb(n):
        return bank[n][:, 384:448].bitcast(BF16)

    def regb2(n):
        return bank[n][:, 448:512].bitcast(BF16)

    def inst_gen(n, d, h, ch, tk):
        gcol = GT[:, ch, d * 4 + h:d * 4 + h + 1]
        bcol = GT[:, ch, 8 + d * 4 + h:8 + d * 4 + h + 1]
        cl = cols[n]
        kc = KN[:, h, tk:tk + 128]
        qc = QN[:, h, tk:tk + 128]
        P.pe(lambda e: e.matmul(reg(n, 0)[:, 0:1], lhsT=(UT if d == 0 else LT), rhs=gcol, start=True, stop=True),
             R=[b_GT[ch], b_cst], W=[bB[n]])
        P.pe(lambda e: e.matmul(reg(n, 0)[:, 1:2], lhsT=onesF, rhs=gcol, start=True, stop=True),
             R=[b_GT[ch], b_cst], W=[bB[n]])
        P.pe(lambda e: e.transpose(regb(n), kc, identb), R=[b_KN[h][ch], b_cst], W=[bB[n]])
        P.pe(lambda e: e.transpose(regb2(n), VV[:, h, tk:tk + 128], identb), R=[b_VV[h][ch], b_cst], W=[bB[n]])
        P.pe(lambda e: e.matmul(reg(n, 1), lhsT=kc, rhs=kc, start=True, stop=True), R=[b_KN[h][ch]], W=[bB[n]])
        P.pe(lambda e: e.matmul(reg(n, 2), lhsT=kc, rhs=qc, start=True, stop=True),
             R=[b_KN[h][ch], b_QN[h][ch]], W=[bB[n]])
        yield
        P.dve(lambda e: e.tensor_copy(out=cl[:, 0:2], in_=reg(n, 0)[:, 0:2]), R=[bB[n]], W=[bI[n]], small=True)
        P.dve(lambda e: e.tensor_scalar_mul(out=cl[:, 2:3], in0=cl[:, 0:1], scalar1=-1.0), R=[bI[n]], W=[bI[n]], small=True)
        P.act(lambda e: e.activation(out=vtk[n], in_=regb2(n), func=AF.Copy), R=[bB[n]], W=[bI[n]])
        yield
        P.act(lambda e: e.activation(out=cl[:, 3:4], in_=cl[:, 0:1], func=AF.Exp), R=[bI[n]], W=[bI[n]], small=True)
        P.act(lambda e: e.activation(out=cl[:, 4:5], in_=cl[:, 0:1], func=AF.Exp, scale=-1.0, bias=cl[:, 1:2]),
              R=[bI[n]], W=[bI[n]], small=True)
        P.act(lambda e: e.activation(out=cl[:, 5:6], in_=cl[:, 1:2], func=AF.Exp), R=[bI[n]], W=[bI[n]], small=True)
        yield
        P.dve(lambda e: e.scalar_tensor_tensor(out=cl[:, 6:7], in0=bcol, scalar=-1.0, in1=cl[:, 3:4],
                                               op0=ALU.mult, op1=ALU.mult), R=[bI[n], b_GT[ch]], W=[bI[n]], small=True)
        P.dve(lambda e: e.tensor_scalar_mul(out=cl[:, 7:8], in0=cl[:, 3:4], scalar1=-1.0), R=[bI[n]], W=[bI[n]], small=True)
        P.dve(lambda e: e.tensor_scalar_mul(out=ktl[n], in0=regb(n), scalar1=cl[:, 4:5]), R=[bB[n], bI[n]], W=[bI[n]],
              small=True)
        P.dve(lambda e: e.tensor_scalar_mul(out=dg[n], in0=ident, scalar1=cl[:, 0:1]), R=[bI[n], b_cst], W=[bI[n]],
              small=True)
        yield
        P.pe(lambda e: e.matmul(reg(n, 0), lhsT=onesF, rhs=dg[n], start=True, stop=True), R=[bI[n], b_cst], W=[bB[n]])
        yield
        P.dve(lambda e: e.tensor_tensor(out=DT[n], in0=reg(n, 0), in1=negm[d], op=ALU.add), R=[bB[n], b_cst], W=[bI[n]])
        yield
        P.act(lambda e: e.activation(out=DT[n], in_=DT[n], func=AF.Exp, bias=cl[:, 2:3]), R=[bI[n]], W=[bI[n]])
        yield
        P.dve(lambda e: e.tensor_tensor(out=AT[n], in0=reg(n, 2), in1=DT[n], op=ALU.mult), R=[bB[n], bI[n]], W=[bI[n]])
        P.dve(lambda e: e.tensor_tensor(out=WA[n], in0=reg(n, 1), in1=DT[n], op=ALU.mult), R=[bB[n], bI[n]], W=[bI[n]])
        P.dve(lambda e: e.scalar_tensor_tensor(out=WA[n], in0=WA[n], scalar=bcol, in1=smask[d],
                                               op0=ALU.mult, op1=ALU.mult), R=[bI[n], b_GT[ch], b_cst], W=[bI[n]])
        P.dve(lambda e: e.tensor_tensor(out=PP[n], in0=ident, in1=WA[n], op=ALU.subtract), R=[bI[n], b_cst], W=[bI[n]])
        yield
        P.pe(lambda e: e.transpose(reg(n, 3), WA[n], ident), R=[bI[n], b_cst], W=[bB[n]])
        yield
        P.act(lambda e: e.activation(out=WB[n], in_=reg(n, 3), func=AF.Copy), R=[bB[n]], W=[bI[n]])
        yield
        for lev in range(6):
            last = lev == 5
            P.pe(lambda e: e.matmul(reg(n, 1), lhsT=WA[n], rhs=WB[n], start=True, stop=True), R=[bI[n]], W=[bB[n]])
            if not last:
                P.pe(lambda e: e.matmul(reg(n, 2), lhsT=WB[n], rhs=WA[n], start=True, stop=True), R=[bI[n]], W=[bB[n]])
            yield
            P.act(lambda e: e.activation(out=WB[n], in_=reg(n, 1), func=AF.Copy), R=[bB[n]], W=[bI[n]])
            if not last:
                P.dve(lambda e: e.tensor_copy(out=WA[n], in_=reg(n, 2)), R=[bB[n]], W=[bI[n]])
            yield
            P.pe(lambda e: e.matmul(reg(n, 3), lhsT=WB[n], rhs=PP[n], start=True, stop=True), R=[bI[n]], W=[bB[n]])
            yield
            P.dve(lambda e: e.tensor_tensor(out=PP[n], in0=reg(n, 3), in1=PP[n], op=ALU.add), R=[bB[n], bI[n]], W=[bI[n]])
            yield
        P.pe(lambda e: e.matmul(reg(n, 0), lhsT=kc, rhs=Sb[n], start=True, stop=True), R=[b_KN[h][ch], bI[n]], W=[bB[n]])
        P.pe(lambda e: e.matmul(reg(n, 2), lhsT=qc, rhs=Sb[n], start=True, stop=True), R=[b_QN[h][ch], bI[n]], W=[bB[n]])
        yield
        P.dve(lambda e: e.scalar_tensor_tensor(out=r2[n], in0=reg(n, 0), scalar=cl[:, 7:8], in1=vtk[n],
                                               op0=ALU.mult, op1=ALU.add), R=[bB[n], bI[n]], W=[bI[n]])
        yield
        P.pe(lambda e: e.matmul(reg(n, 1), lhsT=PP[n], rhs=r2[n], start=True, stop=True), R=[bI[n]], W=[bB[n]])
        yield
        P.dve(lambda e: e.tensor_scalar_mul(out=vn[n], in0=reg(n, 1), scalar1=bcol), R=[bB[n], b_GT[ch]], W=[bI[n]])
        yield
        P.pe(lambda e: e.matmul(reg(n, 3), lhsT=AT[n], rhs=vn[n], start=True, stop=True), R=[bI[n]], W=[bB[n]])
        P.pe(lambda e: e.matmul(reg(n, 0), lhsT=ktl[n], rhs=vn[n], start=True, stop=True), R=[bI[n]], W=[bB[n]])
        yield
        P.act(lambda e: e.activation(out=o1[n], in_=reg(n, 3), func=AF.Copy), R=[bB[n]], W=[bI[n]])
        P.dve(lambda e: e.scalar_tensor_tensor(out=S_[n], in0=S_[n], scalar=cl[:, 5:6], in1=reg(n, 0),
                                               op0=ALU.mult, op1=ALU.add), R=[bB[n], bI[n]], W=[bI[n]])
        yield
        P.dve(lambda e: e.scalar_tensor_tensor(out=oo[n], in0=reg(n, 2), scalar=cl[:, 3:4], in1=o1[n],
                                               op0=ALU.mult, op1=ALU.add), R=[bB[n], bI[n]], W=[bO[n]])
        P.act(lambda e: e.activation(out=Sb[n], in_=S_[n], func=AF.Copy), R=[bI[n]], W=[bI[n]])
        yield
        P.dma("sp", lambda e: e.dma_start(out=k.Od[d][tk:tk + 128, h * 128:(h + 1) * 128], in_=oo[n]),
              R=[bO[n]], W=[b_Od[d][ch]], semkey="eo%d" % n)

    for step in range(NCH):
        gens = [inst_gen(n, d, h, order[d][step], order[d][step] * 128) for n, (d, h) in enumerate(insts)]
        alive = True
        while alive:
            alive = False
            for g_ in gens:
                try:
                    next(g_)
                    alive = True
                except StopIteration:
                    pass
    A.off = A.set_mark2
    P.barrier(lambda e: e.memset(k.scr, 0.0))
    wout = A.alloc([KT, D], BF16)
    for kt in range(KT):
        P.dma("pool", lambda e, kt=kt: e.dma_start(out=wout[:, kt, :], in_=k.even_w_out[kt * 128:(kt + 1) * 128, :]),
              W=[Buf()], semkey="ewout")
    b_wo = Buf()
    P.act(lambda e: e.activation(out=k.scr[:, 0:1], in_=k.scr[:, 1:2], func=AF.Copy), W=[b_wo])
    k.P.ops[-1].dw["ewout"] = k.P.dcnt["ewout"]
    of_ = [A.alloc([512], F32) for _ in range(2)]
    ob_ = [A.alloc([512], F32) for _ in range(2)]
    zl = [A.alloc([512], BF16) for _ in range(2)]
    sqj = A.alloc([512], F32)
    ss = [A.alloc([4], F32) for _ in range(2)]
    gmb = [A.alloc([512], BF16) for _ in range(2)]
    gTt = [A.alloc([4, TN], BF16) for _ in range(2)]
    cfl = [A.alloc([4, TN], BF16) for _ in range(2)]
    b_of, b_ob, b_zl, b_ss, b_gmb, b_gT, b_cfl = ([Buf(), Buf()] for _ in range(7))
    b_sqj = Buf()
    ptr = k.ps[0:2]
    bptr = k.psb[0:2]
    cnm = [0]

    def pre_tile(i, idx):
        bi = idx % 2
        for blk in range(2):
            j = cnm[0] % 2
            cnm[0] += 1
            ch = 2 * i + blk
            P.dma("sp", lambda e, j=j, ch=ch: e.dma_start(out=of_[j], in_=k.Od[0][ch * 128:(ch + 1) * 128, :]),
                  R=[b_Od[0][ch]], W=[b_of[j]], semkey="mof%d" % j)
            P.dma("sp", lambda e, j=j, ch=ch: e.dma_start(out=ob_[j], in_=k.Od[1][ch * 128:(ch + 1) * 128, :]),
                  R=[b_Od[1][ch]], W=[b_ob[j]], semkey="mob%d" % j)
            P.dma("sp", lambda e, j=j, ch=ch: e.dma_start(out=zl[j], in_=k.ZSd[ch * 128:(ch + 1) * 128, :]),
                  R=[b_ZSd[ch]], W=[b_zl[j]], semkey="mzl%d" % j)
            P.dve(lambda e, j=j: e.tensor_tensor(out=of_[j], in0=of_[j], in1=ob_[j], op=ALU.add),
                  R=[b_of[j], b_ob[j]], W=[b_of[j]])
            P.small_on = True
            P.dve(lambda e, j=j: e.memset(ss[j], 0.0), W=[b_ss[j]])
            for hh in range(4):
                P.act(lambda e, j=j, hh=hh: e.activation(out=sqj[:, hh * 128:(hh + 1) * 128],
                                                        in_=of_[j][:, hh * 128:(hh + 1) * 128], func=AF.Square,
                                                        accum_out=ss[j][:, hh:hh + 1]),
                      R=[b_of[j]], W=[b_sqj, b_ss[j]])
            P.act(lambda e, j=j: e.activation(out=ss[j], in_=ss[j], func=AF.Ln, scale=1.0 / 128, bias=RMS_EPS),
                  R=[b_ss[j]], W=[b_ss[j]])
            P.act(lambda e, j=j: e.activation(out=ss[j], in_=ss[j], func=AF.Exp, scale=-0.5), R=[b_ss[j]], W=[b_ss[j]])
            P.small_on = False
            for hh in range(4):
                P.dve(lambda e, j=j, hh=hh: e.scalar_tensor_tensor(
                    out=of_[j][:, hh * 128:(hh + 1) * 128], in0=of_[j][:, hh * 128:(hh + 1) * 128],
                    scalar=ss[j][:, hh:hh + 1], in1=gnorm[:, hh * 128:(hh + 1) * 128], op0=ALU.mult, op1=ALU.mult),
                    R=[b_of[j], b_ss[j], b_cst], W=[b_of[j]])
            P.dve(lambda e, j=j: e.tensor_tensor(out=gmb[j], in0=of_[j], in1=zl[j], op=ALU.mult),
                  R=[b_of[j], b_zl[j]], W=[b_gmb[j]])
            for hh in range(4):
                jj = (hh + blk) % 2
                P.pe(lambda e, j=j, hh=hh, jj=jj: e.transpose(ptr[jj][:, 0:64].bitcast(BF16),
                                                              gmb[j][:, hh * 128:(hh + 1) * 128], identb),
                     R=[b_gmb[j], b_cst], W=[bptr[jj]])
                P.act(lambda e, hh=hh, jj=jj, bi=bi, blk=blk: e.activation(
                    out=gTt[bi][:, hh, blk * 128:(blk + 1) * 128], in_=ptr[jj][:, 0:64].bitcast(BF16), func=AF.Copy),
                    R=[bptr[jj]], W=[b_gT[bi]])
        P.dma("sp", lambda e, bi=bi, i=i: e.dma_start(
            out=cfl[bi], in_=k.CFd[:, i * TN:(i + 1) * TN].rearrange("(ct p) t -> p ct t", p=128)),
            R=[b_CFd[i]], W=[b_cfl[bi]], semkey="mcf%d" % bi)

    def rhs_fn(i, f, idx):
        bi = idx % 2
        return gTt[bi][:, f, :] if f < 4 else cfl[bi][:, f - 4, :]

    def rhs_bufs(i, f, idx):
        bi = idx % 2
        return [b_gT[bi]] if f < 4 else [b_cfl[bi]]

    emit_proj_ln(k, l, l * 3 + 1, 5, 8, wout, b_wo, rhs_fn, rhs_bufs, src, src_bufs, dst, dst_bufs,
                 list(range(NTILE)), pre_tile=pre_tile)


def kernel(**inputs):
    if "nc" not in _CACHE:
        _CACHE["nc"] = build_program()[0]
    nc = _CACHE["nc"]
    maps = prep_inputs(inputs)
    res = run_bass_kernel_spmd(nc, maps, core_ids=list(range(8)))
    out = np.stack([np.ascontiguousarray(r["yT"].T) for r in res.results], axis=0)
    return out.astype(np.float32)
```
